# Optimizing a Trainium2 kernel written in Bass

```python
import math
import jax, jax.numpy as jnp
from jax import lax
import numpy as np

D_MODEL = 1024
BATCH = 32
SEQ = 2048
DEPTH = 4

N_MIXERS = 3
N_SUBLAYERS = 3
N_MOD = 3
Q_BLOCK = 128
LN_EPS = 1e-5
RMS_EPS = 1e-6
DEEPNORM_ALPHA = (2.0 * DEPTH) ** 0.25
DEEPNORM_BETA = (8.0 * DEPTH) ** -0.25
FFN_HIDDEN = ((8 * D_MODEL // 3 + 255) // 256) * 256
FFN_RES_WEIGHT = 0.5

MLA_HEADS = D_MODEL // 128
MLA_Q_RANK = 3 * D_MODEL // 8
MLA_KV_RANK = D_MODEL // 4
MLA_NOPE_DIM = 128
MLA_ROPE_DIM = 64
MLA_V_DIM = 128
ROPE_THETA = 10000.0

DIFF_HEAD_DIM = 64
DIFF_HEADS = D_MODEL // (2 * DIFF_HEAD_DIM)
LAMBDA_STD = 0.1

SSM_INNER = 2 * D_MODEL
SSM_HEAD_DIM = 64
SSM_HEADS = SSM_INNER // SSM_HEAD_DIM
SSM_STATE = 128
SSM_GROUPS = 4
SSM_CONV = 4
SSM_CHUNK = 128
SSM_CONV_DIM = SSM_INNER + 2 * SSM_GROUPS * SSM_STATE

N_MLA_LAYERS = (DEPTH + 2) // 3
N_DIFF_LAYERS = (DEPTH + 1) // 3
N_SSM_LAYERS = DEPTH // 3

kernel_name = "hybrid_mla_diff_ssd_macaron_deepnorm"


def _layernorm(x, g, b):
    xf = x.astype(jnp.float32)
    mu = jnp.mean(xf, axis=-1, keepdims=True)
    var = jnp.mean(jnp.square(xf - mu), axis=-1, keepdims=True)
    return ((xf - mu) * lax.rsqrt(var + LN_EPS) * g + b).astype(x.dtype)


def _rmsnorm(x, g):
    xf = x.astype(jnp.float32)
    return (xf * lax.rsqrt(jnp.mean(xf * xf, axis=-1, keepdims=True) + RMS_EPS) * g).astype(x.dtype)


def _swiglu(h, w_in, w_out):
    gate, up = jnp.split(h @ w_in, 2, axis=-1)
    return (jax.nn.silu(gate) * up) @ w_out


def _sublayer(x, fn, mod_j, g, b, weight):
    shift, scale, gate = mod_j[:, 0, None, :], mod_j[:, 1, None, :], mod_j[:, 2, None, :]
    y = fn(x * (1 + scale) + shift)
    return _layernorm(DEEPNORM_ALPHA * x + weight * gate * y, g, b)


def _causal_blocks(q, k, v, attend):
    seq = q.shape[1]
    outs = []
    for start in range(0, seq, Q_BLOCK):
        end = start + Q_BLOCK
        mask = jnp.arange(end)[None, :] <= jnp.arange(start, end)[:, None]
        outs.append(attend(q[:, start:end], k[:, :end], v[:, :end], mask))
    return jnp.concatenate(outs, axis=1)


def _rope(x, cos, sin):
    xf = x.astype(jnp.float32)
    x1, x2 = jnp.split(xf, 2, axis=-1)
    return jnp.concatenate([x1 * cos - x2 * sin, x1 * sin + x2 * cos], axis=-1).astype(x.dtype)


def _mla(h, positions, w_in, q_norm_g, kv_norm_g, w_q_up, w_kv_up, w_out):
    b, s, _ = h.shape
    cq, ckv, k_rope = jnp.split(h @ w_in, [MLA_Q_RANK, MLA_Q_RANK + MLA_KV_RANK], axis=-1)
    q = (_rmsnorm(cq, q_norm_g) @ w_q_up).reshape(b, s, MLA_HEADS, MLA_NOPE_DIM + MLA_ROPE_DIM)
    kv = (_rmsnorm(ckv, kv_norm_g) @ w_kv_up).reshape(b, s, MLA_HEADS, MLA_NOPE_DIM + MLA_V_DIM)
    q_nope, q_rope = q[..., :MLA_NOPE_DIM], q[..., MLA_NOPE_DIM:]
    k_nope, v = kv[..., :MLA_NOPE_DIM], kv[..., MLA_NOPE_DIM:]
    inv_freq = ROPE_THETA ** (-jnp.arange(0, MLA_ROPE_DIM, 2, dtype=jnp.float32) / MLA_ROPE_DIM)
    ang = positions.astype(jnp.float32)[..., None] * inv_freq
    cos, sin = jnp.cos(ang)[:, :, None, :], jnp.sin(ang)[:, :, None, :]
    q_rope = _rope(q_rope, cos, sin)
    k_rope = _rope(k_rope[:, :, None, :], cos, sin)
    q = jnp.concatenate([q_nope, q_rope], axis=-1)
    k = jnp.concatenate([k_nope, jnp.broadcast_to(k_rope, (b, s, MLA_HEADS, MLA_ROPE_DIM))], axis=-1)
    scale = (MLA_NOPE_DIM + MLA_ROPE_DIM) ** -0.5

    def attend(qb, kb, vb, mask):
        sc = jnp.einsum('bqhd,bkhd->bhqk', qb, kb).astype(jnp.float32) * scale
        p = jax.nn.softmax(jnp.where(mask, sc, -jnp.inf), axis=-1).astype(vb.dtype)
        return jnp.einsum('bhqk,bkhv->bqhv', p, vb)

    o = _causal_blocks(q, k, v, attend)
    return o.reshape(b, s, MLA_HEADS * MLA_V_DIM) @ w_out


def _diff_attn(h, layer, w_in, lambda_q, lambda_k, subln_g, w_out):
    b, s, _ = h.shape
    lam_init = 0.8 - 0.6 * math.exp(-0.3 * layer)
    q, k, v = jnp.split(h @ w_in, 3, axis=-1)
    q = q.reshape(b, s, DIFF_HEADS, 2, DIFF_HEAD_DIM)
    k = k.reshape(b, s, DIFF_HEADS, 2, DIFF_HEAD_DIM)
    v = v.reshape(b, s, DIFF_HEADS, 2 * DIFF_HEAD_DIM)
    lq, lk = lambda_q.astype(jnp.float32), lambda_k.astype(jnp.float32)
    lam = jnp.exp(jnp.sum(lq[0] * lk[0])) - jnp.exp(jnp.sum(lq[1] * lk[1])) + lam_init
    scale = DIFF_HEAD_DIM ** -0.5

    def attend(qb, kb, vb, mask):
        sc = jnp.einsum('bqhid,bkhid->bhiqk', qb, kb).astype(jnp.float32) * scale
        p = jax.nn.softmax(jnp.where(mask, sc, -jnp.inf), axis=-1)
        a = (p[:, :, 0] - lam * p[:, :, 1]).astype(vb.dtype)
        return jnp.einsum('bhqk,bkhe->bqhe', a, vb)

    o = _causal_blocks(q, k, v, attend)
    o = _rmsnorm(o, subln_g) * (1.0 - lam_init)
    return o.reshape(b, s, D_MODEL) @ w_out


def _ssd_scan(x, dt, a, bmat, cmat):
    b, s, nh, p = x.shape
    g, n = bmat.shape[2], bmat.shape[3]
    hg = nh // g
    nc, L = s // SSM_CHUNK, SSM_CHUNK

    def chunks(t):
        return jnp.moveaxis(t.reshape(b, nc, L, *t.shape[2:]), 1, 0)

    xs = chunks(x.reshape(b, s, g, hg, p))
    dts = chunks(dt.reshape(b, s, g, hg))
    bs, cs = chunks(bmat), chunks(cmat)
    causal = jnp.tril(jnp.ones((L, L), dtype=bool))[None, :, :, None, None]
    a_g = a.reshape(g, hg)

    def step(state, inp):
        xc, dtc, bc, cc = inp
        acum = jnp.cumsum(dtc * a_g, axis=1)
        seg = acum[:, :, None] - acum[:, None, :]
        decay = jnp.exp(jnp.where(causal, seg, -jnp.inf))
        cb = jnp.einsum('blgn,bsgn->blsg', cc, bc)
        y = jnp.einsum('blsg,blsgh,bsghp->blghp', cb, decay, xc * dtc[..., None])
        y = y + jnp.einsum('blgn,bghpn->blghp', cc, state) * jnp.exp(acum)[..., None]
        last = acum[:, -1]
        w = jnp.exp(last[:, None] - acum) * dtc
        state = state * jnp.exp(last)[..., None, None] + jnp.einsum('bsgn,bsgh,bsghp->bghpn', bc, w, xc)
        return state, y

    state0 = jnp.zeros((b, g, hg, p, n), jnp.float32)
    _, ys = lax.scan(step, state0, (xs, dts, bs, cs))
    return jnp.moveaxis(ys, 0, 1).reshape(b, s, nh, p)


def _mamba2(h, w_in, conv_w, conv_b, dt_bias, a_log, d_skip, norm_g, w_out):
    b, s, _ = h.shape
    z, xbc, dt = jnp.split(h @ w_in, [SSM_INNER, SSM_INNER + SSM_CONV_DIM], axis=-1)
    xbc = lax.conv_general_dilated(
        xbc, conv_w[:, None, :], window_strides=(1,), padding=((SSM_CONV - 1, 0),),
        dimension_numbers=('NWC', 'WIO', 'NWC'), feature_group_count=SSM_CONV_DIM) + conv_b
    xbc = jax.nn.silu(xbc)
    xs, bmat, cmat = jnp.split(xbc, [SSM_INNER, SSM_INNER + SSM_GROUPS * SSM_STATE], axis=-1)
    dt = jax.nn.softplus(dt.astype(jnp.float32) + dt_bias.astype(jnp.float32))
    a = -jnp.exp(a_log.astype(jnp.float32))
    xh = xs.reshape(b, s, SSM_HEADS, SSM_HEAD_DIM).astype(jnp.float32)
    y = _ssd_scan(xh, dt, a,
                  bmat.reshape(b, s, SSM_GROUPS, SSM_STATE).astype(jnp.float32),
                  cmat.reshape(b, s, SSM_GROUPS, SSM_STATE).astype(jnp.float32))
    y = y + d_skip.astype(jnp.float32)[:, None] * xh
    y = y.reshape(b, s, SSM_INNER).astype(h.dtype) * jax.nn.silu(z)
    gs = SSM_INNER // SSM_GROUPS
    y = _rmsnorm(y.reshape(b, s, SSM_GROUPS, gs), norm_g.reshape(SSM_GROUPS, gs)).reshape(b, s, SSM_INNER)
    return y @ w_out


def setup_inputs(seed: int = 0) -> dict:
    key = jax.random.key(seed)
    ks = iter(jax.random.split(key, 32))

    def nrm(shape, scale):
        return jax.random.normal(next(ks), shape, jnp.float32) * scale

    D, F = D_MODEL, FFN_HIDDEN
    x = nrm((BATCH, SEQ, D), 1.0)
    c = nrm((BATCH, D), 1.0)
    offsets = jax.random.randint(next(ks), (BATCH, 1), 0, 4096, dtype=jnp.int32)
    positions = offsets + jnp.arange(SEQ, dtype=jnp.int32)[None, :]
    w_mod = nrm((DEPTH, D, N_SUBLAYERS * N_MOD * D), 0.5 * D ** -0.5)
    b_mod = nrm((DEPTH, N_SUBLAYERS * N_MOD * D), 0.02)
    ln_g = 1.0 + nrm((DEPTH, N_SUBLAYERS, D), 0.02)
    ln_b = nrm((DEPTH, N_SUBLAYERS, D), 0.02)
    ffn_w_in = nrm((DEPTH, 2, D, 2 * F), D ** -0.5)
    ffn_w_out = nrm((DEPTH, 2, F, D), F ** -0.5 * DEEPNORM_BETA)

    na = N_MLA_LAYERS
    mla_w_in = nrm((na, D, MLA_Q_RANK + MLA_KV_RANK + MLA_ROPE_DIM), D ** -0.5)
    mla_q_norm_g = 1.0 + nrm((na, MLA_Q_RANK), 0.02)
    mla_kv_norm_g = 1.0 + nrm((na, MLA_KV_RANK), 0.02)
    mla_w_q_up = nrm((na, MLA_Q_RANK, MLA_HEADS * (MLA_NOPE_DIM + MLA_ROPE_DIM)), MLA_Q_RANK ** -0.5)
    mla_w_kv_up = nrm((na, MLA_KV_RANK, MLA_HEADS * (MLA_NOPE_DIM + MLA_V_DIM)), MLA_KV_RANK ** -0.5)
    mla_w_out = nrm((na, MLA_HEADS * MLA_V_DIM, D), (MLA_HEADS * MLA_V_DIM) ** -0.5 * DEEPNORM_BETA)

    nb = N_DIFF_LAYERS
    diff_w_in = nrm((nb, D, 3 * D), D ** -0.5)
    diff_lambda_q = nrm((nb, 2, DIFF_HEAD_DIM), LAMBDA_STD)
    diff_lambda_k = nrm((nb, 2, DIFF_HEAD_DIM), LAMBDA_STD)
    diff_subln_g = 1.0 + nrm((nb, 2 * DIFF_HEAD_DIM), 0.02)
    diff_w_out = nrm((nb, D, D), D ** -0.5 * DEEPNORM_BETA)

    nc = N_SSM_LAYERS
    ssm_w_in = nrm((nc, D, 2 * SSM_INNER + 2 * SSM_GROUPS * SSM_STATE + SSM_HEADS), D ** -0.5)
    ssm_conv_w = nrm((nc, SSM_CONV, SSM_CONV_DIM), SSM_CONV ** -0.5)
    ssm_conv_b = nrm((nc, SSM_CONV_DIM), 0.02)
    u = jax.random.uniform(next(ks), (nc, SSM_HEADS), jnp.float32)
    dt0 = jnp.exp(u * (math.log(0.1) - math.log(1e-3)) + math.log(1e-3))
    ssm_dt_bias = dt0 + jnp.log(-jnp.expm1(-dt0))
    ssm_a_log = jnp.log(jax.random.uniform(next(ks), (nc, SSM_HEADS), jnp.float32, 1.0, 16.0))
    ssm_d_skip = 1.0 + nrm((nc, SSM_HEADS), 0.02)
    ssm_norm_g = 1.0 + nrm((nc, SSM_INNER), 0.02)
    ssm_w_out = nrm((nc, SSM_INNER, D), SSM_INNER ** -0.5 * DEEPNORM_BETA)

    return {
        "x": x, "c": c, "positions": positions,
        "w_mod": w_mod, "b_mod": b_mod, "ln_g": ln_g, "ln_b": ln_b,
        "ffn_w_in": ffn_w_in, "ffn_w_out": ffn_w_out,
        "mla_w_in": mla_w_in, "mla_q_norm_g": mla_q_norm_g, "mla_kv_norm_g": mla_kv_norm_g,
        "mla_w_q_up": mla_w_q_up, "mla_w_kv_up": mla_w_kv_up, "mla_w_out": mla_w_out,
        "diff_w_in": diff_w_in, "diff_lambda_q": diff_lambda_q, "diff_lambda_k": diff_lambda_k,
        "diff_subln_g": diff_subln_g, "diff_w_out": diff_w_out,
        "ssm_w_in": ssm_w_in, "ssm_conv_w": ssm_conv_w, "ssm_conv_b": ssm_conv_b,
        "ssm_dt_bias": ssm_dt_bias, "ssm_a_log": ssm_a_log, "ssm_d_skip": ssm_d_skip,
        "ssm_norm_g": ssm_norm_g, "ssm_w_out": ssm_w_out,
    }


def reference(x, c, positions, w_mod, b_mod, ln_g, ln_b, ffn_w_in, ffn_w_out,
              mla_w_in, mla_q_norm_g, mla_kv_norm_g, mla_w_q_up, mla_w_kv_up, mla_w_out,
              diff_w_in, diff_lambda_q, diff_lambda_k, diff_subln_g, diff_w_out,
              ssm_w_in, ssm_conv_w, ssm_conv_b, ssm_dt_bias, ssm_a_log, ssm_d_skip,
              ssm_norm_g, ssm_w_out):
    cond = jax.nn.silu(c)
    for i in range(DEPTH):
        mod = (cond @ w_mod[i] + b_mod[i]).reshape(-1, N_SUBLAYERS, N_MOD, D_MODEL)
        kind, idx = i % N_MIXERS, i // N_MIXERS

        x = _sublayer(x, lambda h: _swiglu(h, ffn_w_in[i, 0], ffn_w_out[i, 0]),
                      mod[:, 0], ln_g[i, 0], ln_b[i, 0], FFN_RES_WEIGHT)

        if kind == 0:
            mixer = lambda h: _mla(h, positions, mla_w_in[idx], mla_q_norm_g[idx], mla_kv_norm_g[idx],
                                   mla_w_q_up[idx], mla_w_kv_up[idx], mla_w_out[idx])
        elif kind == 1:
            mixer = lambda h: _diff_attn(h, i, diff_w_in[idx], diff_lambda_q[idx], diff_lambda_k[idx],
                                         diff_subln_g[idx], diff_w_out[idx])
        else:
            mixer = lambda h: _mamba2(h, ssm_w_in[idx], ssm_conv_w[idx], ssm_conv_b[idx], ssm_dt_bias[idx],
                                      ssm_a_log[idx], ssm_d_skip[idx], ssm_norm_g[idx], ssm_w_out[idx])
        x = _sublayer(x, mixer, mod[:, 1], ln_g[i, 1], ln_b[i, 1], 1.0)

        x = _sublayer(x, lambda h: _swiglu(h, ffn_w_in[i, 1], ffn_w_out[i, 1]),
                      mod[:, 2], ln_g[i, 2], ln_b[i, 2], FFN_RES_WEIGHT)
    return x
```

```python
import numpy as np
import concourse.bass as bass
import concourse.mybir as mybir
from concourse.bass_utils import run_bass_kernel_spmd

F32 = mybir.dt.float32
BF16 = mybir.dt.bfloat16
I32 = mybir.dt.int32
AF = mybir.ActivationFunctionType
ALU = mybir.AluOpType

D = 1024
S = 2048
NB = 4
DEPTH = 4
FH = 2816
NFG = FH // 128
LN_EPS = 1e-5
RMS_EPS = 1e-6
ALPHA = (2.0 * DEPTH) ** 0.25
EPS_LN = LN_EPS / (ALPHA * ALPHA)
TT = 512
NTT = S // TT


class K:
    def __init__(self, nc, same_sync=True):
        self.nc = nc
        self.same_sync = same_sync
        self.eng = {"pe": nc.tensor, "act": nc.scalar, "dve": nc.vector, "pool": nc.gpsimd, "sp": nc.sync}
        self.prog = {e: [] for e in self.eng}
        self.sems = {}
        self.cnt = {}
        for e in ("pe", "act", "dve", "poolc"):
            self.sems[e] = nc.alloc_semaphore("sem_" + e)
            self.cnt[e] = 0
        self.dma_total = {}
        self.waited = {e: {} for e in self.eng}
        self.lastw = {}
        self.readers = {}
        self.nps = 0
        self.ninstr = 0

    def dma_sem(self, key):
        if key not in self.sems:
            self.sems[key] = self.nc.alloc_semaphore("d_" + str(key).replace(" ", ""))
            self.dma_total[key] = 0
        return key

    def _deps(self, eng, own, reads, writes):
        deps = {}

        def need(st):
            if st is None:
                return
            if deps.get(st[0], 0) < st[1]:
                deps[st[0]] = st[1]

        for t in reads:
            need(self.lastw.get(t))
        for t in writes:
            need(self.lastw.get(t))
            for r in self.readers.get(t, {}).items():
                need(r)
        out = []
        for sk, val in deps.items():
            if sk == own and (own == "pe" or not self.same_sync):
                continue
            if sk in self.dma_total:
                val = self.dma_total[sk]
            if self.waited[eng].get(sk, 0) >= val:
                continue
            self.waited[eng][sk] = val
            out.append((sk, val))
        return out

    def _stamp(self, st, reads, writes):
        for t in reads:
            d = self.readers.setdefault(t, {})
            if d.get(st[0], 0) < st[1]:
                d[st[0]] = st[1]
        for t in writes:
            self.lastw[t] = st
            self.readers[t] = {}

    def op(self, eng, fn, reads=(), writes=(), inc=True):
        own = "poolc" if eng == "pool" else eng
        waits = self._deps(eng, own, reads, writes)
        if inc:
            self.cnt[own] += 1
            st = (own, self.cnt[own])
            self.prog[eng].append((waits, fn, (own, 1)))
        else:
            st = (own, self.cnt[own] + 1)
            self.prog[eng].append((waits, fn, None))
        self._stamp(st, reads, writes)
        self.ninstr += 1

    def dma(self, queue, fn, semkey, reads=(), writes=()):
        self.dma_sem(semkey)
        waits = self._deps(queue, None, reads, writes)
        if isinstance(semkey, str) and semkey.startswith("ld_") and semkey != "ld_x":
            prev = self.dma_total[semkey]
            if prev and self.waited[queue].get(semkey, 0) < prev:
                self.waited[queue][semkey] = prev
                waits.append((semkey, prev))
        self.dma_total[semkey] += 16
        st = (semkey, self.dma_total[semkey])
        self.prog[queue].append((waits, fn, (semkey, 16)))
        self._stamp(st, reads, writes)
        self.ninstr += 1

    def barrier(self, engines=("pe", "act", "dve", "sp", "pool"), dma_keys=()):
        for e in engines:
            waits = []
            for sk in ("pe", "act", "dve"):
                if sk == e:
                    continue
                val = self.cnt[sk]
                if self.waited[e].get(sk, 0) < val:
                    self.waited[e][sk] = val
                    waits.append((sk, val))
            for sk in dma_keys:
                val = self.dma_total.get(sk, 0)
                if val and self.waited[e].get(sk, 0) < val:
                    self.waited[e][sk] = val
                    waits.append((sk, val))
            if waits:
                self.prog[e].append((waits, None, None))

    def final_wait(self, eng, semkeys):
        waits = [(sk, self.dma_total[sk]) for sk in semkeys if sk in self.dma_total]
        self.prog[eng].append((waits, None, None))

    def emit(self):
        nc = self.nc
        sems = self.sems

        def run(e, lst):
            for waits, fn, inc in lst:
                for sk, val in waits:
                    e.wait_ge(sems[sk], val)
                if fn is not None:
                    ins = fn(e)
                    if inc is not None:
                        ins.then_inc(sems[inc[0]], inc[1])

        with nc.Block() as block:
            @block.sync
            def _(e):
                run(e, self.prog["sp"])

            @block.gpsimd
            def _(e):
                run(e, self.prog["pool"])

            @block.tensor
            def _(e):
                run(e, self.prog["pe"])

            @block.vector
            def _(e):
                run(e, self.prog["dve"])

            @block.scalar
            def _(e):
                run(e, self.prog["act"])


class Arena:
    def __init__(self, nc):
        self.nc = nc
        base = (nc.sbuf_base + 63) // 64 * 64
        total = nc.sbuf_top - base - 64
        total = total // 64 * 64
        pad = base - nc.sbuf_base
        self.slab = nc.alloc_sbuf_tensor("arena", [128, (total + pad) // 4], F32)
        self.base = base
        self.end = base + total
        self.cur = base
        self.n = 0

    def alloc(self, name, shape, dtype, parts=128):
        nbytes = int(np.prod(shape[1:])) * (2 if dtype == BF16 else 4)
        nbytes = (nbytes + 63) // 64 * 64
        off = self.cur
        assert off + nbytes <= self.end, (name, off, nbytes, self.end)
        self.cur += nbytes
        self.n += 1
        return self.nc.alloc_sbuf_tensor_at("%s_%d" % (name, self.n), list(shape), dtype, offset=off)

    def mark(self):
        return self.cur

    def reset(self, m):
        self.cur = m


def build(n_seq=NB, sublayers=None, same_sync=True, debug=False):
    if sublayers is None:
        sublayers = [(l, j) for l in range(DEPTH) for j in range(3)]
    nc = bass.Bass("TRN2", target_bir_lowering=False)
    dr = {}

    def din(name, shape, dt=F32):
        dr[name] = nc.dram_tensor(name, list(shape), dt, kind="ExternalInput").ap()
        return dr[name]

    def dget(name, shape, dt=F32):
        if name not in dr:
            din(name, shape, dt)
        return dr[name]

    xT_d = din("xT", [NB, D, S])
    cT_d = din("cT", [128, 8, NB])
    bmod_d = din("b_modT", [128, DEPTH * 72])
    lng_d = din("ln_gT", [128, DEPTH * 3 * 8])
    lnb_d = din("ln_bT", [128, DEPTH * 3 * 8])
    oT_d = nc.dram_tensor("oT", [NB, D, S], F32, kind="ExternalOutput").ap()
    layers_used = sorted(set(l for l, _ in sublayers))

    k = K(nc, same_sync=same_sync)
    ar = Arena(nc)
    ps = [nc.alloc_psum_tensor("ps%d" % i, [128, 512], F32) for i in range(8)]

    def psb():
        b = k.nps % 8
        k.nps += 1
        return b

    xT = ar.alloc("xT", [128, 8, S], F32)
    modsb = ar.alloc("modsb", [128, DEPTH * 72 * NB], F32)
    sc1 = ar.alloc("sc1", [128, DEPTH * 3 * 8 * NB], F32)
    gcs = ar.alloc("gcs", [128, DEPTH * 3 * 8 * NB], F32)
    lng = ar.alloc("lng", [128, DEPTH * 3 * 8], F32)
    lnb = ar.alloc("lnb", [128, DEPTH * 3 * 8], F32)
    ones_bf = ar.alloc("ones", [128, 128], BF16)
    epsln = ar.alloc("epsln", [128, 1], F32)
    NWI, NWO = 2, 2
    wi_sl = [ar.alloc("wi", [128, 8, 2, 128], BF16) for _ in range(NWI)]
    wo_sl = [ar.alloc("wo", [128, NFG, 128], BF16) for _ in range(NWO)]
    cnt = {"wi": 0, "wo": 0}
    phase_mark = ar.mark()

    def midx(l, j, m, c, b):
        return (((l * 3 + j) * 3 + m) * 8 + c) * NB + b

    def pidx(l, j, c, b):
        return ((l * 3 + j) * 8 + c) * NB + b

    k.op("dve", lambda e: e.memset(ones_bf[:], 1.0), writes=["ones"])
    k.op("dve", lambda e: e.memset(epsln[:], EPS_LN), writes=["epsln"])
    k.dma("sp", lambda e: e.dma_start(out=lng[:], in_=lng_d[:, :]), "ld_small", writes=["lng"])
    k.dma("sp", lambda e: e.dma_start(out=lnb[:], in_=lnb_d[:, :]), "ld_small", writes=["lnb"])
    m0 = ar.mark()
    cond = ar.alloc("cond", [128, 8, NB], F32)
    bmod = ar.alloc("bmod", [128, DEPTH * 72], F32)
    wm = [ar.alloc("wm", [128, 8, D], F32) for _ in range(2)]
    k.dma("sp", lambda e: e.dma_start(out=cond[:], in_=cT_d[:, :, :]), "ld_small", writes=["cond"])
    k.dma("sp", lambda e: e.dma_start(out=bmod[:], in_=bmod_d[:, :]), "ld_small", writes=["bmod"])
    k.op("act", lambda e: e.activation(out=cond[:], in_=cond[:], func=AF.Silu), reads=["cond"], writes=["cond"])
    g = 0
    for l in layers_used:
        wmod_l = dget("w_mod_%d" % l, [D, 9 * D])
        for jm in range(9):
            sl = g % 2
            g += 1
            wmt = wm[sl]
            for kc in range(8):
                k.dma("sp", lambda e, wmt=wmt, kc=kc, wmod_l=wmod_l, jm=jm: e.dma_start(
                    out=wmt[:, kc, :], in_=wmod_l[kc * 128:(kc + 1) * 128, jm * D:(jm + 1) * D]),
                    ("wm", sl), writes=[("wm", sl)])
            b_ = psb()
            for c in range(8):
                for kc in range(8):
                    k.op("pe", lambda e, b_=b_, wmt=wmt, c=c, kc=kc: e.matmul(
                        ps[b_][:, c * NB:(c + 1) * NB], lhsT=wmt[:, kc, c * 128:(c + 1) * 128], rhs=cond[:, kc, :],
                        start=(kc == 0), stop=(kc == 7)),
                        reads=[("wm", sl), "cond"], writes=[("ps", b_)], inc=(c == 7 and kc == 7))
            base = (l * 9 + jm) * 8 * NB
            bb = (l * 9 + jm) * 8
            k.op("dve", lambda e, b_=b_, base=base, bb=bb: e.tensor_tensor(
                out=modsb[:, base:base + 8 * NB].rearrange("p (c b) -> p c b", b=NB),
                in0=ps[b_][:, 0:8 * NB].rearrange("p (c b) -> p c b", b=NB),
                in1=bmod[:, bb:bb + 8].unsqueeze(2).to_broadcast([128, 8, NB]), op=ALU.add),
                reads=[("ps", b_), "bmod"], writes=["modsb"])
    for l in layers_used:
        for j in range(3):
            wgt = (0.5 if j != 1 else 1.0) / ALPHA
            s0 = midx(l, j, 1, 0, 0)
            g0 = midx(l, j, 2, 0, 0)
            p0 = pidx(l, j, 0, 0)
            n = 8 * NB
            k.op("dve", lambda e, s0=s0, p0=p0, n=n: e.tensor_scalar(
                out=sc1[:, p0:p0 + n], in0=modsb[:, s0:s0 + n], scalar1=1.0, scalar2=None, op0=ALU.add),
                reads=["modsb"], writes=["sc1"])
            k.op("dve", lambda e, g0=g0, p0=p0, n=n, wgt=wgt: e.tensor_scalar(
                out=gcs[:, p0:p0 + n], in0=modsb[:, g0:g0 + n], scalar1=wgt, scalar2=None, op0=ALU.mult),
                reads=["modsb"], writes=["gcs"])
    k.barrier(dma_keys=["ld_small", ("wm", 0), ("wm", 1)])
    ar.reset(m0)
    phase_mark = ar.mark()

    def load_wi(l, jj, fg, slots):
        sl = cnt["wi"] % len(slots)
        cnt["wi"] += 1
        t = slots[sl]
        k.dma("pool", lambda e: e.dma_start(out=t[:].rearrange("p a b c -> p (a b c)"), in_=dget("ffn_wi_%d_%d" % (l, jj), [NFG, 128, 2048])[fg, :, :]),
              ("wi", sl), writes=[("wi", sl)])
        return sl

    def load_wo(l, jj, dc):
        sl = cnt["wo"] % NWO
        cnt["wo"] += 1
        t = wo_sl[sl]
        for h in range(2):
            k.dma("pool", lambda e, h=h: e.dma_start(
                out=t[:, h * 11:(h + 1) * 11, :].rearrange("p a b -> p (a b)"),
                in_=dget("ffn_wo_%d_%d" % (l, jj), [8, 128, NFG * 128])[dc, :, h * 1408:(h + 1) * 1408]),
                ("wo", sl), writes=[("wo", sl)])
        return sl

    def layer_norm_tile(l, j, gtt, zb, zq, st):
        tsl = slice(gtt * TT, (gtt + 1) * TT)
        for c in range(8):
            k.op("act", lambda e, c=c: e.activation(out=zb[:, c, :], in_=xT[:, c, tsl], func=AF.Copy),
                 reads=[("x", c, gtt)], writes=[("zb", c)])
            k.op("act", lambda e, c=c: e.activation(out=zq[:, c, :], in_=xT[:, c, tsl], func=AF.Square),
                 reads=[("x", c, gtt)], writes=[("zq", c)])
        b1 = psb()
        for c in range(8):
            k.op("pe", lambda e, c=c: e.matmul(ps[b1][:, :], lhsT=ones_bf[:], rhs=zb[:, c, :], start=(c == 0), stop=(c == 7)),
                 reads=[("zb", c), "ones"], writes=[("ps", b1)], inc=(c == 7))
        b2 = psb()
        for c in range(8):
            k.op("pe", lambda e, c=c: e.matmul(ps[b2][:, :], lhsT=ones_bf[:], rhs=zq[:, c, :], start=(c == 0), stop=(c == 7)),
                 reads=[("zq", c), "ones"], writes=[("ps", b2)], inc=(c == 7))
        mean, msq, var, rstd, nmr = (st[:, i, :] for i in range(5))
        k.op("act", lambda e: e.activation(out=mean, in_=ps[b1][:, :], func=AF.Copy, scale=1.0 / D),
             reads=[("ps", b1)], writes=["st_mean"])
        k.op("act", lambda e: e.activation(out=msq, in_=ps[b1][:, :], func=AF.Square, scale=1.0 / D),
             reads=[("ps", b1)], writes=["st_msq"])
        k.op("dve", lambda e: e.scalar_tensor_tensor(out=var, in0=ps[b2][:, :], scalar=1.0 / D, in1=msq,
                                                     op0=ALU.mult, op1=ALU.subtract),
             reads=[("ps", b2), "st_msq"], writes=["st_var"])
        k.op("act", lambda e: e.activation(out=var, in_=var, func=AF.Ln, bias=epsln[:, 0:1], scale=1.0),
             reads=["st_var", "epsln"], writes=["st_var"])
        k.op("act", lambda e: e.activation(out=rstd, in_=var, func=AF.Exp, scale=-0.5),
             reads=["st_var"], writes=["st_rstd"])
        k.op("dve", lambda e: e.scalar_tensor_tensor(out=nmr, in0=mean, scalar=-1.0, in1=rstd,
                                                     op0=ALU.mult, op1=ALU.mult),
             reads=["st_mean", "st_rstd"], writes=["st_nmr"])
        tmp = st[:, 5, :]
        for c in range(8):
            gi = (l * 3 + j) * 8 + c
            k.op("dve", lambda e, c=c: e.tensor_tensor(out=tmp, in0=xT[:, c, tsl], in1=rstd, op=ALU.mult),
                 reads=[("x", c, gtt), "st_rstd"], writes=["st_tmp"])
            k.op("dve", lambda e: e.tensor_tensor(out=tmp, in0=tmp, in1=nmr, op=ALU.add),
                 reads=["st_tmp", "st_nmr"], writes=["st_tmp"])
            k.op("act", lambda e, c=c, gi=gi: e.activation(out=xT[:, c, tsl], in_=tmp, func=AF.Identity,
                                                            scale=lng[:, gi:gi + 1], bias=lnb[:, gi:gi + 1]),
                 reads=["st_tmp", "lng", "lnb"], writes=[("x", c, gtt)])

    def modulate(l, j, b, dst, dst_tok, gtts):
        for i, gtt in enumerate(gtts):
            for c in range(8):
                si = pidx(l, j, c, b)
                hi = midx(l, j, 0, c, b)
                k.op("act", lambda e, i=i, c=c, gtt=gtt, si=si, hi=hi: e.activation(
                    out=dst[:, c, i * TT:(i + 1) * TT], in_=xT[:, c, gtt * TT:(gtt + 1) * TT], func=AF.Identity,
                    scale=sc1[:, si:si + 1], bias=modsb[:, hi:hi + 1]),
                    reads=[("x", c, gtt), "sc1", "modsb"], writes=[(dst_tok, c, i)])

    def ffn_sublayer(l, j, b):
        jj = 0 if j == 0 else 1
        m = ar.mark()
        hT = [ar.alloc("hT", [128, 8, 2 * TT], BF16) for _ in range(2)]
        aT = ar.alloc("aT", [128, NFG, 2 * TT], BF16)
        slots = wi_sl + [ar.alloc("wix", [128, 8, 2, 128], BF16) for _ in range(2)]
        sg = [ar.alloc("sg", [128, TT], F32) for _ in range(2)]
        zb = ar.alloc("zb", [128, 8, TT], BF16)
        zq = ar.alloc("zq", [128, 8, TT], BF16)
        st = ar.alloc("st", [128, 6, TT], F32)
        nsg = 0
        for p in range(2):
            h = hT[p]
            htok = "h%d" % p
            modulate(l, j, b, h, htok, [2 * p, 2 * p + 1])
            for fg in range(NFG):
                sl = load_wi(l, jj, fg, slots)
                w = slots[sl]
                for tt in range(2):
                    bg, bu = psb(), psb()
                    for gu, bk in ((0, bg), (1, bu)):
                        for kc in range(8):
                            k.op("pe", lambda e, bk=bk, gu=gu, kc=kc, tt=tt, w=w, h=h: e.matmul(
                                ps[bk][:, :], lhsT=w[:, kc, gu, :], rhs=h[:, kc, tt * TT:(tt + 1) * TT],
                                start=(kc == 0), stop=(kc == 7)),
                                reads=[("wi", sl), (htok, kc, tt)], writes=[("ps", bk)], inc=(kc == 7))
                    sgt = sg[nsg % 2]
                    sgk = ("sg", nsg % 2)
                    nsg += 1
                    k.op("act", lambda e, bg=bg, sgt=sgt: e.activation(out=sgt[:], in_=ps[bg][:, :], func=AF.Silu),
                         reads=[("ps", bg)], writes=[sgk])
                    k.op("dve", lambda e, bu=bu, sgt=sgt, fg=fg, tt=tt: e.tensor_tensor(
                        out=aT[:, fg, tt * TT:(tt + 1) * TT], in0=ps[bu][:, :], in1=sgt[:], op=ALU.mult),
                        reads=[("ps", bu), sgk], writes=[("a", fg, tt)])
            for dc in range(8):
                sl = load_wo(l, jj, dc)
                w = wo_sl[sl]
                gi = pidx(l, j, dc, b)
                for tt in range(2):
                    gtt = 2 * p + tt
                    bk = psb()
                    for fc in range(NFG):
                        k.op("pe", lambda e, bk=bk, fc=fc, tt=tt, w=w: e.matmul(
                            ps[bk][:, :], lhsT=w[:, fc, :], rhs=aT[:, fc, tt * TT:(tt + 1) * TT],
                            start=(fc == 0), stop=(fc == NFG - 1)),
                            reads=[("wo", sl), ("a", fc, tt)], writes=[("ps", bk)], inc=(fc == NFG - 1))
                    k.op("dve", lambda e, bk=bk, dc=dc, gtt=gtt, gi=gi: e.scalar_tensor_tensor(
                        out=xT[:, dc, gtt * TT:(gtt + 1) * TT], in0=ps[bk][:, :], scalar=gcs[:, gi:gi + 1],
                        in1=xT[:, dc, gtt * TT:(gtt + 1) * TT], op0=ALU.mult, op1=ALU.add),
                        reads=[("ps", bk), ("x", dc, gtt), "gcs"], writes=[("x", dc, gtt)])
            for tt in range(2):
                layer_norm_tile(l, j, 2 * p + tt, zb, zq, st)
        k.barrier()
        ar.reset(m)

    dumps = []

    def dump(name, tile_, shape, dtype, reads):
        if not debug or name in dumps:
            return
        dumps.append(name)
        d_ = nc.dram_tensor("dbg_" + name, list(shape), dtype, kind="ExternalOutput").ap()
        k.dma("sp", lambda e: e.dma_start(out=d_, in_=tile_), "st_x", reads=reads)

    rr = {}

    def psr(lo, hi, key):
        i = rr.get(key, 0)
        rr[key] = i + 1
        return lo + i % (hi - lo)

    def out_proj_and_ln(l, b, oT, nchunks, wname, tokp):
        wd = dget(wname, [8, 128, nchunks * 128])
        for dc in range(8):
            sl = cnt["wo"] % NWO
            cnt["wo"] += 1
            w = wo_sl[sl]
            k.dma("pool", lambda e, w=w, dc=dc: e.dma_start(
                out=w[:, 0:nchunks, :].rearrange("p a b -> p (a b)"), in_=wd[dc, :, :]),
                ("wo", sl), writes=[("wo", sl)])
            gi = pidx(l, 1, dc, b)
            for gtt in range(NTT):
                bk = psb()
                for hc in range(nchunks):
                    k.op("pe", lambda e, bk=bk, hc=hc, gtt=gtt, w=w: e.matmul(
                        ps[bk][:, :], lhsT=w[:, hc, :], rhs=oT[:, hc, gtt * TT:(gtt + 1) * TT],
                        start=(hc == 0), stop=(hc == nchunks - 1)),
                        reads=[("wo", sl), (tokp, hc, gtt)], writes=[("ps", bk)], inc=(hc == nchunks - 1))
                k.op("dve", lambda e, bk=bk, dc=dc, gtt=gtt, gi=gi: e.scalar_tensor_tensor(
                    out=xT[:, dc, gtt * TT:(gtt + 1) * TT], in0=ps[bk][:, :], scalar=gcs[:, gi:gi + 1],
                    in1=xT[:, dc, gtt * TT:(gtt + 1) * TT], op0=ALU.mult, op1=ALU.add),
                    reads=[("ps", bk), ("x", dc, gtt), "gcs"], writes=[("x", dc, gtt)])
        zb = ar.alloc("zb", [128, 8, TT], BF16)
        zq = ar.alloc("zq", [128, 8, TT], BF16)
        st = ar.alloc("st", [128, 6, TT], F32)
        for gtt in range(NTT):
            layer_norm_tile(l, 1, gtt, zb, zq, st)

    def attention_head(q_parts, k_parts, v_fn, out_cb, scale, rd, pT_sl, rden, tri):
        LA = len(pT_sl) - 1
        np_ = len(q_parts)
        for qt in range(NTT):
            b_o = psr(0, 2, "ao")
            b_d = psr(2, 4, "ad")
            nkb = 4 * qt + 4
            info = {}

            def stage_ab(kb):
                off = max(0, kb - 4 * qt) * 128
                ncol = TT - off
                q0 = qt * TT + off
                b_s = psr(4, 8, "as")
                for i in range(np_):
                    k.op("pe", lambda e, i=i, kb=kb, q0=q0, ncol=ncol, b_s=b_s: e.matmul(
                        ps[b_s][:, 0:ncol], lhsT=k_parts[i](kb), rhs=q_parts[i](q0, q0 + ncol),
                        start=(i == 0), stop=(i == np_ - 1)),
                        reads=rd, writes=[("ps", b_s)], inc=(i == np_ - 1))
                pi = psr(0, len(pT_sl), "pT")
                pT = pT_sl[pi]
                k.op("act", lambda e, b_s=b_s, ncol=ncol, pT=pT: e.activation(
                    out=pT[:, 0:ncol], in_=ps[b_s][:, 0:ncol], func=AF.Exp, scale=scale),
                    reads=[("ps", b_s)], writes=[("pT", pi)])
                if kb >= 4 * qt:
                    k.op("dve", lambda e, pT=pT: e.tensor_tensor(out=pT[:, 0:128], in0=pT[:, 0:128], in1=tri[:], op=ALU.mult),
                         reads=[("pT", pi), "tri"], writes=[("pT", pi)])
                info[kb] = (off, ncol, pi, pT)

            def stage_c(kb):
                off, ncol, pi, pT = info[kb]
                k.op("pe", lambda e, kb=kb, off=off, ncol=ncol, pT=pT, b_o=b_o: e.matmul(
                    ps[b_o][:, off:TT], lhsT=v_fn(kb), rhs=pT[:, 0:ncol], start=(kb == 0), stop=(kb == nkb - 1)),
                    reads=rd + [("pT", pi)], writes=[("ps", b_o)], inc=False)
                k.op("pe", lambda e, kb=kb, off=off, ncol=ncol, pT=pT, b_d=b_d: e.matmul(
                    ps[b_d][:, off:TT], lhsT=ones_bf[:], rhs=pT[:, 0:ncol], start=(kb == 0), stop=(kb == nkb - 1)),
                    reads=[("pT", pi), "ones"], writes=[("ps", b_d)], inc=True)

            for step in range(nkb + LA):
                if step < nkb:
                    stage_ab(step)
                if step - LA >= 0:
                    stage_c(step - LA)
            k.op("dve", lambda e, b_d=b_d: e.reciprocal(out=rden[:], in_=ps[b_d][:, :]),
                 reads=[("ps", b_d)], writes=["rden"])
            out_cb(qt, b_o)

    def rope_tables(b, cosb, sinb):
        m = ar.mark()
        posi = ar.alloc("posi", [128, S], I32)
        ang = ar.alloc("ang", [128, S], F32)
        kk = ar.alloc("kk", [128, S], F32)
        r2 = ar.alloc("r2", [128, S], F32)
        pos_d = dget("pos", [NB, S], I32)
        invf_d = dget("invf4", [128, 1])
        invf = ar.alloc("invf", [128, 1], F32)
        k.dma("sp", lambda e: e.dma_start(out=posi[:], in_=pos_d[b:b + 1, :].to_broadcast([128, S])), "ld_small", writes=["posi"])
        k.dma("sp", lambda e: e.dma_start(out=invf[:], in_=invf_d[:, :]), "ld_small", writes=["invf"])
        MAGIC = 12582912.0
        C1 = 6.28125
        C2 = 2.0 * np.pi - 6.28125
        PI = 3.14159
        k.op("dve", lambda e: e.tensor_copy(out=ang[:], in_=posi[:]), reads=["posi"], writes=["ang"])
        k.op("dve", lambda e: e.tensor_scalar(out=ang[:], in0=ang[:], scalar1=invf[:, 0:1], scalar2=None, op0=ALU.mult),
             reads=["ang", "invf"], writes=["ang"])
        k.op("dve", lambda e: e.tensor_scalar(out=kk[:], in0=ang[:], scalar1=float(1.0 / (2 * np.pi)), scalar2=None, op0=ALU.mult),
             reads=["ang"], writes=["kk"])
        k.op("dve", lambda e: e.tensor_scalar(out=kk[:], in0=kk[:], scalar1=MAGIC, scalar2=None, op0=ALU.add),
             reads=["kk"], writes=["kk"])
        k.op("dve", lambda e: e.tensor_scalar(out=kk[:], in0=kk[:], scalar1=MAGIC, scalar2=None, op0=ALU.subtract),
             reads=["kk"], writes=["kk"])
        k.op("dve", lambda e: e.scalar_tensor_tensor(out=ang[:], in0=kk[:], scalar=-C1, in1=ang[:], op0=ALU.mult, op1=ALU.add),
             reads=["kk", "ang"], writes=["ang"])
        k.op("dve", lambda e: e.scalar_tensor_tensor(out=ang[:], in0=kk[:], scalar=float(-C2), in1=ang[:], op0=ALU.mult, op1=ALU.add),
             reads=["kk", "ang"], writes=["ang"])
        k.op("dve", lambda e: e.tensor_scalar(out=r2[:], in0=ang[:], scalar1=float(np.pi / 2), scalar2=None, op0=ALU.add),
             reads=["ang"], writes=["r2"])
        k.op("dve", lambda e: e.tensor_scalar(out=kk[:], in0=r2[:], scalar1=float(np.pi), scalar2=None, op0=ALU.is_gt),
             reads=["r2"], writes=["kk"])
        k.op("dve", lambda e: e.scalar_tensor_tensor(out=r2[:], in0=kk[:], scalar=float(-2 * np.pi), in1=r2[:], op0=ALU.mult, op1=ALU.add),
             reads=["kk", "r2"], writes=["r2"])
        for t_, nm in ((ang, "ang"), (r2, "r2")):
            k.op("dve", lambda e, t_=t_: e.tensor_scalar(out=t_[:], in0=t_[:], scalar1=-PI, scalar2=PI, op0=ALU.max, op1=ALU.min),
                 reads=[nm], writes=[nm])
        k.op("act", lambda e: e.activation(out=sinb[:], in_=ang[:], func=AF.Sin), reads=["ang"], writes=["sinb"])
        k.op("act", lambda e: e.activation(out=cosb[:], in_=r2[:], func=AF.Sin), reads=["r2"], writes=["cosb"])
        k.barrier(dma_keys=["ld_small"])
        ar.reset(m)

    def rms_scale(bank, n, eps_t, dst):
        k.op("act", lambda e: e.activation(out=dst, in_=ps[bank][:, :], func=AF.Ln, bias=eps_t[:, 0:1], scale=1.0 / n),
             reads=[("ps", bank), "epsrms"], writes=["rs_tmp"])
        k.op("act", lambda e: e.activation(out=dst, in_=dst, func=AF.Exp, scale=-0.5),
             reads=["rs_tmp"], writes=["rs_tmp"])

    def load_const(name, shape, tile_, key):
        d_ = dget(name, shape)
        k.dma("sp", lambda e: e.dma_start(out=tile_[:], in_=d_), "ld_small", writes=[key])

    def mla_sublayer(l, b):
        a = l // 3
        m = ar.mark()
        scale = 192.0 ** -0.5
        epsr = ar.alloc("epsr", [128, 1], F32)
        k.op("dve", lambda e: e.memset(epsr[:], RMS_EPS), writes=["epsrms"])
        tri_f = ar.alloc("trif", [128, 128], F32)
        tri = ar.alloc("tri", [128, 128], BF16)
        load_const("tri", [128, 128], tri_f, "trif")
        k.op("dve", lambda e: e.tensor_copy(out=tri[:], in_=tri_f[:]), reads=["trif"], writes=["tri"])
        qg = ar.alloc("qg", [128, 3], F32)
        kvg = ar.alloc("kvg", [128, 2], F32)
        load_const("mla_qg_%d" % a, [128, 3], qg, "qg")
        load_const("mla_kvg_%d" % a, [128, 2], kvg, "kvg")
        cosb = ar.alloc("cosb", [128, S], BF16)
        sinb = ar.alloc("sinb", [128, S], BF16)
        rope_tables(b, cosb, sinb)
        cqn = ar.alloc("cqn", [128, 3, S], BF16)
        ckvn = ar.alloc("ckvn", [128, 2, S], BF16)
        kRz = [ar.alloc("kRz", [128, S], BF16) for _ in range(2)]
        for i_ in range(2):
            k.op("dve", lambda e, i_=i_: e.memset(kRz[i_][:], 0.0), writes=[("kR", t_) for t_ in range(NTT)])
        mB = ar.mark()
        hT = ar.alloc("hT", [128, 8, S], BF16)
        wmi = ar.alloc("wmi", [128, 8, 704], BF16)
        wkr2 = ar.alloc("wkr2", [128, 8, 128], BF16)
        wkrot2 = ar.alloc("wkrot2", [128, 8, 128], BF16)
        lat = ar.alloc("lat", [128, 5, TT], F32)
        sqb = ar.alloc("sqb", [128, 5, TT], BF16)
        rsq = ar.alloc("rsq", [128, TT], F32)
        t1 = ar.alloc("t1", [128, TT], F32)
        t2 = ar.alloc("t2", [128, TT], F32)
        win_d = dget("mla_win_%d" % a, [8, 128, 704])
        for kc in range(8):
            k.dma("pool", lambda e, kc=kc: e.dma_start(out=wmi[:, kc, :], in_=win_d[kc, :, :]), "ld_mw", writes=["wmi"])
        for hh in range(2):
            o = hh * 64
            k.op("act", lambda e, o=o: e.activation(out=wkr2[:, :, o:o + 64], in_=wmi[:, :, 640:704], func=AF.Copy),
                 reads=["wmi"], writes=["wkr2"])
            k.op("act", lambda e, o=o: e.activation(out=wkrot2[:, :, o:o + 32], in_=wmi[:, :, 672:704], func=AF.Copy, scale=-1.0),
                 reads=["wmi"], writes=["wkrot2"])
            k.op("act", lambda e, o=o: e.activation(out=wkrot2[:, :, o + 32:o + 64], in_=wmi[:, :, 640:672], func=AF.Copy),
                 reads=["wmi"], writes=["wkrot2"])
        modulate(l, 1, b, hT, "hm", list(range(NTT)))
        for tt in range(NTT):
            tsl = slice(tt * TT, (tt + 1) * TT)
            for mc in range(5):
                bk = psb()
                for kc in range(8):
                    k.op("pe", lambda e, tsl=tsl, bk=bk, mc=mc, kc=kc: e.matmul(
                        ps[bk][:, :], lhsT=wmi[:, kc, mc * 128:(mc + 1) * 128], rhs=hT[:, kc, tsl],
                        start=(kc == 0), stop=(kc == 7)),
                        reads=["wmi", ("hm", kc, tt)], writes=[("ps", bk)], inc=(kc == 7))
                k.op("act", lambda e, tsl=tsl, bk=bk, mc=mc: e.activation(out=lat[:, mc, :], in_=ps[bk][:, :], func=AF.Copy),
                     reads=[("ps", bk)], writes=[("lat", mc)])
                k.op("act", lambda e, tsl=tsl, bk=bk, mc=mc: e.activation(out=sqb[:, mc, :], in_=ps[bk][:, :], func=AF.Square),
                     reads=[("ps", bk)], writes=[("sqb", mc)])
            for (c0, c1, n, gt, gk, dst, dk) in ((0, 3, 384, qg, "qg", cqn, "cqn"), (3, 5, 256, kvg, "kvg", ckvn, "ckvn")):
                bk = psb()
                for mc in range(c0, c1):
                    k.op("pe", lambda e, tsl=tsl, bk=bk, mc=mc, c0=c0, c1=c1: e.matmul(
                        ps[bk][:, :], lhsT=ones_bf[:], rhs=sqb[:, mc, :], start=(mc == c0), stop=(mc == c1 - 1)),
                        reads=[("sqb", mc), "ones"], writes=[("ps", bk)], inc=(mc == c1 - 1))
                rms_scale(bk, n, epsr, rsq[:])
                for mc in range(c0, c1):
                    k.op("dve", lambda e, tsl=tsl, mc=mc, c0=c0, gt=gt, dst=dst: e.scalar_tensor_tensor(
                        out=dst[:, mc - c0, tsl], in0=lat[:, mc, :], scalar=gt[:, mc - c0:mc - c0 + 1], in1=rsq[:],
                        op0=ALU.mult, op1=ALU.mult),
                        reads=[("lat", mc), gk, "rs_tmp"], writes=[(dk, mc - c0, tt)])
            b1, b2 = psb(), psb()
            for (bk, wt, wk_) in ((b1, wkr2, "wkr2"), (b2, wkrot2, "wkrot2")):
                for kc in range(8):
                    k.op("pe", lambda e, tsl=tsl, bk=bk, wt=wt, kc=kc: e.matmul(
                        ps[bk][:, :], lhsT=wt[:, kc, :], rhs=hT[:, kc, tsl], start=(kc == 0), stop=(kc == 7)),
                        reads=[wk_, ("hm", kc, tt)], writes=[("ps", bk)], inc=(kc == 7))
            k.op("dve", lambda e, tsl=tsl, b1=b1: e.tensor_tensor(out=t1[:], in0=ps[b1][:, :], in1=cosb[:, tsl], op=ALU.mult),
                 reads=[("ps", b1), "cosb"], writes=["t1"])
            k.op("dve", lambda e, tsl=tsl, b2=b2: e.tensor_tensor(out=t2[:], in0=ps[b2][:, :], in1=sinb[:, tsl], op=ALU.mult),
                 reads=[("ps", b2), "sinb"], writes=["t2"])
            for i_ in range(2):
                lo_, hi_ = i_ * 64, i_ * 64 + 64
                k.op("dve", lambda e, tsl=tsl, i_=i_, lo_=lo_, hi_=hi_: e.tensor_tensor(
                    out=kRz[i_][lo_:hi_, tsl], in0=t1[lo_:hi_, :], in1=t2[lo_:hi_, :], op=ALU.add),
                    reads=["t1", "t2"], writes=[("kR", tt)])
        k.barrier(dma_keys=["ld_mw", "ld_small"])
        dump("cosb", cosb[:], [128, S], BF16, ["cosb"])
        dump("sinb", sinb[:], [128, S], BF16, ["sinb"])
        dump("cqn", cqn[:], [128, 3, S], BF16, [])
        dump("ckvn", ckvn[:], [128, 2, S], BF16, [])
        dump("hT", hT[:], [128, 8, S], BF16, [])
        k.barrier(dma_keys=["st_x"])
        ar.reset(mB)
        oT = ar.alloc("oT", [128, 8, S], BF16)
        mB = ar.mark()
        wqn = ar.alloc("wqn", [128, 3, 256], BF16)
        wqr = ar.alloc("wqr", [128, 3, 128], BF16)
        wqrot = ar.alloc("wqrot", [128, 3, 128], BF16)
        wk = ar.alloc("wk", [128, 2, 256], BF16)
        wv = ar.alloc("wv", [128, 2, 256], BF16)
        qn = ar.alloc("qn", [128, 2, S], BF16)
        qr = ar.alloc("qr", [128, S], BF16)
        kn = ar.alloc("kn", [128, 2, S], BF16)
        V = ar.alloc("V", [128, 16, 256], BF16)
        pT_sl = [ar.alloc("pT", [128, TT], BF16) for _ in range(4)]
        rden = ar.alloc("rden", [128, TT], F32)
        t1 = ar.alloc("t1", [128, TT], F32)
        t2 = ar.alloc("t2", [128, TT], F32)
        wqn_d = dget("mla_wqn_%d" % a, [3, 128, 1024])
        wqr_d = dget("mla_wqr_%d" % a, [3, 128, 512])
        wk_d = dget("mla_wk_%d" % a, [2, 128, 1024])
        wv_d = dget("mla_wv_%d" % a, [2, 128, 1024])
        for pr in range(4):
            for kc in range(3):
                k.dma("pool", lambda e, kc=kc, pr=pr: e.dma_start(out=wqn[:, kc, :], in_=wqn_d[kc, :, pr * 256:(pr + 1) * 256]),
                      "ld_mw", writes=["wqn"])
                k.dma("pool", lambda e, kc=kc, pr=pr: e.dma_start(out=wqr[:, kc, :], in_=wqr_d[kc, :, pr * 128:(pr + 1) * 128]),
                      "ld_mw", writes=["wqr"])
            for kc in range(2):
                k.dma("pool", lambda e, kc=kc, pr=pr: e.dma_start(out=wk[:, kc, :], in_=wk_d[kc, :, pr * 256:(pr + 1) * 256]),
                      "ld_mw", writes=["wk"])
                k.dma("pool", lambda e, kc=kc, pr=pr: e.dma_start(out=wv[:, kc, :], in_=wv_d[kc, :, pr * 256:(pr + 1) * 256]),
                      "ld_mw", writes=["wv"])
            for hh in range(2):
                o = hh * 64
                k.op("act", lambda e, o=o: e.activation(out=wqrot[:, :, o:o + 32], in_=wqr[:, :, o + 32:o + 64], func=AF.Copy, scale=-1.0),
                     reads=["wqr"], writes=["wqrot"])
                k.op("act", lambda e, o=o: e.activation(out=wqrot[:, :, o + 32:o + 64], in_=wqr[:, :, o:o + 32], func=AF.Copy),
                     reads=["wqr"], writes=["wqrot"])
            for tt in range(NTT):
                tsl = slice(tt * TT, (tt + 1) * TT)
                for hh in range(2):
                    bk = psb()
                    for kc in range(3):
                        k.op("pe", lambda e, bk=bk, hh=hh, kc=kc, tsl=tsl: e.matmul(
                            ps[bk][:, :], lhsT=wqn[:, kc, hh * 128:(hh + 1) * 128], rhs=cqn[:, kc, tsl],
                            start=(kc == 0), stop=(kc == 2)),
                            reads=["wqn", ("cqn", kc, tt)], writes=[("ps", bk)], inc=(kc == 2))
                    k.op("act", lambda e, bk=bk, hh=hh, tsl=tsl: e.activation(out=qn[:, hh, tsl], in_=ps[bk][:, :], func=AF.Copy),
                         reads=[("ps", bk)], writes=[("qn", hh, tt)])
                    bk = psb()
                    for kc in range(2):
                        k.op("pe", lambda e, bk=bk, hh=hh, kc=kc, tsl=tsl: e.matmul(
                            ps[bk][:, :], lhsT=wk[:, kc, hh * 128:(hh + 1) * 128], rhs=ckvn[:, kc, tsl],
                            start=(kc == 0), stop=(kc == 1)),
                            reads=["wk", ("ckvn", kc, tt)], writes=[("ps", bk)], inc=(kc == 1))
                    k.op("act", lambda e, bk=bk, hh=hh, tsl=tsl: e.activation(out=kn[:, hh, tsl], in_=ps[bk][:, :], func=AF.Copy),
                         reads=[("ps", bk)], writes=[("kn", hh, tt)])
                b1, b2 = psb(), psb()
                for (bk, wt, wk_) in ((b1, wqr, "wqr"), (b2, wqrot, "wqrot")):
                    for kc in range(3):
                        k.op("pe", lambda e, bk=bk, wt=wt, kc=kc, tsl=tsl: e.matmul(
                            ps[bk][:, :], lhsT=wt[:, kc, :], rhs=cqn[:, kc, tsl], start=(kc == 0), stop=(kc == 2)),
                            reads=[wk_, ("cqn", kc, tt)], writes=[("ps", bk)], inc=(kc == 2))
                k.op("dve", lambda e, b1=b1, tsl=tsl: e.tensor_tensor(out=t1[:], in0=ps[b1][:, :], in1=cosb[:, tsl], op=ALU.mult),
                     reads=[("ps", b1), "cosb"], writes=["t1"])
                k.op("dve", lambda e, b2=b2, tsl=tsl: e.tensor_tensor(out=t2[:], in0=ps[b2][:, :], in1=sinb[:, tsl], op=ALU.mult),
                     reads=[("ps", b2), "sinb"], writes=["t2"])
                k.op("dve", lambda e, tsl=tsl: e.tensor_tensor(out=qr[:, tsl], in0=t1[:], in1=t2[:], op=ALU.add),
                     reads=["t1", "t2"], writes=[("qr", tt)])
                for tb in range(4):
                    gtb = tt * 4 + tb
                    bk = psb()
                    for kc in range(2):
                        k.op("pe", lambda e, bk=bk, kc=kc, gtb=gtb: e.matmul(
                            ps[bk][:, 0:256], lhsT=ckvn[:, kc, gtb * 128:(gtb + 1) * 128], rhs=wv[:, kc, :],
                            start=(kc == 0), stop=(kc == 1)),
                            reads=["wv", ("ckvn", kc, tt)], writes=[("ps", bk)], inc=(kc == 1))
                    k.op("act", lambda e, bk=bk, gtb=gtb: e.activation(out=V[:, gtb, :], in_=ps[bk][:, 0:256], func=AF.Copy),
                         reads=[("ps", bk)], writes=[("V", gtb)])
            rd_all = [("qn", hh_, t_) for hh_ in range(2) for t_ in range(NTT)] + [("kn", hh_, t_) for hh_ in range(2) for t_ in range(NTT)] \
                + [("qr", t_) for t_ in range(NTT)] + [("kR", t_) for t_ in range(NTT)] + [("V", g_) for g_ in range(16)]
            for hh in range(2):
                h = pr * 2 + hh
                lo, hi = hh * 64, hh * 64 + 64

                def out_cb(qt, b_o, h=h):
                    k.op("dve", lambda e, qt=qt, b_o=b_o, h=h: e.tensor_tensor(
                        out=oT[:, h, qt * TT:(qt + 1) * TT], in0=ps[b_o][:, :], in1=rden[:], op=ALU.mult),
                        reads=[("ps", b_o), "rden"], writes=[("oT", h, qt)])

                attention_head(
                    q_parts=[lambda c0, c1, hh=hh: qn[:, hh, c0:c1], lambda c0, c1: qr[:, c0:c1]],
                    k_parts=[lambda kb, hh=hh: kn[:, hh, kb * 128:(kb + 1) * 128], lambda kb, hh=hh: kRz[hh][:, kb * 128:(kb + 1) * 128]],
                    v_fn=lambda kb, hh=hh: V[:, kb, hh * 128:(hh + 1) * 128],
                    out_cb=out_cb, scale=scale, rd=rd_all, pT_sl=pT_sl, rden=rden, tri=tri)
        k.barrier(dma_keys=["ld_mw"])
        dump("tri", tri[:], [128, 128], BF16, [])
        dump("kRz0", kRz[0][:], [128, S], BF16, [])
        dump("kRz1", kRz[1][:], [128, S], BF16, [])
        for i_ in range(4):
            dump("pT%d" % i_, pT_sl[i_][:], [128, TT], BF16, [])
        dump("rden", rden[:], [128, TT], F32, [])
        dump("qn", qn[:], [128, 2, S], BF16, [])
        dump("qr", qr[:], [128, S], BF16, [])
        dump("kn", kn[:], [128, 2, S], BF16, [])
        dump("V", V[:], [128, 16, 256], BF16, [])
        dump("oT", oT[:], [128, 8, S], BF16, [])
        k.barrier(dma_keys=["st_x"])
        ar.reset(mB)
        out_proj_and_ln(l, b, oT, 8, "mla_wo_%d" % a, "oT")
        k.barrier()
        ar.reset(m)

    def diff_sublayer(l, b):
        m = ar.mark()
        scale = 64.0 ** -0.5
        lam_init = 0.8 - 0.6 * float(np.exp(-0.3 * l))
        epsr = ar.alloc("epsr", [128, 1], F32)
        k.op("dve", lambda e: e.memset(epsr[:], RMS_EPS), writes=["epsrms"])
        tri_f = ar.alloc("trif", [128, 128], F32)
        tri = ar.alloc("tri", [128, 128], BF16)
        load_const("tri", [128, 128], tri_f, "trif")
        k.op("dve", lambda e: e.tensor_copy(out=tri[:], in_=tri_f[:]), reads=["trif"], writes=["tri"])
        lq = ar.alloc("lq", [128, 128], F32)
        lk = ar.alloc("lk", [128, 128], F32)
        lam = ar.alloc("lam", [128, 4], F32)
        gsub = ar.alloc("gsub", [128, 1], F32)
        lq_d = dget("diff_lq", [1, 128])
        lk_d = dget("diff_lk", [1, 128])
        k.dma("sp", lambda e: e.dma_start(out=lq[:], in_=lq_d[0:1, :].to_broadcast([128, 128])), "ld_small", writes=["lq"])
        k.dma("sp", lambda e: e.dma_start(out=lk[:], in_=lk_d[0:1, :].to_broadcast([128, 128])), "ld_small", writes=["lk"])
        load_const("diff_g", [128, 1], gsub, "gsub")
        k.op("dve", lambda e: e.tensor_tensor(out=lq[:], in0=lq[:], in1=lk[:], op=ALU.mult), reads=["lq", "lk"], writes=["lq"])
        k.op("dve", lambda e: e.tensor_reduce(out=lam[:, 0:2], in_=lq[:].rearrange("p (a b) -> p a b", a=2),
                                              axis=mybir.AxisListType.X, op=ALU.add), reads=["lq"], writes=["lam"])
        k.op("act", lambda e: e.activation(out=lam[:, 0:2], in_=lam[:, 0:2], func=AF.Exp), reads=["lam"], writes=["lam"])
        k.op("dve", lambda e: e.tensor_tensor(out=lam[:, 2:3], in0=lam[:, 1:2], in1=lam[:, 0:1], op=ALU.subtract),
             reads=["lam"], writes=["lam"])
        k.op("dve", lambda e: e.tensor_scalar(out=lam[:, 3:4], in0=lam[:, 2:3], scalar1=-lam_init, scalar2=None, op0=ALU.add),
             reads=["lam"], writes=["lam"])
        k.op("dve", lambda e: e.tensor_scalar(out=gsub[:], in0=gsub[:], scalar1=1.0 - lam_init, scalar2=None, op0=ALU.mult),
             reads=["gsub"], writes=["gsub"])
        hT = ar.alloc("hT", [128, 8, S], BF16)
        oT = ar.alloc("oT", [128, 8, S], BF16)
        mB = ar.mark()
        qT = ar.alloc("qT", [128, 2, S], BF16)
        kz = [ar.alloc("kz", [128, 2, S], BF16) for _ in range(2)]
        for i_ in range(2):
            k.op("dve", lambda e, i_=i_: e.memset(kz[i_][:], 0.0), writes=[("kT", hh_, t_) for hh_ in range(2) for t_ in range(NTT)])
        V = ar.alloc("V", [128, 16, 256], BF16)
        o0 = ar.alloc("o0", [128, S], F32)
        pT_sl = [ar.alloc("pT", [128, TT], BF16) for _ in range(3)]
        rden = ar.alloc("rden", [128, TT], F32)
        t1 = ar.alloc("t1", [128, TT], F32)
        sq = ar.alloc("sq", [128, TT], BF16)
        rs = ar.alloc("rs", [128, TT], F32)
        wd = dget("diff_wi", [12, 128, 2048])
        modulate(l, 1, b, hT, "hm", list(range(NTT)))

        def load_grp(g):
            sl = cnt["wi"] % NWI
            cnt["wi"] += 1
            t = wi_sl[sl]
            k.dma("pool", lambda e: e.dma_start(out=t[:].rearrange("p a b c -> p (a b c)"), in_=wd[g, :, :]),
                  ("wi", sl), writes=[("wi", sl)])
            return sl, t

        for pr in range(4):
            for (g, dst, dk) in ((pr, qT, "qT"), (4 + pr, None, "kT")):
                sl, w = load_grp(g)
                for tt in range(NTT):
                    for hh in range(2):
                        bk = psb()
                        for kc in range(8):
                            k.op("pe", lambda e, bk=bk, w=w, kc=kc, hh=hh, tt=tt: e.matmul(
                                ps[bk][:, :], lhsT=w[:, kc, hh, :], rhs=hT[:, kc, tt * TT:(tt + 1) * TT],
                                start=(kc == 0), stop=(kc == 7)),
                                reads=[("wi", sl), ("hm", kc, tt)], writes=[("ps", bk)], inc=(kc == 7))
                        if dst is not None:
                            k.op("act", lambda e, bk=bk, dst=dst, hh=hh, tt=tt: e.activation(
                                out=dst[:, hh, tt * TT:(tt + 1) * TT], in_=ps[bk][:, :], func=AF.Copy),
                                reads=[("ps", bk)], writes=[(dk, hh, tt)])
                        else:
                            for i_ in range(2):
                                lo_, hi_ = i_ * 64, i_ * 64 + 64
                                k.op("act", lambda e, bk=bk, hh=hh, tt=tt, i_=i_, lo_=lo_, hi_=hi_: e.activation(
                                    out=kz[i_][lo_:hi_, hh, tt * TT:(tt + 1) * TT], in_=ps[bk][lo_:hi_, :], func=AF.Copy),
                                    reads=[("ps", bk)], writes=[(dk, hh, tt)])
            sl, w = load_grp(8 + pr)
            for gtb in range(16):
                bk = psb()
                for kc in range(8):
                    k.op("pe", lambda e, bk=bk, w=w, kc=kc, gtb=gtb: e.matmul(
                        ps[bk][:, 0:256], lhsT=hT[:, kc, gtb * 128:(gtb + 1) * 128],
                        rhs=w[:, kc, :, :].rearrange("p a b -> p (a b)"), start=(kc == 0), stop=(kc == 7)),
                        reads=[("wi", sl), ("hm", kc, gtb // 4)], writes=[("ps", bk)], inc=(kc == 7))
                k.op("act", lambda e, bk=bk, gtb=gtb: e.activation(out=V[:, gtb, :], in_=ps[bk][:, 0:256], func=AF.Copy),
                     reads=[("ps", bk)], writes=[("V", gtb)])
            rd_all = [("qT", hh_, t_) for hh_ in range(2) for t_ in range(NTT)] + [("kT", hh_, t_) for hh_ in range(2) for t_ in range(NTT)] \
                + [("V", g_) for g_ in range(16)]
            for hh in range(2):
                h = pr * 2 + hh
                for i in range(2):
                    lo, hi = i * 64, i * 64 + 64

                    def out_cb(qt, b_o, h=h, i=i):
                        qs = slice(qt * TT, (qt + 1) * TT)
                        if i == 0:
                            k.op("dve", lambda e: e.tensor_tensor(out=o0[:, qs], in0=ps[b_o][:, :], in1=rden[:], op=ALU.mult),
                                 reads=[("ps", b_o), "rden"], writes=[("o0", qt)])
                            return
                        k.op("dve", lambda e: e.tensor_tensor(out=t1[:], in0=ps[b_o][:, :], in1=rden[:], op=ALU.mult),
                             reads=[("ps", b_o), "rden"], writes=["t1"])
                        k.op("dve", lambda e: e.scalar_tensor_tensor(out=t1[:], in0=t1[:], scalar=lam[:, 3:4], in1=o0[:, qs],
                                                                     op0=ALU.mult, op1=ALU.add),
                             reads=["t1", "lam", ("o0", qt)], writes=["t1"])
                        k.op("act", lambda e: e.activation(out=sq[:], in_=t1[:], func=AF.Square), reads=["t1"], writes=["sq"])
                        bk = psb()
                        k.op("pe", lambda e: e.matmul(ps[bk][:, :], lhsT=ones_bf[:], rhs=sq[:], start=True, stop=True),
                             reads=["sq", "ones"], writes=[("ps", bk)])
                        rms_scale(bk, 128, epsr, rs[:])
                        k.op("dve", lambda e: e.scalar_tensor_tensor(out=oT[:, h, qs], in0=t1[:], scalar=gsub[:, 0:1], in1=rs[:],
                                                                     op0=ALU.mult, op1=ALU.mult),
                             reads=["t1", "gsub", "rs_tmp"], writes=[("oT", h, qt)])

                    attention_head(
                        q_parts=[lambda c0, c1, hh=hh: qT[:, hh, c0:c1]],
                        k_parts=[lambda kb, hh=hh, i=i: kz[i][:, hh, kb * 128:(kb + 1) * 128]],
                        v_fn=lambda kb, hh=hh: V[:, kb, hh * 128:(hh + 1) * 128],
                        out_cb=out_cb, scale=scale, rd=rd_all, pT_sl=pT_sl, rden=rden, tri=tri)
        k.barrier(dma_keys=["ld_small"])
        dump("d_hT", hT[:], [128, 8, S], BF16, [])
        dump("d_qT", qT[:], [128, 2, S], BF16, [])
        dump("d_kz0", kz[0][:], [128, 2, S], BF16, [])
        dump("d_kz1", kz[1][:], [128, 2, S], BF16, [])
        dump("d_V", V[:], [128, 16, 256], BF16, [])
        dump("d_oT", oT[:], [128, 8, S], BF16, [])
        dump("d_lam", lam[:], [128, 4], F32, [])
        dump("d_gsub", gsub[:], [128, 1], F32, [])
        dump("d_tri", tri[:], [128, 128], BF16, [])
        k.barrier(dma_keys=["st_x"])
        ar.reset(mB)
        out_proj_and_ln(l, b, oT, 8, "diff_wo", "oT")
        k.barrier()
        ar.reset(m)

    def ssd_sublayer(l, b):
        m = ar.mark()
        NH = 32
        ident_f = ar.alloc("identf", [128, 128], F32)
        ident_b = ar.alloc("identb", [128, 128], BF16)
        tri_f = ar.alloc("trif", [128, 128], F32)
        tri = ar.alloc("tri", [128, 128], BF16)
        ones_f = ar.alloc("onesf", [128, 128], F32)
        onec = ar.alloc("onec", [128, 1], F32)
        epsr = ar.alloc("epsr", [128, 1], F32)
        cw = ar.alloc("cw", [128, 24, 4], F32)
        cb = ar.alloc("cb", [128, 24], F32)
        dsk = ar.alloc("dsk", [128, 16], F32)
        ng = ar.alloc("ng", [128, 16], F32)
        a_b = ar.alloc("a_b", [128, NH], F32)
        dtb = ar.alloc("dtb", [128, NH], F32)
        wdt = ar.alloc("wdt", [128, 8, NH], BF16)
        load_const("ident", [128, 128], ident_f, "identf")
        load_const("tri", [128, 128], tri_f, "trif")
        k.op("dve", lambda e: e.tensor_copy(out=ident_b[:], in_=ident_f[:]), reads=["identf"], writes=["identb"])
        k.op("dve", lambda e: e.tensor_copy(out=tri[:], in_=tri_f[:]), reads=["trif"], writes=["tri"])
        k.op("dve", lambda e: e.memset(ones_f[:], 1.0), writes=["onesf"])
        k.op("dve", lambda e: e.memset(onec[:], 1.0), writes=["onec"])
        k.op("dve", lambda e: e.memset(epsr[:], RMS_EPS), writes=["epsrms"])
        cw_d = dget("ssm_cw", [128, 96])
        k.dma("sp", lambda e: e.dma_start(out=cw[:].rearrange("p a b -> p (a b)"), in_=cw_d[:, :]), "ld_small", writes=["cw"])
        load_const("ssm_cb", [128, 24], cb, "cb")
        load_const("ssm_dsk", [128, 16], dsk, "dsk")
        load_const("ssm_ng", [128, 16], ng, "ng")
        al_d = dget("ssm_alog", [1, NH])
        db_d = dget("ssm_dtb", [1, NH])
        k.dma("sp", lambda e: e.dma_start(out=a_b[:], in_=al_d[0:1, :].to_broadcast([128, NH])), "ld_small", writes=["a_b"])
        k.dma("sp", lambda e: e.dma_start(out=dtb[:], in_=db_d[0:1, :].to_broadcast([128, NH])), "ld_small", writes=["dtb"])
        k.op("act", lambda e: e.activation(out=a_b[:], in_=a_b[:], func=AF.Exp), reads=["a_b"], writes=["a_b"])
        k.op("dve", lambda e: e.tensor_scalar(out=a_b[:], in0=a_b[:], scalar1=-1.0, scalar2=None, op0=ALU.mult),
             reads=["a_b"], writes=["a_b"])
        wdt_d = dget("ssm_wdt", [128, 8 * NH])
        k.dma("pool", lambda e: e.dma_start(out=wdt[:].rearrange("p a b -> p (a b)"), in_=wdt_d[:, :]), "ld_mw", writes=["wdt"])
        hT = ar.alloc("hT", [128, 8, S], BF16)
        modulate(l, 1, b, hT, "hm", list(range(NTT)))
        hm_all = [("hm", kc_, t_) for kc_ in range(8) for t_ in range(NTT)]
        dt_tok = ar.alloc("dt_tok", [128, 16, NH], F32)
        dA_tok = ar.alloc("dA_tok", [128, 16, NH], F32)
        A_tok = ar.alloc("A_tok", [128, 16, NH], F32)
        for tb in range(16):
            bk = psb()
            for kc in range(8):
                k.op("pe", lambda e, bk=bk, kc=kc, tb=tb: e.matmul(
                    ps[bk][:, 0:NH], lhsT=hT[:, kc, tb * 128:(tb + 1) * 128], rhs=wdt[:, kc, :], start=(kc == 0), stop=(kc == 7)),
                    reads=["wdt"] + hm_all, writes=[("ps", bk)], inc=(kc == 7))
            k.op("dve", lambda e, bk=bk, tb=tb: e.tensor_tensor(out=dt_tok[:, tb, :], in0=ps[bk][:, 0:NH], in1=dtb[:], op=ALU.add),
                 reads=[("ps", bk), "dtb"], writes=["dt_tok"])
        k.op("act", lambda e: e.activation(out=dt_tok[:], in_=dt_tok[:], func=AF.Exp), reads=["dt_tok"], writes=["dt_tok"])
        k.op("act", lambda e: e.activation(out=dt_tok[:], in_=dt_tok[:], func=AF.Ln, bias=onec[:, 0:1], scale=1.0),
             reads=["dt_tok", "onec"], writes=["dt_tok"])
        k.op("dve", lambda e: e.tensor_tensor(out=dA_tok[:], in0=dt_tok[:], in1=a_b[:].unsqueeze(1).to_broadcast([128, 16, NH]), op=ALU.mult),
             reads=["dt_tok", "a_b"], writes=["dA_tok"])
        for tb in range(16):
            bk = psb()
            for t2 in range(tb + 1):
                k.op("pe", lambda e, bk=bk, t2=t2, tb=tb: e.matmul(
                    ps[bk][:, 0:NH], lhsT=(tri_f[:] if t2 == tb else ones_f[:]), rhs=dA_tok[:, t2, :],
                    start=(t2 == 0), stop=(t2 == tb)),
                    reads=["dA_tok", "trif", "onesf"], writes=[("ps", bk)], inc=(t2 == tb))
            k.op("act", lambda e, bk=bk, tb=tb: e.activation(out=A_tok[:, tb, :], in_=ps[bk][:, 0:NH], func=AF.Copy),
                 reads=[("ps", bk)], writes=["A_tok"])
        nA_tok = ar.alloc("nA_tok", [128, 16, NH], F32)
        k.op("dve", lambda e: e.tensor_scalar(out=nA_tok[:], in0=A_tok[:], scalar1=-1.0, scalar2=None, op0=ALU.mult),
             reads=["A_tok"], writes=["nA_tok"])
        BT = ar.alloc("BT", [128, S], BF16)
        CT = ar.alloc("CT", [128, S], BF16)
        HS = S // 2
        pc = ar.alloc("pc", [128, 3 + HS], F32)
        acc = ar.alloc("acc", [128, HS], F32)
        zs = ar.alloc("zs", [128, S], BF16)
        xc = ar.alloc("xc", [128, S], BF16)
        xdt = ar.alloc("xdt", [128, 16, 128], BF16)
        yg = ar.alloc("yg", [128, 4, S], BF16)
        ssq = ar.alloc("ssq", [128, S], F32)
        dec = [ar.alloc("dec", [128, TT], F32) for _ in range(3)]
        pT_sl = [ar.alloc("pT", [128, TT], BF16) for _ in range(3)]
        dg = [ar.alloc("dg", [128, 128], F32) for _ in range(2)]
        tmp = ar.alloc("tmpy", [128, TT], F32)
        sq = ar.alloc("sq", [128, TT], BF16)
        rs = ar.alloc("rs", [128, TT], F32)

        def conv_silu(ch, wslot_tok, w_ap, dst, dkey):
            k.op("dve", lambda e: e.memset(pc[:, 0:3], 0.0), writes=["pc"])
            for hf in range(2):
                for t2 in range(2):
                    tt = hf * 2 + t2
                    bk = psb()
                    for kc in range(8):
                        k.op("pe", lambda e, bk=bk, kc=kc, tt=tt: e.matmul(
                            ps[bk][:, :], lhsT=w_ap(kc), rhs=hT[:, kc, tt * TT:(tt + 1) * TT], start=(kc == 0), stop=(kc == 7)),
                            reads=[wslot_tok, ("hm", kc, tt)], writes=[("ps", bk)], inc=(kc == 7))
                    k.op("act", lambda e, bk=bk, t2=t2: e.activation(out=pc[:, 3 + t2 * TT:3 + (t2 + 1) * TT], in_=ps[bk][:, :], func=AF.Copy),
                         reads=[("ps", bk)], writes=["pc"])
                k.op("act", lambda e: e.activation(out=acc[:], in_=pc[:, 3:3 + HS], func=AF.Identity,
                                                   scale=cw[:, ch, 3:4], bias=cb[:, ch:ch + 1]),
                     reads=["pc", "cw", "cb"], writes=["acc"])
                for j_ in (2, 1, 0):
                    k.op("dve", lambda e, j_=j_: e.scalar_tensor_tensor(out=acc[:], in0=pc[:, j_:j_ + HS], scalar=cw[:, ch, j_:j_ + 1],
                                                                       in1=acc[:], op0=ALU.mult, op1=ALU.add),
                         reads=["pc", "cw", "acc"], writes=["acc"])
                k.op("act", lambda e, hf=hf: e.activation(out=dst[:, hf * HS:(hf + 1) * HS], in_=acc[:], func=AF.Silu),
                     reads=["acc"], writes=[dkey])
                if hf == 0:
                    k.op("dve", lambda e: e.tensor_copy(out=pc[:, 0:3], in_=pc[:, HS:HS + 3]), reads=["pc"], writes=["pc"])

        wz_d = dget("ssm_wz", [16, 128, 1024])
        wx_d = dget("ssm_wx", [16, 128, 1024])
        wbc_d = dget("ssm_wbc", [8, 128, 1024])
        wo_d = dget("ssm_wo", [4, 8, 128, 512])

        def load_pair(d0, i0, d1, i1):
            sl = cnt["wi"] % NWI
            cnt["wi"] += 1
            t = wi_sl[sl]
            k.dma("pool", lambda e: e.dma_start(out=t[:, :, 0, :], in_=d0[i0, :, :].rearrange("p (a b) -> p a b", a=8)),
                  ("wi", sl), writes=[("wi", sl)])
            k.dma("pool", lambda e: e.dma_start(out=t[:, :, 1, :], in_=d1[i1, :, :].rearrange("p (a b) -> p a b", a=8)),
                  ("wi", sl), writes=[("wi", sl)])
            return sl, t

        for g in range(4):
            sl, w = load_pair(wbc_d, g, wbc_d, 4 + g)
            conv_silu(16 + g, ("wi", sl), lambda kc, w=w: w[:, kc, 0, :], BT, "BT")
            conv_silu(20 + g, ("wi", sl), lambda kc, w=w: w[:, kc, 1, :], CT, "CT")
            for jj in range(4):
                j = g * 4 + jj
                sl, w = load_pair(wz_d, j, wx_d, j)
                for tt in range(NTT):
                    bk = psb()
                    for kc in range(8):
                        k.op("pe", lambda e, bk=bk, kc=kc, tt=tt, w=w: e.matmul(
                            ps[bk][:, :], lhsT=w[:, kc, 0, :], rhs=hT[:, kc, tt * TT:(tt + 1) * TT], start=(kc == 0), stop=(kc == 7)),
                            reads=[("wi", sl), ("hm", kc, tt)], writes=[("ps", bk)], inc=(kc == 7))
                    k.op("act", lambda e, bk=bk, tt=tt: e.activation(out=zs[:, tt * TT:(tt + 1) * TT], in_=ps[bk][:, :], func=AF.Silu),
                         reads=[("ps", bk)], writes=["zs"])
                conv_silu(j, ("wi", sl), lambda kc, w=w: w[:, kc, 1, :], xc, "xc")
                for tb in range(16):
                    bk = psb()
                    k.op("pe", lambda e, bk=bk, tb=tb: e.matmul(ps[bk][:, 0:128], lhsT=xc[:, tb * 128:(tb + 1) * 128], rhs=ident_b[:],
                                                                start=True, stop=True),
                         reads=["xc", "identb"], writes=[("ps", bk)])
                    for hh in range(2):
                        h = 2 * j + hh
                        k.op("act", lambda e, bk=bk, tb=tb, hh=hh, h=h: e.activation(
                            out=xdt[:, tb, hh * 64:(hh + 1) * 64], in_=ps[bk][:, hh * 64:(hh + 1) * 64], func=AF.Copy,
                            scale=dt_tok[:, tb, h:h + 1]),
                            reads=[("ps", bk), "dt_tok"], writes=["xdt"])
                for hh in range(2):
                    h = 2 * j + hh
                    lo, hi = hh * 64, hh * 64 + 64
                    for qt in range(NTT):
                        b_y = psr(0, 3, "sy")
                        b_r = psr(3, 5, "sr")
                        for lb in range(4):
                            di = psr(0, 2, "dg")
                            k.op("dve", lambda e, di=di, lb=lb, qt=qt, h=h: e.tensor_scalar(
                                out=dg[di][:], in0=ident_f[:], scalar1=A_tok[:, 4 * qt + lb, h:h + 1], scalar2=None, op0=ALU.mult),
                                reads=["identf", "A_tok"], writes=[("dg", di)])
                            k.op("pe", lambda e, di=di, lb=lb, b_r=b_r: e.matmul(
                                ps[b_r][:, lb * 128:(lb + 1) * 128], lhsT=ones_f[:], rhs=dg[di][:], start=True, stop=True),
                                reads=[("dg", di), "onesf"], writes=[("ps", b_r)])
                        nkb = 4 * qt + 4
                        info = {}

                        def stage_ab(kb, qt=qt, h=h, b_r=b_r, info=info):
                            off = max(0, kb - 4 * qt) * 128
                            ncol = TT - off
                            q0 = qt * TT + off
                            b_s = psr(5, 8, "ss")
                            k.op("pe", lambda e, b_s=b_s, kb=kb, q0=q0, ncol=ncol: e.matmul(
                                ps[b_s][:, 0:ncol], lhsT=BT[:, kb * 128:(kb + 1) * 128], rhs=CT[:, q0:q0 + ncol], start=True, stop=True),
                                reads=["BT", "CT"], writes=[("ps", b_s)])
                            dix = psr(0, len(dec), "dec")
                            d_ = dec[dix]
                            if kb >= 4 * qt:
                                k.op("dve", lambda e, d_=d_, b_r=b_r, off=off, ncol=ncol, kb=kb, h=h: e.tensor_scalar(
                                    out=d_[:, 0:ncol], in0=ps[b_r][:, off:TT], scalar1=A_tok[:, kb, h:h + 1], scalar2=0.0,
                                    op0=ALU.subtract, op1=ALU.min),
                                    reads=[("ps", b_r), "A_tok"], writes=[("dec", dix)])
                                k.op("act", lambda e, d_=d_, ncol=ncol: e.activation(out=d_[:, 0:ncol], in_=d_[:, 0:ncol], func=AF.Exp),
                                     reads=[("dec", dix)], writes=[("dec", dix)])
                            else:
                                k.op("act", lambda e, d_=d_, b_r=b_r, kb=kb, h=h: e.activation(
                                    out=d_[:, :], in_=ps[b_r][:, :], func=AF.Exp, bias=nA_tok[:, kb, h:h + 1], scale=1.0),
                                    reads=[("ps", b_r), "nA_tok"], writes=[("dec", dix)])
                            info[kb] = (off, ncol, b_s, dix, d_)

                        def stage_b(kb, qt=qt, info=info):
                            off, ncol, b_s, dix, d_ = info[kb]
                            pi = psr(0, len(pT_sl), "spT")
                            pT = pT_sl[pi]
                            k.op("dve", lambda e, pT=pT, d_=d_, b_s=b_s, ncol=ncol: e.tensor_tensor(
                                out=pT[:, 0:ncol], in0=ps[b_s][:, 0:ncol], in1=d_[:, 0:ncol], op=ALU.mult),
                                reads=[("ps", b_s), ("dec", dix)], writes=[("pT", pi)])
                            if kb >= 4 * qt:
                                k.op("dve", lambda e, pT=pT: e.tensor_tensor(out=pT[:, 0:128], in0=pT[:, 0:128], in1=tri[:], op=ALU.mult),
                                     reads=[("pT", pi), "tri"], writes=[("pT", pi)])
                            info[kb] = (off, ncol, pi, pT)

                        def stage_c(kb, b_y=b_y, nkb=nkb, info=info):
                            off, ncol, pi, pT = info[kb]
                            k.op("pe", lambda e, kb=kb, off=off, ncol=ncol, pT=pT, b_y=b_y: e.matmul(
                                ps[b_y][:, off:TT], lhsT=xdt[:, kb, :], rhs=pT[:, 0:ncol], start=(kb == 0), stop=(kb == nkb - 1)),
                                reads=["xdt", ("pT", pi)], writes=[("ps", b_y)], inc=(kb == nkb - 1))

                        for step in range(nkb + 2):
                            if step < nkb:
                                stage_ab(step)
                            if 0 <= step - 1 < nkb:
                                stage_b(step - 1)
                            if step - 2 >= 0:
                                stage_c(step - 2)
                        qs = slice(qt * TT, (qt + 1) * TT)
                        k.op("dve", lambda e, b_y=b_y, lo=lo, hi=hi, qs=qs, j=j: e.scalar_tensor_tensor(
                            out=tmp[lo:hi, :], in0=xc[lo:hi, qs], scalar=dsk[lo:hi, j:j + 1], in1=ps[b_y][lo:hi, :],
                            op0=ALU.mult, op1=ALU.add),
                            reads=["xc", "dsk", ("ps", b_y)], writes=["tmpy"])
                        k.op("dve", lambda e, lo=lo, hi=hi, qs=qs, jj=jj: e.tensor_tensor(
                            out=yg[lo:hi, jj, qs], in0=tmp[lo:hi, :], in1=zs[lo:hi, qs], op=ALU.mult),
                            reads=["tmpy", "zs"], writes=[("yg", jj, qt)])
                for qt in range(NTT):
                    qs = slice(qt * TT, (qt + 1) * TT)
                    k.op("act", lambda e, jj=jj, qs=qs: e.activation(out=sq[:], in_=yg[:, jj, qs], func=AF.Square),
                         reads=[("yg", jj, qt)], writes=["sq"])
                    bk = psb()
                    k.op("pe", lambda e, bk=bk: e.matmul(ps[bk][:, :], lhsT=ones_bf[:], rhs=sq[:], start=True, stop=True),
                         reads=["sq", "ones"], writes=[("ps", bk)])
                    if jj == 0:
                        k.op("act", lambda e, bk=bk, qs=qs: e.activation(out=ssq[:, qs], in_=ps[bk][:, :], func=AF.Copy),
                             reads=[("ps", bk)], writes=[("ssq", qt)])
                    else:
                        k.op("dve", lambda e, bk=bk, qs=qs: e.tensor_tensor(out=ssq[:, qs], in0=ps[bk][:, :], in1=ssq[:, qs], op=ALU.add),
                             reads=[("ps", bk), ("ssq", qt)], writes=[("ssq", qt)])
            for qt in range(NTT):
                qs = slice(qt * TT, (qt + 1) * TT)
                k.op("act", lambda e, qs=qs: e.activation(out=rs[:], in_=ssq[:, qs], func=AF.Ln, bias=epsr[:, 0:1], scale=1.0 / 512),
                     reads=[("ssq", qt), "epsrms"], writes=["rs_tmp"])
                k.op("act", lambda e: e.activation(out=rs[:], in_=rs[:], func=AF.Exp, scale=-0.5), reads=["rs_tmp"], writes=["rs_tmp"])
                for jj in range(4):
                    j = g * 4 + jj
                    k.op("dve", lambda e, jj=jj, j=j, qs=qs: e.scalar_tensor_tensor(
                        out=yg[:, jj, qs], in0=yg[:, jj, qs], scalar=ng[:, j:j + 1], in1=rs[:], op0=ALU.mult, op1=ALU.mult),
                        reads=[("yg", jj, qt), "ng", "rs_tmp"], writes=[("yg", jj, qt)])
            for dc in range(8):
                sl = cnt["wo"] % NWO
                cnt["wo"] += 1
                w = wo_sl[sl]
                k.dma("pool", lambda e, w=w, dc=dc, g=g: e.dma_start(
                    out=w[:, 0:4, :].rearrange("p a b -> p (a b)"), in_=wo_d[g, dc, :, :]), ("wo", sl), writes=[("wo", sl)])
                gi = pidx(l, 1, dc, b)
                for gtt in range(NTT):
                    bk = psb()
                    for jj in range(4):
                        k.op("pe", lambda e, bk=bk, jj=jj, gtt=gtt, w=w: e.matmul(
                            ps[bk][:, :], lhsT=w[:, jj, :], rhs=yg[:, jj, gtt * TT:(gtt + 1) * TT], start=(jj == 0), stop=(jj == 3)),
                            reads=[("wo", sl), ("yg", jj, gtt)], writes=[("ps", bk)], inc=(jj == 3))
                    k.op("dve", lambda e, bk=bk, dc=dc, gtt=gtt, gi=gi: e.scalar_tensor_tensor(
                        out=xT[:, dc, gtt * TT:(gtt + 1) * TT], in0=ps[bk][:, :], scalar=gcs[:, gi:gi + 1],
                        in1=xT[:, dc, gtt * TT:(gtt + 1) * TT], op0=ALU.mult, op1=ALU.add),
                        reads=[("ps", bk), ("x", dc, gtt), "gcs"], writes=[("x", dc, gtt)])
        k.barrier(dma_keys=["ld_small", "ld_mw"])
        ar.reset(m)
        zb = ar.alloc("zb", [128, 8, TT], BF16)
        zq = ar.alloc("zq", [128, 8, TT], BF16)
        st = ar.alloc("st", [128, 6, TT], F32)
        for gtt in range(NTT):
            layer_norm_tile(l, 1, gtt, zb, zq, st)
        k.barrier()
        ar.reset(m)

    for b in range(n_seq):
        for c in range(8):
            for gtt in range(NTT):
                k.dma("sp", lambda e, c=c, gtt=gtt, b=b: e.dma_start(
                    out=xT[:, c, gtt * TT:(gtt + 1) * TT],
                    in_=xT_d[b, c * 128:(c + 1) * 128, gtt * TT:(gtt + 1) * TT]),
                    "ld_x", writes=[("x", c, gtt)])
        for (l, j) in sublayers:
            if j != 1:
                ffn_sublayer(l, j, b)
            elif l % 3 == 0:
                mla_sublayer(l, b)
            elif l % 3 == 1:
                diff_sublayer(l, b)
            else:
                ssd_sublayer(l, b)
        for c in range(8):
            for gtt in range(NTT):
                k.dma("sp", lambda e, c=c, gtt=gtt, b=b: e.dma_start(
                    out=oT_d[b, c * 128:(c + 1) * 128, gtt * TT:(gtt + 1) * TT],
                    in_=xT[:, c, gtt * TT:(gtt + 1) * TT]),
                    "st_x", reads=[("x", c, gtt)])
    k.final_wait("sp", ["st_x"])
    k.emit()
    k.names = list(dr.keys())
    return nc, k


def prep_inputs(inp, names, n_cores=8):
    f = lambda a: np.asarray(a, dtype=np.float32)
    shared = {}
    for nm in names:
        p = nm.split("_")
        if nm in ("xT", "cT", "pos"):
            continue
        if nm == "b_modT":
            v = f(inp["b_mod"]).reshape(DEPTH, 72, 128).transpose(2, 0, 1).reshape(128, DEPTH * 72)
        elif nm == "ln_gT":
            v = f(inp["ln_g"]).reshape(DEPTH, 3, 8, 128).transpose(3, 0, 1, 2).reshape(128, -1)
        elif nm == "ln_bT":
            v = f(inp["ln_b"]).reshape(DEPTH, 3, 8, 128).transpose(3, 0, 1, 2).reshape(128, -1)
        elif nm.startswith("w_mod_"):
            v = f(inp["w_mod"][int(p[2])])
        elif nm.startswith("ffn_wi_"):
            l, jj = int(p[2]), int(p[3])
            w = f(inp["ffn_w_in"][l, jj]).reshape(8, 128, 2, NFG, 128)
            v = w.transpose(3, 1, 0, 2, 4).reshape(NFG, 128, 2048)
        elif nm.startswith("ffn_wo_"):
            l, jj = int(p[2]), int(p[3])
            w = f(inp["ffn_w_out"][l, jj]).reshape(NFG, 128, 8, 128)
            v = w.transpose(2, 1, 0, 3).reshape(8, 128, NFG * 128)
        elif nm == "invf4":
            inv = (10000.0 ** (-np.arange(0, 64, 2, dtype=np.float32) / 64.0)).astype(np.float32)
            v = np.tile(inv, 4).reshape(128, 1)
        elif nm == "tri":
            v = (np.arange(128)[None, :] >= np.arange(128)[:, None]).astype(np.float32)
        elif nm.startswith("mla_win_"):
            v = f(inp["mla_w_in"][int(p[2])]).reshape(8, 128, 704)
        elif nm.startswith("mla_qg_"):
            v = f(inp["mla_q_norm_g"][int(p[2])]).reshape(3, 128).T
        elif nm.startswith("mla_kvg_"):
            v = f(inp["mla_kv_norm_g"][int(p[2])]).reshape(2, 128).T
        elif nm.startswith("mla_wqn_"):
            v = f(inp["mla_w_q_up"][int(p[2])]).reshape(3, 128, 8, 192)[:, :, :, :128].reshape(3, 128, 1024)
        elif nm.startswith("mla_wqr_"):
            v = f(inp["mla_w_q_up"][int(p[2])]).reshape(3, 128, 8, 192)[:, :, :, 128:].reshape(3, 128, 512)
        elif nm.startswith("mla_wk_"):
            v = f(inp["mla_w_kv_up"][int(p[2])]).reshape(2, 128, 8, 256)[:, :, :, :128].reshape(2, 128, 1024)
        elif nm.startswith("mla_wv_"):
            v = f(inp["mla_w_kv_up"][int(p[2])]).reshape(2, 128, 8, 256)[:, :, :, 128:].reshape(2, 128, 1024)
        elif nm.startswith("mla_wo_"):
            w = f(inp["mla_w_out"][int(p[2])]).reshape(8, 128, 8, 128)
            v = w.transpose(2, 1, 0, 3).reshape(8, 128, 1024)
        elif nm == "ident":
            v = np.eye(128, dtype=np.float32)
        elif nm in ("ssm_wz", "ssm_wx"):
            o_ = 0 if nm == "ssm_wz" else 2048
            w = f(inp["ssm_w_in"][0])[:, o_:o_ + 2048].reshape(8, 128, 16, 128)
            v = w.transpose(2, 1, 0, 3).reshape(16, 128, 1024)
        elif nm == "ssm_wbc":
            w = f(inp["ssm_w_in"][0])[:, 4096:5120].reshape(8, 128, 8, 128)
            v = w.transpose(2, 1, 0, 3).reshape(8, 128, 1024)
        elif nm == "ssm_wdt":
            w = f(inp["ssm_w_in"][0])[:, 5120:5152].reshape(8, 128, 32)
            v = w.transpose(1, 0, 2).reshape(128, 256)
        elif nm == "ssm_cw":
            w = f(inp["ssm_conv_w"][0]).reshape(4, 24, 128)
            v = w.transpose(2, 1, 0).reshape(128, 96)
        elif nm == "ssm_cb":
            v = f(inp["ssm_conv_b"][0]).reshape(24, 128).T
        elif nm == "ssm_dsk":
            v = np.repeat(f(inp["ssm_d_skip"][0]), 64).reshape(16, 128).T
        elif nm == "ssm_ng":
            v = f(inp["ssm_norm_g"][0]).reshape(16, 128).T
        elif nm == "ssm_alog":
            v = f(inp["ssm_a_log"][0]).reshape(1, 32)
        elif nm == "ssm_dtb":
            v = f(inp["ssm_dt_bias"][0]).reshape(1, 32)
        elif nm == "ssm_wo":
            w = f(inp["ssm_w_out"][0]).reshape(4, 4, 128, 8, 128)
            v = w.transpose(0, 3, 2, 1, 4).reshape(4, 8, 128, 512)
        elif nm == "diff_wi":
            w = f(inp["diff_w_in"][0]).reshape(8, 128, 12, 256)
            v = w.transpose(2, 1, 0, 3).reshape(12, 128, 2048)
        elif nm == "diff_wo":
            w = f(inp["diff_w_out"][0]).reshape(8, 128, 8, 128)
            v = w.transpose(2, 1, 0, 3).reshape(8, 128, 1024)
        elif nm == "diff_lq":
            v = f(inp["diff_lambda_q"][0]).reshape(1, 128)
        elif nm == "diff_lk":
            v = f(inp["diff_lambda_k"][0]).reshape(1, 128)
        elif nm == "diff_g":
            v = f(inp["diff_subln_g"][0]).reshape(128, 1)
        else:
            raise KeyError(nm)
        shared[nm] = np.ascontiguousarray(v, dtype=np.float32)
    x = f(inp["x"])
    c = f(inp["c"])
    maps = []
    for i in range(n_cores):
        m = dict(shared)
        m["xT"] = np.ascontiguousarray(x[NB * i:NB * (i + 1)].transpose(0, 2, 1))
        m["cT"] = np.ascontiguousarray(c[NB * i:NB * (i + 1)].T.reshape(8, 128, NB).transpose(1, 0, 2))
        if "pos" in names:
            m["pos"] = np.ascontiguousarray(np.asarray(inp["positions"])[NB * i:NB * (i + 1)].astype(np.int32))
        maps.append(m)
    return maps


_CACHE = {}


def run(inp, n_seq=NB, sublayers=None, trace=False, same_sync=True, debug=False):
    key = (n_seq, tuple(sublayers) if sublayers is not None else None, same_sync)
    nc, k = build(n_seq, sublayers, same_sync, debug)
    maps = prep_inputs(inp, k.names)
    res = run_bass_kernel_spmd(nc, maps, core_ids=list(range(8)), trace=trace)
    outs = [r["oT"] for r in res.results]
    out = np.concatenate([o.transpose(0, 2, 1) for o in outs], axis=0)
    return out, res


def kernel(**inputs):
    out, _ = run(inputs)
    return np.ascontiguousarray(out.astype(np.float32))
```

```python
import numpy as np
import concourse.bass as bass
import concourse.mybir as mybir
from concourse.bass_utils import run_bass_kernel_spmd

F32 = mybir.dt.float32
BF16 = mybir.dt.bfloat16
I32 = mybir.dt.int32
AF = mybir.ActivationFunctionType
ALU = mybir.AluOpType

D = 1024
S = 2048
NB = 4
DEPTH = 4
FH = 2816
NFG = FH // 128
LN_EPS = 1e-5
RMS_EPS = 1e-6
ALPHA = (2.0 * DEPTH) ** 0.25
EPS_LN = LN_EPS / (ALPHA * ALPHA)
TT = 512
NTT = S // TT


class K:
    def __init__(self, nc, same_sync=True):
        self.nc = nc
        self.same_sync = same_sync
        self.eng = {"pe": nc.tensor, "act": nc.scalar, "dve": nc.vector, "pool": nc.gpsimd, "sp": nc.sync}
        self.prog = {e: [] for e in self.eng}
        self.sems = {}
        self.cnt = {}
        for e in ("pe", "act", "dve", "poolc"):
            self.sems[e] = nc.alloc_semaphore("sem_" + e)
            self.cnt[e] = 0
        self.dma_total = {}
        self.waited = {e: {} for e in self.eng}
        self.lastw = {}
        self.readers = {}
        self.nps = 0
        self.ninstr = 0

    def dma_sem(self, key):
        if key not in self.sems:
            self.sems[key] = self.nc.alloc_semaphore("d_" + str(key).replace(" ", ""))
            self.dma_total[key] = 0
        return key

    def _deps(self, eng, own, reads, writes):
        deps = {}

        def need(st):
            if st is None:
                return
            if deps.get(st[0], 0) < st[1]:
                deps[st[0]] = st[1]

        for t in reads:
            need(self.lastw.get(t))
        for t in writes:
            need(self.lastw.get(t))
            for r in self.readers.get(t, {}).items():
                need(r)
        out = []
        for sk, val in deps.items():
            if sk == own and (own == "pe" or not self.same_sync):
                continue
            if sk in self.dma_total:
                val = self.dma_total[sk]
            if self.waited[eng].get(sk, 0) >= val:
                continue
            self.waited[eng][sk] = val
            out.append((sk, val))
        return out

    def _stamp(self, st, reads, writes):
        for t in reads:
            d = self.readers.setdefault(t, {})
            if d.get(st[0], 0) < st[1]:
                d[st[0]] = st[1]
        for t in writes:
            self.lastw[t] = st
            self.readers[t] = {}

    def op(self, eng, fn, reads=(), writes=(), inc=True):
        own = "poolc" if eng == "pool" else eng
        waits = self._deps(eng, own, reads, writes)
        if inc:
            self.cnt[own] += 1
            st = (own, self.cnt[own])
            self.prog[eng].append((waits, fn, (own, 1)))
        else:
            st = (own, self.cnt[own] + 1)
            self.prog[eng].append((waits, fn, None))
        self._stamp(st, reads, writes)
        self.ninstr += 1

    def dma(self, queue, fn, semkey, reads=(), writes=()):
        self.dma_sem(semkey)
        waits = self._deps(queue, None, reads, writes)
        if isinstance(semkey, str) and semkey.startswith("ld_") and semkey != "ld_x":
            prev = self.dma_total[semkey]
            if prev and self.waited[queue].get(semkey, 0) < prev:
                self.waited[queue][semkey] = prev
                waits.append((semkey, prev))
        self.dma_total[semkey] += 16
        st = (semkey, self.dma_total[semkey])
        self.prog[queue].append((waits, fn, (semkey, 16)))
        self._stamp(st, reads, writes)
        self.ninstr += 1

    def barrier(self, engines=("pe", "act", "dve", "sp", "pool"), dma_keys=()):
        for e in engines:
            waits = []
            for sk in ("pe", "act", "dve"):
                if sk == e:
                    continue
                val = self.cnt[sk]
                if self.waited[e].get(sk, 0) < val:
                    self.waited[e][sk] = val
                    waits.append((sk, val))
            for sk in dma_keys:
                val = self.dma_total.get(sk, 0)
                if val and self.waited[e].get(sk, 0) < val:
                    self.waited[e][sk] = val
                    waits.append((sk, val))
            if waits:
                self.prog[e].append((waits, None, None))

    def final_wait(self, eng, semkeys):
        waits = [(sk, self.dma_total[sk]) for sk in semkeys if sk in self.dma_total]
        self.prog[eng].append((waits, None, None))

    def emit(self):
        nc = self.nc
        sems = self.sems

        def run(e, lst):
            for waits, fn, inc in lst:
                for sk, val in waits:
                    e.wait_ge(sems[sk], val)
                if fn is not None:
                    ins = fn(e)
                    if inc is not None:
                        ins.then_inc(sems[inc[0]], inc[1])

        with nc.Block() as block:
            @block.sync
            def _(e):
                run(e, self.prog["sp"])

            @block.gpsimd
            def _(e):
                run(e, self.prog["pool"])

            @block.tensor
            def _(e):
                run(e, self.prog["pe"])

            @block.vector
            def _(e):
                run(e, self.prog["dve"])

            @block.scalar
            def _(e):
                run(e, self.prog["act"])


class Arena:
    def __init__(self, nc):
        self.nc = nc
        base = (nc.sbuf_base + 63) // 64 * 64
        total = nc.sbuf_top - base - 64
        total = total // 64 * 64
        pad = base - nc.sbuf_base
        self.slab = nc.alloc_sbuf_tensor("arena", [128, (total + pad) // 4], F32)
        self.base = base
        self.end = base + total
        self.cur = base
        self.n = 0

    def alloc(self, name, shape, dtype, parts=128):
        nbytes = int(np.prod(shape[1:])) * (2 if dtype == BF16 else 4)
        nbytes = (nbytes + 63) // 64 * 64
        off = self.cur
        assert off + nbytes <= self.end, (name, off, nbytes, self.end)
        self.cur += nbytes
        self.n += 1
        return self.nc.alloc_sbuf_tensor_at("%s_%d" % (name, self.n), list(shape), dtype, offset=off)

    def mark(self):
        return self.cur

    def reset(self, m):
        self.cur = m


def build(n_seq=NB, sublayers=None, same_sync=True, debug=False):
    if sublayers is None:
        sublayers = [(l, j) for l in range(DEPTH) for j in range(3)]
    nc = bass.Bass("TRN2", target_bir_lowering=False)
    dr = {}

    def din(name, shape, dt=F32):
        dr[name] = nc.dram_tensor(name, list(shape), dt, kind="ExternalInput").ap()
        return dr[name]

    def dget(name, shape, dt=F32):
        if name not in dr:
            din(name, shape, dt)
        return dr[name]

    xT_d = din("xT", [NB, D, S])
    cT_d = din("cT", [128, 8, NB])
    bmod_d = din("b_modT", [128, DEPTH * 72])
    lng_d = din("ln_gT", [128, DEPTH * 3 * 8])
    lnb_d = din("ln_bT", [128, DEPTH * 3 * 8])
    oT_d = nc.dram_tensor("oT", [NB, D, S], F32, kind="ExternalOutput").ap()
    layers_used = sorted(set(l for l, _ in sublayers))

    k = K(nc, same_sync=same_sync)
    ar = Arena(nc)
    ps = [nc.alloc_psum_tensor("ps%d" % i, [128, 512], F32) for i in range(8)]

    def psb():
        b = k.nps % 8
        k.nps += 1
        return b

    xT = ar.alloc("xT", [128, 8, S], F32)
    modsb = ar.alloc("modsb", [128, DEPTH * 72 * NB], F32)
    sc1 = ar.alloc("sc1", [128, DEPTH * 3 * 8 * NB], F32)
    gcs = ar.alloc("gcs", [128, DEPTH * 3 * 8 * NB], F32)
    lng = ar.alloc("lng", [128, DEPTH * 3 * 8], F32)
    lnb = ar.alloc("lnb", [128, DEPTH * 3 * 8], F32)
    ones_bf = ar.alloc("ones", [128, 128], BF16)
    epsln = ar.alloc("epsln", [128, 1], F32)
    NWI, NWO = 2, 2
    wi_sl = [ar.alloc("wi", [128, 8, 2, 128], BF16) for _ in range(NWI)]
    wo_sl = [ar.alloc("wo", [128, NFG, 128], BF16) for _ in range(NWO)]
    cnt = {"wi": 0, "wo": 0}
    phase_mark = ar.mark()

    def midx(l, j, m, c, b):
        return (((l * 3 + j) * 3 + m) * 8 + c) * NB + b

    def pidx(l, j, c, b):
        return ((l * 3 + j) * 8 + c) * NB + b

    k.op("dve", lambda e: e.memset(ones_bf[:], 1.0), writes=["ones"])
    k.op("dve", lambda e: e.memset(epsln[:], EPS_LN), writes=["epsln"])
    k.dma("sp", lambda e: e.dma_start(out=lng[:], in_=lng_d[:, :]), "ld_small", writes=["lng"])
    k.dma("sp", lambda e: e.dma_start(out=lnb[:], in_=lnb_d[:, :]), "ld_small", writes=["lnb"])
    m0 = ar.mark()
    cond = ar.alloc("cond", [128, 8, NB], F32)
    bmod = ar.alloc("bmod", [128, DEPTH * 72], F32)
    wm = [ar.alloc("wm", [128, 8, D], F32) for _ in range(2)]
    k.dma("sp", lambda e: e.dma_start(out=cond[:], in_=cT_d[:, :, :]), "ld_small", writes=["cond"])
    k.dma("sp", lambda e: e.dma_start(out=bmod[:], in_=bmod_d[:, :]), "ld_small", writes=["bmod"])
    k.op("act", lambda e: e.activation(out=cond[:], in_=cond[:], func=AF.Silu), reads=["cond"], writes=["cond"])
    g = 0
    for l in layers_used:
        wmod_l = dget("w_mod_%d" % l, [D, 9 * D])
        for jm in range(9):
            sl = g % 2
            g += 1
            wmt = wm[sl]
            for kc in range(8):
                k.dma("sp", lambda e, wmt=wmt, kc=kc, wmod_l=wmod_l, jm=jm: e.dma_start(
                    out=wmt[:, kc, :], in_=wmod_l[kc * 128:(kc + 1) * 128, jm * D:(jm + 1) * D]),
                    ("wm", sl), writes=[("wm", sl)])
            b_ = psb()
            for c in range(8):
                for kc in range(8):
                    k.op("pe", lambda e, b_=b_, wmt=wmt, c=c, kc=kc: e.matmul(
                        ps[b_][:, c * NB:(c + 1) * NB], lhsT=wmt[:, kc, c * 128:(c + 1) * 128], rhs=cond[:, kc, :],
                        start=(kc == 0), stop=(kc == 7)),
                        reads=[("wm", sl), "cond"], writes=[("ps", b_)], inc=(c == 7 and kc == 7))
            base = (l * 9 + jm) * 8 * NB
            bb = (l * 9 + jm) * 8
            k.op("dve", lambda e, b_=b_, base=base, bb=bb: e.tensor_tensor(
                out=modsb[:, base:base + 8 * NB].rearrange("p (c b) -> p c b", b=NB),
                in0=ps[b_][:, 0:8 * NB].rearrange("p (c b) -> p c b", b=NB),
                in1=bmod[:, bb:bb + 8].unsqueeze(2).to_broadcast([128, 8, NB]), op=ALU.add),
                reads=[("ps", b_), "bmod"], writes=["modsb"])
    for l in layers_used:
        for j in range(3):
            wgt = (0.5 if j != 1 else 1.0) / ALPHA
            s0 = midx(l, j, 1, 0, 0)
            g0 = midx(l, j, 2, 0, 0)
            p0 = pidx(l, j, 0, 0)
            n = 8 * NB
            k.op("dve", lambda e, s0=s0, p0=p0, n=n: e.tensor_scalar(
                out=sc1[:, p0:p0 + n], in0=modsb[:, s0:s0 + n], scalar1=1.0, scalar2=None, op0=ALU.add),
                reads=["modsb"], writes=["sc1"])
            k.op("dve", lambda e, g0=g0, p0=p0, n=n, wgt=wgt: e.tensor_scalar(
                out=gcs[:, p0:p0 + n], in0=modsb[:, g0:g0 + n], scalar1=wgt, scalar2=None, op0=ALU.mult),
                reads=["modsb"], writes=["gcs"])
    k.barrier(dma_keys=["ld_small", ("wm", 0), ("wm", 1)])
    ar.reset(m0)
    phase_mark = ar.mark()

    def load_wi(l, jj, fg, slots):
        sl = cnt["wi"] % len(slots)
        cnt["wi"] += 1
        t = slots[sl]
        k.dma("pool", lambda e: e.dma_start(out=t[:].rearrange("p a b c -> p (a b c)"), in_=dget("ffn_wi_%d_%d" % (l, jj), [NFG, 128, 2048])[fg, :, :]),
              ("wi", sl), writes=[("wi", sl)])
        return sl

    def load_wo(l, jj, dc):
        sl = cnt["wo"] % NWO
        cnt["wo"] += 1
        t = wo_sl[sl]
        for h in range(2):
            k.dma("pool", lambda e, h=h: e.dma_start(
                out=t[:, h * 11:(h + 1) * 11, :].rearrange("p a b -> p (a b)"),
                in_=dget("ffn_wo_%d_%d" % (l, jj), [8, 128, NFG * 128])[dc, :, h * 1408:(h + 1) * 1408]),
                ("wo", sl), writes=[("wo", sl)])
        return sl

    def layer_norm_tile(l, j, gtt, zb, zq, st):
        tsl = slice(gtt * TT, (gtt + 1) * TT)
        for c in range(8):
            k.op("act", lambda e, c=c: e.activation(out=zb[:, c, :], in_=xT[:, c, tsl], func=AF.Copy),
                 reads=[("x", c, gtt)], writes=[("zb", c)])
            k.op("act", lambda e, c=c: e.activation(out=zq[:, c, :], in_=xT[:, c, tsl], func=AF.Square),
                 reads=[("x", c, gtt)], writes=[("zq", c)])
        b1 = psb()
        for c in range(8):
            k.op("pe", lambda e, c=c: e.matmul(ps[b1][:, :], lhsT=ones_bf[:], rhs=zb[:, c, :], start=(c == 0), stop=(c == 7)),
                 reads=[("zb", c), "ones"], writes=[("ps", b1)], inc=(c == 7))
        b2 = psb()
        for c in range(8):
            k.op("pe", lambda e, c=c: e.matmul(ps[b2][:, :], lhsT=ones_bf[:], rhs=zq[:, c, :], start=(c == 0), stop=(c == 7)),
                 reads=[("zq", c), "ones"], writes=[("ps", b2)], inc=(c == 7))
        mean, msq, var, rstd, nmr = (st[:, i, :] for i in range(5))
        k.op("act", lambda e: e.activation(out=mean, in_=ps[b1][:, :], func=AF.Copy, scale=1.0 / D),
             reads=[("ps", b1)], writes=["st_mean"])
        k.op("act", lambda e: e.activation(out=msq, in_=ps[b1][:, :], func=AF.Square, scale=1.0 / D),
             reads=[("ps", b1)], writes=["st_msq"])
        k.op("dve", lambda e: e.scalar_tensor_tensor(out=var, in0=ps[b2][:, :], scalar=1.0 / D, in1=msq,
                                                     op0=ALU.mult, op1=ALU.subtract),
             reads=[("ps", b2), "st_msq"], writes=["st_var"])
        k.op("act", lambda e: e.activation(out=var, in_=var, func=AF.Ln, bias=epsln[:, 0:1], scale=1.0),
             reads=["st_var", "epsln"], writes=["st_var"])
        rstd = ps[b1][:, :]
        nmr = ps[b2][:, :]
        k.op("act", lambda e: e.activation(out=rstd, in_=var, func=AF.Exp, scale=-0.5),
             reads=["st_var"], writes=[("ps", b1)])
        k.op("dve", lambda e: e.scalar_tensor_tensor(out=nmr, in0=mean, scalar=-1.0, in1=rstd,
                                                     op0=ALU.mult, op1=ALU.mult),
             reads=["st_mean", ("ps", b1)], writes=[("ps", b2)])
        tmps = [st[:, 5, :], st[:, 3, :]]
        for c in range(8):
            gi = (l * 3 + j) * 8 + c
            tmp = tmps[c % 2]
            tk = "st_tmp%d" % (c % 2)
            k.op("dve", lambda e, c=c, tmp=tmp: e.tensor_tensor(out=tmp, in0=xT[:, c, tsl], in1=rstd, op=ALU.mult),
                 reads=[("x", c, gtt), ("ps", b1)], writes=[tk])
            k.op("dve", lambda e, tmp=tmp: e.tensor_tensor(out=tmp, in0=tmp, in1=nmr, op=ALU.add),
                 reads=[tk, ("ps", b2)], writes=[tk])
            k.op("act", lambda e, c=c, gi=gi, tmp=tmp: e.activation(out=xT[:, c, tsl], in_=tmp, func=AF.Identity,
                                                            scale=lng[:, gi:gi + 1], bias=lnb[:, gi:gi + 1]),
                 reads=[tk, "lng", "lnb"], writes=[("x", c, gtt)])

    def modulate(l, j, b, dst, dst_tok, gtts):
        for i, gtt in enumerate(gtts):
            for c in range(8):
                si = pidx(l, j, c, b)
                hi = midx(l, j, 0, c, b)
                k.op("act", lambda e, i=i, c=c, gtt=gtt, si=si, hi=hi: e.activation(
                    out=dst[:, c, i * TT:(i + 1) * TT], in_=xT[:, c, gtt * TT:(gtt + 1) * TT], func=AF.Identity,
                    scale=sc1[:, si:si + 1], bias=modsb[:, hi:hi + 1]),
                    reads=[("x", c, gtt), "sc1", "modsb"], writes=[(dst_tok, c, i)])

    def ffn_sublayer(l, j, b):
        jj = 0 if j == 0 else 1
        m = ar.mark()
        hT = [ar.alloc("hT", [128, 8, 2 * TT], BF16) for _ in range(2)]
        aT = ar.alloc("aT", [128, NFG, 2 * TT], BF16)
        slots = wi_sl + [ar.alloc("wix", [128, 8, 2, 128], BF16) for _ in range(2)]
        sg = [ar.alloc("sg", [128, TT], F32) for _ in range(2)]
        zb = ar.alloc("zb", [128, 8, TT], BF16)
        zq = ar.alloc("zq", [128, 8, TT], BF16)
        st = ar.alloc("st", [128, 6, TT], F32)
        nsg = 0
        for p in range(2):
            h = hT[p]
            htok = "h%d" % p
            modulate(l, j, b, h, htok, [2 * p, 2 * p + 1])
            for fg in range(NFG):
                sl = load_wi(l, jj, fg, slots)
                w = slots[sl]
                for tt in range(2):
                    bg, bu = psb(), psb()
                    for gu, bk in ((0, bg), (1, bu)):
                        for kc in range(8):
                            k.op("pe", lambda e, bk=bk, gu=gu, kc=kc, tt=tt, w=w, h=h: e.matmul(
                                ps[bk][:, :], lhsT=w[:, kc, gu, :], rhs=h[:, kc, tt * TT:(tt + 1) * TT],
                                start=(kc == 0), stop=(kc == 7)),
                                reads=[("wi", sl), (htok, kc, tt)], writes=[("ps", bk)], inc=(kc == 7))
                    sgt = sg[nsg % 2]
                    sgk = ("sg", nsg % 2)
                    nsg += 1
                    k.op("act", lambda e, bg=bg, sgt=sgt: e.activation(out=sgt[:], in_=ps[bg][:, :], func=AF.Silu),
                         reads=[("ps", bg)], writes=[sgk])
                    k.op("dve", lambda e, bu=bu, sgt=sgt, fg=fg, tt=tt: e.tensor_tensor(
                        out=aT[:, fg, tt * TT:(tt + 1) * TT], in0=ps[bu][:, :], in1=sgt[:], op=ALU.mult),
                        reads=[("ps", bu), sgk], writes=[("a", fg, tt)])
            for dc in range(8):
                sl = load_wo(l, jj, dc)
                w = wo_sl[sl]
                gi = pidx(l, j, dc, b)
                for tt in range(2):
                    gtt = 2 * p + tt
                    bk = psb()
                    for fc in range(NFG):
                        k.op("pe", lambda e, bk=bk, fc=fc, tt=tt, w=w: e.matmul(
                            ps[bk][:, :], lhsT=w[:, fc, :], rhs=aT[:, fc, tt * TT:(tt + 1) * TT],
                            start=(fc == 0), stop=(fc == NFG - 1)),
                            reads=[("wo", sl), ("a", fc, tt)], writes=[("ps", bk)], inc=(fc == NFG - 1))
                    k.op("dve", lambda e, bk=bk, dc=dc, gtt=gtt, gi=gi: e.scalar_tensor_tensor(
                        out=xT[:, dc, gtt * TT:(gtt + 1) * TT], in0=ps[bk][:, :], scalar=gcs[:, gi:gi + 1],
                        in1=xT[:, dc, gtt * TT:(gtt + 1) * TT], op0=ALU.mult, op1=ALU.add),
                        reads=[("ps", bk), ("x", dc, gtt), "gcs"], writes=[("x", dc, gtt)])
            for tt in range(2):
                layer_norm_tile(l, j, 2 * p + tt, zb, zq, st)
        k.barrier()
        ar.reset(m)

    dumps = []

    def dump(name, tile_, shape, dtype, reads):
        if not debug or name in dumps:
            return
        dumps.append(name)
        d_ = nc.dram_tensor("dbg_" + name, list(shape), dtype, kind="ExternalOutput").ap()
        k.dma("sp", lambda e: e.dma_start(out=d_, in_=tile_), "st_x", reads=reads)

    rr = {}

    def psr(lo, hi, key):
        i = rr.get(key, 0)
        rr[key] = i + 1
        return lo + i % (hi - lo)

    def out_proj_and_ln(l, b, oT, nchunks, wname, tokp):
        wd = dget(wname, [8, 128, nchunks * 128])
        for dc in range(8):
            sl = cnt["wo"] % NWO
            cnt["wo"] += 1
            w = wo_sl[sl]
            k.dma("pool", lambda e, w=w, dc=dc: e.dma_start(
                out=w[:, 0:nchunks, :].rearrange("p a b -> p (a b)"), in_=wd[dc, :, :]),
                ("wo", sl), writes=[("wo", sl)])
            gi = pidx(l, 1, dc, b)
            for gtt in range(NTT):
                bk = psb()
                for hc in range(nchunks):
                    k.op("pe", lambda e, bk=bk, hc=hc, gtt=gtt, w=w: e.matmul(
                        ps[bk][:, :], lhsT=w[:, hc, :], rhs=oT[:, hc, gtt * TT:(gtt + 1) * TT],
                        start=(hc == 0), stop=(hc == nchunks - 1)),
                        reads=[("wo", sl), (tokp, hc, gtt)], writes=[("ps", bk)], inc=(hc == nchunks - 1))
                k.op("dve", lambda e, bk=bk, dc=dc, gtt=gtt, gi=gi: e.scalar_tensor_tensor(
                    out=xT[:, dc, gtt * TT:(gtt + 1) * TT], in0=ps[bk][:, :], scalar=gcs[:, gi:gi + 1],
                    in1=xT[:, dc, gtt * TT:(gtt + 1) * TT], op0=ALU.mult, op1=ALU.add),
                    reads=[("ps", bk), ("x", dc, gtt), "gcs"], writes=[("x", dc, gtt)])
        zb = ar.alloc("zb", [128, 8, TT], BF16)
        zq = ar.alloc("zq", [128, 8, TT], BF16)
        st = ar.alloc("st", [128, 6, TT], F32)
        for gtt in range(NTT):
            layer_norm_tile(l, 1, gtt, zb, zq, st)

    def attention_head(q_parts, k_parts, v_fn, out_cb, scale, rd, pT_sl, rden, tri):
        LA = len(pT_sl) - 1
        np_ = len(q_parts)
        for qt in range(NTT):
            b_o = psr(0, 2, "ao")
            b_d = psr(2, 4, "ad")
            nkb = 4 * qt + 4
            info = {}

            def stage_ab(kb):
                off = max(0, kb - 4 * qt) * 128
                ncol = TT - off
                q0 = qt * TT + off
                b_s = psr(4, 8, "as")
                for i in range(np_):
                    k.op("pe", lambda e, i=i, kb=kb, q0=q0, ncol=ncol, b_s=b_s: e.matmul(
                        ps[b_s][:, 0:ncol], lhsT=k_parts[i](kb), rhs=q_parts[i](q0, q0 + ncol),
                        start=(i == 0), stop=(i == np_ - 1)),
                        reads=rd, writes=[("ps", b_s)], inc=(i == np_ - 1))
                pi = psr(0, len(pT_sl), "pT")
                pT = pT_sl[pi]
                k.op("act", lambda e, b_s=b_s, ncol=ncol, pT=pT: e.activation(
                    out=pT[:, 0:ncol], in_=ps[b_s][:, 0:ncol], func=AF.Exp, scale=scale),
                    reads=[("ps", b_s)], writes=[("pT", pi)])
                if kb >= 4 * qt:
                    k.op("dve", lambda e, pT=pT: e.tensor_tensor(out=pT[:, 0:128], in0=pT[:, 0:128], in1=tri[:], op=ALU.mult),
                         reads=[("pT", pi), "tri"], writes=[("pT", pi)])
                info[kb] = (off, ncol, pi, pT)

            def stage_c(kb):
                off, ncol, pi, pT = info[kb]
                k.op("pe", lambda e, kb=kb, off=off, ncol=ncol, pT=pT, b_o=b_o: e.matmul(
                    ps[b_o][:, off:TT], lhsT=v_fn(kb), rhs=pT[:, 0:ncol], start=(kb == 0), stop=(kb == nkb - 1)),
                    reads=rd + [("pT", pi)], writes=[("ps", b_o)], inc=False)
                k.op("pe", lambda e, kb=kb, off=off, ncol=ncol, pT=pT, b_d=b_d: e.matmul(
                    ps[b_d][:, off:TT], lhsT=ones_bf[:], rhs=pT[:, 0:ncol], start=(kb == 0), stop=(kb == nkb - 1)),
                    reads=[("pT", pi), "ones"], writes=[("ps", b_d)], inc=True)

            for step in range(nkb + LA):
                if step < nkb:
                    stage_ab(step)
                if step - LA >= 0:
                    stage_c(step - LA)
            k.op("dve", lambda e, b_d=b_d: e.reciprocal(out=rden[:], in_=ps[b_d][:, :]),
                 reads=[("ps", b_d)], writes=["rden"])
            out_cb(qt, b_o)

    def rope_tables(b, cosb, sinb):
        m = ar.mark()
        posi = ar.alloc("posi", [128, S], I32)
        ang = ar.alloc("ang", [128, S], F32)
        kk = ar.alloc("kk", [128, S], F32)
        r2 = ar.alloc("r2", [128, S], F32)
        pos_d = dget("pos", [NB, S], I32)
        invf_d = dget("invf4", [128, 1])
        invf = ar.alloc("invf", [128, 1], F32)
        k.dma("sp", lambda e: e.dma_start(out=posi[:], in_=pos_d[b:b + 1, :].to_broadcast([128, S])), "ld_small", writes=["posi"])
        k.dma("sp", lambda e: e.dma_start(out=invf[:], in_=invf_d[:, :]), "ld_small", writes=["invf"])
        MAGIC = 12582912.0
        C1 = 6.28125
        C2 = 2.0 * np.pi - 6.28125
        PI = 3.14159
        k.op("dve", lambda e: e.tensor_copy(out=ang[:], in_=posi[:]), reads=["posi"], writes=["ang"])
        k.op("dve", lambda e: e.tensor_scalar(out=ang[:], in0=ang[:], scalar1=invf[:, 0:1], scalar2=None, op0=ALU.mult),
             reads=["ang", "invf"], writes=["ang"])
        k.op("dve", lambda e: e.tensor_scalar(out=kk[:], in0=ang[:], scalar1=float(1.0 / (2 * np.pi)), scalar2=None, op0=ALU.mult),
             reads=["ang"], writes=["kk"])
        k.op("dve", lambda e: e.tensor_scalar(out=kk[:], in0=kk[:], scalar1=MAGIC, scalar2=None, op0=ALU.add),
             reads=["kk"], writes=["kk"])
        k.op("dve", lambda e: e.tensor_scalar(out=kk[:], in0=kk[:], scalar1=MAGIC, scalar2=None, op0=ALU.subtract),
             reads=["kk"], writes=["kk"])
        k.op("dve", lambda e: e.scalar_tensor_tensor(out=ang[:], in0=kk[:], scalar=-C1, in1=ang[:], op0=ALU.mult, op1=ALU.add),
             reads=["kk", "ang"], writes=["ang"])
        k.op("dve", lambda e: e.scalar_tensor_tensor(out=ang[:], in0=kk[:], scalar=float(-C2), in1=ang[:], op0=ALU.mult, op1=ALU.add),
             reads=["kk", "ang"], writes=["ang"])
        k.op("dve", lambda e: e.tensor_scalar(out=r2[:], in0=ang[:], scalar1=float(np.pi / 2), scalar2=None, op0=ALU.add),
             reads=["ang"], writes=["r2"])
        k.op("dve", lambda e: e.tensor_scalar(out=kk[:], in0=r2[:], scalar1=float(np.pi), scalar2=None, op0=ALU.is_gt),
             reads=["r2"], writes=["kk"])
        k.op("dve", lambda e: e.scalar_tensor_tensor(out=r2[:], in0=kk[:], scalar=float(-2 * np.pi), in1=r2[:], op0=ALU.mult, op1=ALU.add),
             reads=["kk", "r2"], writes=["r2"])
        for t_, nm in ((ang, "ang"), (r2, "r2")):
            k.op("dve", lambda e, t_=t_: e.tensor_scalar(out=t_[:], in0=t_[:], scalar1=-PI, scalar2=PI, op0=ALU.max, op1=ALU.min),
                 reads=[nm], writes=[nm])
        k.op("act", lambda e: e.activation(out=sinb[:], in_=ang[:], func=AF.Sin), reads=["ang"], writes=["sinb"])
        k.op("act", lambda e: e.activation(out=cosb[:], in_=r2[:], func=AF.Sin), reads=["r2"], writes=["cosb"])
        k.barrier(dma_keys=["ld_small"])
        ar.reset(m)

    def rms_scale(bank, n, eps_t, dst):
        k.op("act", lambda e: e.activation(out=dst, in_=ps[bank][:, :], func=AF.Ln, bias=eps_t[:, 0:1], scale=1.0 / n),
             reads=[("ps", bank), "epsrms"], writes=["rs_tmp"])
        k.op("act", lambda e: e.activation(out=dst, in_=dst, func=AF.Exp, scale=-0.5),
             reads=["rs_tmp"], writes=["rs_tmp"])

    def load_const(name, shape, tile_, key):
        d_ = dget(name, shape)
        k.dma("sp", lambda e: e.dma_start(out=tile_[:], in_=d_), "ld_small", writes=[key])

    def mla_sublayer(l, b):
        a = l // 3
        m = ar.mark()
        scale = 192.0 ** -0.5
        epsr = ar.alloc("epsr", [128, 1], F32)
        k.op("dve", lambda e: e.memset(epsr[:], RMS_EPS), writes=["epsrms"])
        tri_f = ar.alloc("trif", [128, 128], F32)
        tri = ar.alloc("tri", [128, 128], BF16)
        load_const("tri", [128, 128], tri_f, "trif")
        k.op("dve", lambda e: e.tensor_copy(out=tri[:], in_=tri_f[:]), reads=["trif"], writes=["tri"])
        qg = ar.alloc("qg", [128, 3], F32)
        kvg = ar.alloc("kvg", [128, 2], F32)
        load_const("mla_qg_%d" % a, [128, 3], qg, "qg")
        load_const("mla_kvg_%d" % a, [128, 2], kvg, "kvg")
        cosb = ar.alloc("cosb", [128, S], BF16)
        sinb = ar.alloc("sinb", [128, S], BF16)
        rope_tables(b, cosb, sinb)
        cqn = ar.alloc("cqn", [128, 3, S], BF16)
        ckvn = ar.alloc("ckvn", [128, 2, S], BF16)
        kRz = [ar.alloc("kRz", [128, S], BF16) for _ in range(2)]
        for i_ in range(2):
            k.op("dve", lambda e, i_=i_: e.memset(kRz[i_][:], 0.0), writes=[("kR", t_) for t_ in range(NTT)])
        mB = ar.mark()
        hT = ar.alloc("hT", [128, 8, S], BF16)
        wmi = ar.alloc("wmi", [128, 8, 704], BF16)
        wkr2 = ar.alloc("wkr2", [128, 8, 128], BF16)
        wkrot2 = ar.alloc("wkrot2", [128, 8, 128], BF16)
        lat = ar.alloc("lat", [128, 5, TT], F32)
        sqb = ar.alloc("sqb", [128, 5, TT], BF16)
        rsq = ar.alloc("rsq", [128, TT], F32)
        t1 = ar.alloc("t1", [128, TT], F32)
        t2 = ar.alloc("t2", [128, TT], F32)
        win_d = dget("mla_win_%d" % a, [8, 128, 704])
        for kc in range(8):
            k.dma("pool", lambda e, kc=kc: e.dma_start(out=wmi[:, kc, :], in_=win_d[kc, :, :]), "ld_mw", writes=["wmi"])
        for hh in range(2):
            o = hh * 64
            k.op("act", lambda e, o=o: e.activation(out=wkr2[:, :, o:o + 64], in_=wmi[:, :, 640:704], func=AF.Copy),
                 reads=["wmi"], writes=["wkr2"])
            k.op("act", lambda e, o=o: e.activation(out=wkrot2[:, :, o:o + 32], in_=wmi[:, :, 672:704], func=AF.Copy, scale=-1.0),
                 reads=["wmi"], writes=["wkrot2"])
            k.op("act", lambda e, o=o: e.activation(out=wkrot2[:, :, o + 32:o + 64], in_=wmi[:, :, 640:672], func=AF.Copy),
                 reads=["wmi"], writes=["wkrot2"])
        modulate(l, 1, b, hT, "hm", list(range(NTT)))
        for tt in range(NTT):
            tsl = slice(tt * TT, (tt + 1) * TT)
            for mc in range(5):
                bk = psb()
                for kc in range(8):
                    k.op("pe", lambda e, tsl=tsl, bk=bk, mc=mc, kc=kc: e.matmul(
                        ps[bk][:, :], lhsT=wmi[:, kc, mc * 128:(mc + 1) * 128], rhs=hT[:, kc, tsl],
                        start=(kc == 0), stop=(kc == 7)),
                        reads=["wmi", ("hm", kc, tt)], writes=[("ps", bk)], inc=(kc == 7))
                k.op("act", lambda e, tsl=tsl, bk=bk, mc=mc: e.activation(out=lat[:, mc, :], in_=ps[bk][:, :], func=AF.Copy),
                     reads=[("ps", bk)], writes=[("lat", mc)])
                k.op("act", lambda e, tsl=tsl, bk=bk, mc=mc: e.activation(out=sqb[:, mc, :], in_=ps[bk][:, :], func=AF.Square),
                     reads=[("ps", bk)], writes=[("sqb", mc)])
            for (c0, c1, n, gt, gk, dst, dk) in ((0, 3, 384, qg, "qg", cqn, "cqn"), (3, 5, 256, kvg, "kvg", ckvn, "ckvn")):
                bk = psb()
                for mc in range(c0, c1):
                    k.op("pe", lambda e, tsl=tsl, bk=bk, mc=mc, c0=c0, c1=c1: e.matmul(
                        ps[bk][:, :], lhsT=ones_bf[:], rhs=sqb[:, mc, :], start=(mc == c0), stop=(mc == c1 - 1)),
                        reads=[("sqb", mc), "ones"], writes=[("ps", bk)], inc=(mc == c1 - 1))
                rms_scale(bk, n, epsr, rsq[:])
                for mc in range(c0, c1):
                    k.op("dve", lambda e, tsl=tsl, mc=mc, c0=c0, gt=gt, dst=dst: e.scalar_tensor_tensor(
                        out=dst[:, mc - c0, tsl], in0=lat[:, mc, :], scalar=gt[:, mc - c0:mc - c0 + 1], in1=rsq[:],
                        op0=ALU.mult, op1=ALU.mult),
                        reads=[("lat", mc), gk, "rs_tmp"], writes=[(dk, mc - c0, tt)])
            b1, b2 = psb(), psb()
            for (bk, wt, wk_) in ((b1, wkr2, "wkr2"), (b2, wkrot2, "wkrot2")):
                for kc in range(8):
                    k.op("pe", lambda e, tsl=tsl, bk=bk, wt=wt, kc=kc: e.matmul(
                        ps[bk][:, :], lhsT=wt[:, kc, :], rhs=hT[:, kc, tsl], start=(kc == 0), stop=(kc == 7)),
                        reads=[wk_, ("hm", kc, tt)], writes=[("ps", bk)], inc=(kc == 7))
            k.op("dve", lambda e, tsl=tsl, b1=b1: e.tensor_tensor(out=t1[:], in0=ps[b1][:, :], in1=cosb[:, tsl], op=ALU.mult),
                 reads=[("ps", b1), "cosb"], writes=["t1"])
            k.op("dve", lambda e, tsl=tsl, b2=b2: e.tensor_tensor(out=t2[:], in0=ps[b2][:, :], in1=sinb[:, tsl], op=ALU.mult),
                 reads=[("ps", b2), "sinb"], writes=["t2"])
            for i_ in range(2):
                lo_, hi_ = i_ * 64, i_ * 64 + 64
                k.op("dve", lambda e, tsl=tsl, i_=i_, lo_=lo_, hi_=hi_: e.tensor_tensor(
                    out=kRz[i_][lo_:hi_, tsl], in0=t1[lo_:hi_, :], in1=t2[lo_:hi_, :], op=ALU.add),
                    reads=["t1", "t2"], writes=[("kR", tt)])
        k.barrier(dma_keys=["ld_mw", "ld_small"])
        dump("cosb", cosb[:], [128, S], BF16, ["cosb"])
        dump("sinb", sinb[:], [128, S], BF16, ["sinb"])
        dump("cqn", cqn[:], [128, 3, S], BF16, [])
        dump("ckvn", ckvn[:], [128, 2, S], BF16, [])
        dump("hT", hT[:], [128, 8, S], BF16, [])
        k.barrier(dma_keys=["st_x"])
        ar.reset(mB)
        oT = ar.alloc("oT", [128, 8, S], BF16)
        mB = ar.mark()
        wqn = ar.alloc("wqn", [128, 3, 256], BF16)
        wqr = ar.alloc("wqr", [128, 3, 128], BF16)
        wqrot = ar.alloc("wqrot", [128, 3, 128], BF16)
        wk = ar.alloc("wk", [128, 2, 256], BF16)
        wv = ar.alloc("wv", [128, 2, 256], BF16)
        qn = ar.alloc("qn", [128, 2, S], BF16)
        qr = ar.alloc("qr", [128, S], BF16)
        kn = ar.alloc("kn", [128, 2, S], BF16)
        V = ar.alloc("V", [128, 16, 256], BF16)
        pT_sl = [ar.alloc("pT", [128, TT], BF16) for _ in range(4)]
        rden = ar.alloc("rden", [128, TT], F32)
        t1 = ar.alloc("t1", [128, TT], F32)
        t2 = ar.alloc("t2", [128, TT], F32)
        wqn_d = dget("mla_wqn_%d" % a, [3, 128, 1024])
        wqr_d = dget("mla_wqr_%d" % a, [3, 128, 512])
        wk_d = dget("mla_wk_%d" % a, [2, 128, 1024])
        wv_d = dget("mla_wv_%d" % a, [2, 128, 1024])
        for pr in range(4):
            for kc in range(3):
                k.dma("pool", lambda e, kc=kc, pr=pr: e.dma_start(out=wqn[:, kc, :], in_=wqn_d[kc, :, pr * 256:(pr + 1) * 256]),
                      "ld_mw", writes=["wqn"])
                k.dma("pool", lambda e, kc=kc, pr=pr: e.dma_start(out=wqr[:, kc, :], in_=wqr_d[kc, :, pr * 128:(pr + 1) * 128]),
                      "ld_mw", writes=["wqr"])
            for kc in range(2):
                k.dma("pool", lambda e, kc=kc, pr=pr: e.dma_start(out=wk[:, kc, :], in_=wk_d[kc, :, pr * 256:(pr + 1) * 256]),
                      "ld_mw", writes=["wk"])
                k.dma("pool", lambda e, kc=kc, pr=pr: e.dma_start(out=wv[:, kc, :], in_=wv_d[kc, :, pr * 256:(pr + 1) * 256]),
                      "ld_mw", writes=["wv"])
            for hh in range(2):
                o = hh * 64
                k.op("act", lambda e, o=o: e.activation(out=wqrot[:, :, o:o + 32], in_=wqr[:, :, o + 32:o + 64], func=AF.Copy, scale=-1.0),
                     reads=["wqr"], writes=["wqrot"])
                k.op("act", lambda e, o=o: e.activation(out=wqrot[:, :, o + 32:o + 64], in_=wqr[:, :, o:o + 32], func=AF.Copy),
                     reads=["wqr"], writes=["wqrot"])
            for tt in range(NTT):
                tsl = slice(tt * TT, (tt + 1) * TT)
                for hh in range(2):
                    bk = psb()
                    for kc in range(3):
                        k.op("pe", lambda e, bk=bk, hh=hh, kc=kc, tsl=tsl: e.matmul(
                            ps[bk][:, :], lhsT=wqn[:, kc, hh * 128:(hh + 1) * 128], rhs=cqn[:, kc, tsl],
                            start=(kc == 0), stop=(kc == 2)),
                            reads=["wqn", ("cqn", kc, tt)], writes=[("ps", bk)], inc=(kc == 2))
                    k.op("act", lambda e, bk=bk, hh=hh, tsl=tsl: e.activation(out=qn[:, hh, tsl], in_=ps[bk][:, :], func=AF.Copy),
                         reads=[("ps", bk)], writes=[("qn", hh, tt)])
                    bk = psb()
                    for kc in range(2):
                        k.op("pe", lambda e, bk=bk, hh=hh, kc=kc, tsl=tsl: e.matmul(
                            ps[bk][:, :], lhsT=wk[:, kc, hh * 128:(hh + 1) * 128], rhs=ckvn[:, kc, tsl],
                            start=(kc == 0), stop=(kc == 1)),
                            reads=["wk", ("ckvn", kc, tt)], writes=[("ps", bk)], inc=(kc == 1))
                    k.op("act", lambda e, bk=bk, hh=hh, tsl=tsl: e.activation(out=kn[:, hh, tsl], in_=ps[bk][:, :], func=AF.Copy),
                         reads=[("ps", bk)], writes=[("kn", hh, tt)])
                b1, b2 = psb(), psb()
                for (bk, wt, wk_) in ((b1, wqr, "wqr"), (b2, wqrot, "wqrot")):
                    for kc in range(3):
                        k.op("pe", lambda e, bk=bk, wt=wt, kc=kc, tsl=tsl: e.matmul(
                            ps[bk][:, :], lhsT=wt[:, kc, :], rhs=cqn[:, kc, tsl], start=(kc == 0), stop=(kc == 2)),
                            reads=[wk_, ("cqn", kc, tt)], writes=[("ps", bk)], inc=(kc == 2))
                k.op("dve", lambda e, b1=b1, tsl=tsl: e.tensor_tensor(out=t1[:], in0=ps[b1][:, :], in1=cosb[:, tsl], op=ALU.mult),
                     reads=[("ps", b1), "cosb"], writes=["t1"])
                k.op("dve", lambda e, b2=b2, tsl=tsl: e.tensor_tensor(out=t2[:], in0=ps[b2][:, :], in1=sinb[:, tsl], op=ALU.mult),
                     reads=[("ps", b2), "sinb"], writes=["t2"])
                k.op("dve", lambda e, tsl=tsl: e.tensor_tensor(out=qr[:, tsl], in0=t1[:], in1=t2[:], op=ALU.add),
                     reads=["t1", "t2"], writes=[("qr", tt)])
                for tb in range(4):
                    gtb = tt * 4 + tb
                    bk = psb()
                    for kc in range(2):
                        k.op("pe", lambda e, bk=bk, kc=kc, gtb=gtb: e.matmul(
                            ps[bk][:, 0:256], lhsT=ckvn[:, kc, gtb * 128:(gtb + 1) * 128], rhs=wv[:, kc, :],
                            start=(kc == 0), stop=(kc == 1)),
                            reads=["wv", ("ckvn", kc, tt)], writes=[("ps", bk)], inc=(kc == 1))
                    k.op("act", lambda e, bk=bk, gtb=gtb: e.activation(out=V[:, gtb, :], in_=ps[bk][:, 0:256], func=AF.Copy),
                         reads=[("ps", bk)], writes=[("V", gtb)])
            rd_all = [("qn", hh_, t_) for hh_ in range(2) for t_ in range(NTT)] + [("kn", hh_, t_) for hh_ in range(2) for t_ in range(NTT)] \
                + [("qr", t_) for t_ in range(NTT)] + [("kR", t_) for t_ in range(NTT)] + [("V", g_) for g_ in range(16)]
            for hh in range(2):
                h = pr * 2 + hh
                lo, hi = hh * 64, hh * 64 + 64

                def out_cb(qt, b_o, h=h):
                    k.op("dve", lambda e, qt=qt, b_o=b_o, h=h: e.tensor_tensor(
                        out=oT[:, h, qt * TT:(qt + 1) * TT], in0=ps[b_o][:, :], in1=rden[:], op=ALU.mult),
                        reads=[("ps", b_o), "rden"], writes=[("oT", h, qt)])

                attention_head(
                    q_parts=[lambda c0, c1, hh=hh: qn[:, hh, c0:c1], lambda c0, c1: qr[:, c0:c1]],
                    k_parts=[lambda kb, hh=hh: kn[:, hh, kb * 128:(kb + 1) * 128], lambda kb, hh=hh: kRz[hh][:, kb * 128:(kb + 1) * 128]],
                    v_fn=lambda kb, hh=hh: V[:, kb, hh * 128:(hh + 1) * 128],
                    out_cb=out_cb, scale=scale, rd=rd_all, pT_sl=pT_sl, rden=rden, tri=tri)
        k.barrier(dma_keys=["ld_mw"])
        dump("tri", tri[:], [128, 128], BF16, [])
        dump("kRz0", kRz[0][:], [128, S], BF16, [])
        dump("kRz1", kRz[1][:], [128, S], BF16, [])
        for i_ in range(4):
            dump("pT%d" % i_, pT_sl[i_][:], [128, TT], BF16, [])
        dump("rden", rden[:], [128, TT], F32, [])
        dump("qn", qn[:], [128, 2, S], BF16, [])
        dump("qr", qr[:], [128, S], BF16, [])
        dump("kn", kn[:], [128, 2, S], BF16, [])
        dump("V", V[:], [128, 16, 256], BF16, [])
        dump("oT", oT[:], [128, 8, S], BF16, [])
        k.barrier(dma_keys=["st_x"])
        ar.reset(mB)
        out_proj_and_ln(l, b, oT, 8, "mla_wo_%d" % a, "oT")
        k.barrier()
        ar.reset(m)

    def diff_sublayer(l, b):
        m = ar.mark()
        scale = 64.0 ** -0.5
        lam_init = 0.8 - 0.6 * float(np.exp(-0.3 * l))
        epsr = ar.alloc("epsr", [128, 1], F32)
        k.op("dve", lambda e: e.memset(epsr[:], RMS_EPS), writes=["epsrms"])
        tri_f = ar.alloc("trif", [128, 128], F32)
        tri = ar.alloc("tri", [128, 128], BF16)
        load_const("tri", [128, 128], tri_f, "trif")
        k.op("dve", lambda e: e.tensor_copy(out=tri[:], in_=tri_f[:]), reads=["trif"], writes=["tri"])
        lq = ar.alloc("lq", [128, 128], F32)
        lk = ar.alloc("lk", [128, 128], F32)
        lam = ar.alloc("lam", [128, 4], F32)
        gsub = ar.alloc("gsub", [128, 1], F32)
        lq_d = dget("diff_lq", [1, 128])
        lk_d = dget("diff_lk", [1, 128])
        k.dma("sp", lambda e: e.dma_start(out=lq[:], in_=lq_d[0:1, :].to_broadcast([128, 128])), "ld_small", writes=["lq"])
        k.dma("sp", lambda e: e.dma_start(out=lk[:], in_=lk_d[0:1, :].to_broadcast([128, 128])), "ld_small", writes=["lk"])
        load_const("diff_g", [128, 1], gsub, "gsub")
        k.op("dve", lambda e: e.tensor_tensor(out=lq[:], in0=lq[:], in1=lk[:], op=ALU.mult), reads=["lq", "lk"], writes=["lq"])
        k.op("dve", lambda e: e.tensor_reduce(out=lam[:, 0:2], in_=lq[:].rearrange("p (a b) -> p a b", a=2),
                                              axis=mybir.AxisListType.X, op=ALU.add), reads=["lq"], writes=["lam"])
        k.op("act", lambda e: e.activation(out=lam[:, 0:2], in_=lam[:, 0:2], func=AF.Exp), reads=["lam"], writes=["lam"])
        k.op("dve", lambda e: e.tensor_tensor(out=lam[:, 2:3], in0=lam[:, 1:2], in1=lam[:, 0:1], op=ALU.subtract),
             reads=["lam"], writes=["lam"])
        k.op("dve", lambda e: e.tensor_scalar(out=lam[:, 3:4], in0=lam[:, 2:3], scalar1=-lam_init, scalar2=None, op0=ALU.add),
             reads=["lam"], writes=["lam"])
        k.op("dve", lambda e: e.tensor_scalar(out=gsub[:], in0=gsub[:], scalar1=1.0 - lam_init, scalar2=None, op0=ALU.mult),
             reads=["gsub"], writes=["gsub"])
        hT = ar.alloc("hT", [128, 8, S], BF16)
        oT = ar.alloc("oT", [128, 8, S], BF16)
        mB = ar.mark()
        qT = ar.alloc("qT", [128, 2, S], BF16)
        kz = [ar.alloc("kz", [128, 2, S], BF16) for _ in range(2)]
        for i_ in range(2):
            k.op("dve", lambda e, i_=i_: e.memset(kz[i_][:], 0.0), writes=[("kT", hh_, t_) for hh_ in range(2) for t_ in range(NTT)])
        V = ar.alloc("V", [128, 16, 256], BF16)
        o0 = ar.alloc("o0", [128, S], F32)
        pT_sl = [ar.alloc("pT", [128, TT], BF16) for _ in range(3)]
        rden = ar.alloc("rden", [128, TT], F32)
        t1 = ar.alloc("t1", [128, TT], F32)
        sq = ar.alloc("sq", [128, TT], BF16)
        rs = ar.alloc("rs", [128, TT], F32)
        wd = dget("diff_wi", [12, 128, 2048])
        modulate(l, 1, b, hT, "hm", list(range(NTT)))

        def load_grp(g):
            sl = cnt["wi"] % NWI
            cnt["wi"] += 1
            t = wi_sl[sl]
            k.dma("pool", lambda e: e.dma_start(out=t[:].rearrange("p a b c -> p (a b c)"), in_=wd[g, :, :]),
                  ("wi", sl), writes=[("wi", sl)])
            return sl, t

        for pr in range(4):
            for (g, dst, dk) in ((pr, qT, "qT"), (4 + pr, None, "kT")):
                sl, w = load_grp(g)
                for tt in range(NTT):
                    for hh in range(2):
                        bk = psb()
                        for kc in range(8):
                            k.op("pe", lambda e, bk=bk, w=w, kc=kc, hh=hh, tt=tt: e.matmul(
                                ps[bk][:, :], lhsT=w[:, kc, hh, :], rhs=hT[:, kc, tt * TT:(tt + 1) * TT],
                                start=(kc == 0), stop=(kc == 7)),
                                reads=[("wi", sl), ("hm", kc, tt)], writes=[("ps", bk)], inc=(kc == 7))
                        if dst is not None:
                            k.op("act", lambda e, bk=bk, dst=dst, hh=hh, tt=tt: e.activation(
                                out=dst[:, hh, tt * TT:(tt + 1) * TT], in_=ps[bk][:, :], func=AF.Copy),
                                reads=[("ps", bk)], writes=[(dk, hh, tt)])
                        else:
                            for i_ in range(2):
                                lo_, hi_ = i_ * 64, i_ * 64 + 64
                                k.op("act", lambda e, bk=bk, hh=hh, tt=tt, i_=i_, lo_=lo_, hi_=hi_: e.activation(
                                    out=kz[i_][lo_:hi_, hh, tt * TT:(tt + 1) * TT], in_=ps[bk][lo_:hi_, :], func=AF.Copy),
                                    reads=[("ps", bk)], writes=[(dk, hh, tt)])
            sl, w = load_grp(8 + pr)
            for gtb in range(16):
                bk = psb()
                for kc in range(8):
                    k.op("pe", lambda e, bk=bk, w=w, kc=kc, gtb=gtb: e.matmul(
                        ps[bk][:, 0:256], lhsT=hT[:, kc, gtb * 128:(gtb + 1) * 128],
                        rhs=w[:, kc, :, :].rearrange("p a b -> p (a b)"), start=(kc == 0), stop=(kc == 7)),
                        reads=[("wi", sl), ("hm", kc, gtb // 4)], writes=[("ps", bk)], inc=(kc == 7))
                k.op("act", lambda e, bk=bk, gtb=gtb: e.activation(out=V[:, gtb, :], in_=ps[bk][:, 0:256], func=AF.Copy),
                     reads=[("ps", bk)], writes=[("V", gtb)])
            rd_all = [("qT", hh_, t_) for hh_ in range(2) for t_ in range(NTT)] + [("kT", hh_, t_) for hh_ in range(2) for t_ in range(NTT)] \
                + [("V", g_) for g_ in range(16)]
            for hh in range(2):
                h = pr * 2 + hh
                for i in range(2):
                    lo, hi = i * 64, i * 64 + 64

                    def out_cb(qt, b_o, h=h, i=i):
                        qs = slice(qt * TT, (qt + 1) * TT)
                        if i == 0:
                            k.op("dve", lambda e: e.tensor_tensor(out=o0[:, qs], in0=ps[b_o][:, :], in1=rden[:], op=ALU.mult),
                                 reads=[("ps", b_o), "rden"], writes=[("o0", qt)])
                            return
                        k.op("dve", lambda e: e.tensor_tensor(out=t1[:], in0=ps[b_o][:, :], in1=rden[:], op=ALU.mult),
                             reads=[("ps", b_o), "rden"], writes=["t1"])
                        k.op("dve", lambda e: e.scalar_tensor_tensor(out=t1[:], in0=t1[:], scalar=lam[:, 3:4], in1=o0[:, qs],
                                                                     op0=ALU.mult, op1=ALU.add),
                             reads=["t1", "lam", ("o0", qt)], writes=["t1"])
                        k.op("act", lambda e: e.activation(out=sq[:], in_=t1[:], func=AF.Square), reads=["t1"], writes=["sq"])
                        bk = psb()
                        k.op("pe", lambda e: e.matmul(ps[bk][:, :], lhsT=ones_bf[:], rhs=sq[:], start=True, stop=True),
                             reads=["sq", "ones"], writes=[("ps", bk)])
                        rms_scale(bk, 128, epsr, rs[:])
                        k.op("dve", lambda e: e.scalar_tensor_tensor(out=oT[:, h, qs], in0=t1[:], scalar=gsub[:, 0:1], in1=rs[:],
                                                                     op0=ALU.mult, op1=ALU.mult),
                             reads=["t1", "gsub", "rs_tmp"], writes=[("oT", h, qt)])

                    attention_head(
                        q_parts=[lambda c0, c1, hh=hh: qT[:, hh, c0:c1]],
                        k_parts=[lambda kb, hh=hh, i=i: kz[i][:, hh, kb * 128:(kb + 1) * 128]],
                        v_fn=lambda kb, hh=hh: V[:, kb, hh * 128:(hh + 1) * 128],
                        out_cb=out_cb, scale=scale, rd=rd_all, pT_sl=pT_sl, rden=rden, tri=tri)
        k.barrier(dma_keys=["ld_small"])
        dump("d_hT", hT[:], [128, 8, S], BF16, [])
        dump("d_qT", qT[:], [128, 2, S], BF16, [])
        dump("d_kz0", kz[0][:], [128, 2, S], BF16, [])
        dump("d_kz1", kz[1][:], [128, 2, S], BF16, [])
        dump("d_V", V[:], [128, 16, 256], BF16, [])
        dump("d_oT", oT[:], [128, 8, S], BF16, [])
        dump("d_lam", lam[:], [128, 4], F32, [])
        dump("d_gsub", gsub[:], [128, 1], F32, [])
        dump("d_tri", tri[:], [128, 128], BF16, [])
        k.barrier(dma_keys=["st_x"])
        ar.reset(mB)
        out_proj_and_ln(l, b, oT, 8, "diff_wo", "oT")
        k.barrier()
        ar.reset(m)

    def ssd_sublayer(l, b):
        m = ar.mark()
        NH = 32
        ident_f = ar.alloc("identf", [128, 128], F32)
        ident_b = ar.alloc("identb", [128, 128], BF16)
        tri_f = ar.alloc("trif", [128, 128], F32)
        tri = ar.alloc("tri", [128, 128], BF16)
        ones_f = ar.alloc("onesf", [128, 128], F32)
        onec = ar.alloc("onec", [128, 1], F32)
        epsr = ar.alloc("epsr", [128, 1], F32)
        cw = ar.alloc("cw", [128, 24, 4], F32)
        cb = ar.alloc("cb", [128, 24], F32)
        dsk = ar.alloc("dsk", [128, 16], F32)
        ng = ar.alloc("ng", [128, 16], F32)
        a_b = ar.alloc("a_b", [128, NH], F32)
        dtb = ar.alloc("dtb", [128, NH], F32)
        wdt = ar.alloc("wdt", [128, 8, NH], BF16)
        load_const("ident", [128, 128], ident_f, "identf")
        load_const("tri", [128, 128], tri_f, "trif")
        k.op("dve", lambda e: e.tensor_copy(out=ident_b[:], in_=ident_f[:]), reads=["identf"], writes=["identb"])
        k.op("dve", lambda e: e.tensor_copy(out=tri[:], in_=tri_f[:]), reads=["trif"], writes=["tri"])
        k.op("dve", lambda e: e.memset(ones_f[:], 1.0), writes=["onesf"])
        k.op("dve", lambda e: e.memset(onec[:], 1.0), writes=["onec"])
        k.op("dve", lambda e: e.memset(epsr[:], RMS_EPS), writes=["epsrms"])
        cw_d = dget("ssm_cw", [128, 96])
        k.dma("sp", lambda e: e.dma_start(out=cw[:].rearrange("p a b -> p (a b)"), in_=cw_d[:, :]), "ld_small", writes=["cw"])
        load_const("ssm_cb", [128, 24], cb, "cb")
        load_const("ssm_dsk", [128, 16], dsk, "dsk")
        load_const("ssm_ng", [128, 16], ng, "ng")
        al_d = dget("ssm_alog", [1, NH])
        db_d = dget("ssm_dtb", [1, NH])
        k.dma("sp", lambda e: e.dma_start(out=a_b[:], in_=al_d[0:1, :].to_broadcast([128, NH])), "ld_small", writes=["a_b"])
        k.dma("sp", lambda e: e.dma_start(out=dtb[:], in_=db_d[0:1, :].to_broadcast([128, NH])), "ld_small", writes=["dtb"])
        k.op("act", lambda e: e.activation(out=a_b[:], in_=a_b[:], func=AF.Exp), reads=["a_b"], writes=["a_b"])
        k.op("dve", lambda e: e.tensor_scalar(out=a_b[:], in0=a_b[:], scalar1=-1.0, scalar2=None, op0=ALU.mult),
             reads=["a_b"], writes=["a_b"])
        wdt_d = dget("ssm_wdt", [128, 8 * NH])
        k.dma("pool", lambda e: e.dma_start(out=wdt[:].rearrange("p a b -> p (a b)"), in_=wdt_d[:, :]), "ld_mw", writes=["wdt"])
        hT = ar.alloc("hT", [128, 8, S], BF16)
        modulate(l, 1, b, hT, "hm", list(range(NTT)))
        hm_all = [("hm", kc_, t_) for kc_ in range(8) for t_ in range(NTT)]
        dt_tok = ar.alloc("dt_tok", [128, 16, NH], F32)
        dA_tok = ar.alloc("dA_tok", [128, 16, NH], F32)
        A_tok = ar.alloc("A_tok", [128, 16, NH], F32)
        for tb in range(16):
            bk = psb()
            for kc in range(8):
                k.op("pe", lambda e, bk=bk, kc=kc, tb=tb: e.matmul(
                    ps[bk][:, 0:NH], lhsT=hT[:, kc, tb * 128:(tb + 1) * 128], rhs=wdt[:, kc, :], start=(kc == 0), stop=(kc == 7)),
                    reads=["wdt"] + hm_all, writes=[("ps", bk)], inc=(kc == 7))
            k.op("dve", lambda e, bk=bk, tb=tb: e.tensor_tensor(out=dt_tok[:, tb, :], in0=ps[bk][:, 0:NH], in1=dtb[:], op=ALU.add),
                 reads=[("ps", bk), "dtb"], writes=["dt_tok"])
        k.op("act", lambda e: e.activation(out=dt_tok[:], in_=dt_tok[:], func=AF.Exp), reads=["dt_tok"], writes=["dt_tok"])
        k.op("act", lambda e: e.activation(out=dt_tok[:], in_=dt_tok[:], func=AF.Ln, bias=onec[:, 0:1], scale=1.0),
             reads=["dt_tok", "onec"], writes=["dt_tok"])
        k.op("dve", lambda e: e.tensor_tensor(out=dA_tok[:], in0=dt_tok[:], in1=a_b[:].unsqueeze(1).to_broadcast([128, 16, NH]), op=ALU.mult),
             reads=["dt_tok", "a_b"], writes=["dA_tok"])
        for tb in range(16):
            bk = psb()
            for t2 in range(tb + 1):
                k.op("pe", lambda e, bk=bk, t2=t2, tb=tb: e.matmul(
                    ps[bk][:, 0:NH], lhsT=(tri_f[:] if t2 == tb else ones_f[:]), rhs=dA_tok[:, t2, :],
                    start=(t2 == 0), stop=(t2 == tb)),
                    reads=["dA_tok", "trif", "onesf"], writes=[("ps", bk)], inc=(t2 == tb))
            k.op("act", lambda e, bk=bk, tb=tb: e.activation(out=A_tok[:, tb, :], in_=ps[bk][:, 0:NH], func=AF.Copy),
                 reads=[("ps", bk)], writes=["A_tok"])
        nA_tok = ar.alloc("nA_tok", [128, 16, NH], F32)
        k.op("dve", lambda e: e.tensor_scalar(out=nA_tok[:], in0=A_tok[:], scalar1=-1.0, scalar2=None, op0=ALU.mult),
             reads=["A_tok"], writes=["nA_tok"])
        BT = ar.alloc("BT", [128, S], BF16)
        CT = ar.alloc("CT", [128, S], BF16)
        HS = S // 2
        pc = ar.alloc("pc", [128, 3 + HS], F32)
        acc = ar.alloc("acc", [128, HS], F32)
        zs = ar.alloc("zs", [128, S], BF16)
        xc = ar.alloc("xc", [128, S], BF16)
        xdt = ar.alloc("xdt", [128, 16, 128], BF16)
        yg = ar.alloc("yg", [128, 4, S], BF16)
        ssq = ar.alloc("ssq", [128, S], F32)
        dec = [ar.alloc("dec", [128, TT], F32) for _ in range(3)]
        pT_sl = [ar.alloc("pT", [128, TT], BF16) for _ in range(3)]
        dg = [ar.alloc("dg", [128, 128], F32) for _ in range(2)]
        tmp = ar.alloc("tmpy", [128, TT], F32)
        sq = ar.alloc("sq", [128, TT], BF16)
        rs = ar.alloc("rs", [128, TT], F32)

        def conv_silu(ch, wslot_tok, w_ap, dst, dkey):
            k.op("dve", lambda e: e.memset(pc[:, 0:3], 0.0), writes=["pc"])
            for hf in range(2):
                for t2 in range(2):
                    tt = hf * 2 + t2
                    bk = psb()
                    for kc in range(8):
                        k.op("pe", lambda e, bk=bk, kc=kc, tt=tt: e.matmul(
                            ps[bk][:, :], lhsT=w_ap(kc), rhs=hT[:, kc, tt * TT:(tt + 1) * TT], start=(kc == 0), stop=(kc == 7)),
                            reads=[wslot_tok, ("hm", kc, tt)], writes=[("ps", bk)], inc=(kc == 7))
                    k.op("act", lambda e, bk=bk, t2=t2: e.activation(out=pc[:, 3 + t2 * TT:3 + (t2 + 1) * TT], in_=ps[bk][:, :], func=AF.Copy),
                         reads=[("ps", bk)], writes=["pc"])
                k.op("act", lambda e: e.activation(out=acc[:], in_=pc[:, 3:3 + HS], func=AF.Identity,
                                                   scale=cw[:, ch, 3:4], bias=cb[:, ch:ch + 1]),
                     reads=["pc", "cw", "cb"], writes=["acc"])
                for j_ in (2, 1, 0):
                    k.op("dve", lambda e, j_=j_: e.scalar_tensor_tensor(out=acc[:], in0=pc[:, j_:j_ + HS], scalar=cw[:, ch, j_:j_ + 1],
                                                                       in1=acc[:], op0=ALU.mult, op1=ALU.add),
                         reads=["pc", "cw", "acc"], writes=["acc"])
                k.op("act", lambda e, hf=hf: e.activation(out=dst[:, hf * HS:(hf + 1) * HS], in_=acc[:], func=AF.Silu),
                     reads=["acc"], writes=[dkey])
                if hf == 0:
                    k.op("dve", lambda e: e.tensor_copy(out=pc[:, 0:3], in_=pc[:, HS:HS + 3]), reads=["pc"], writes=["pc"])

        wz_d = dget("ssm_wz", [16, 128, 1024])
        wx_d = dget("ssm_wx", [16, 128, 1024])
        wbc_d = dget("ssm_wbc", [8, 128, 1024])
        wo_d = dget("ssm_wo", [4, 8, 128, 512])

        def load_pair(d0, i0, d1, i1):
            sl = cnt["wi"] % NWI
            cnt["wi"] += 1
            t = wi_sl[sl]
            k.dma("pool", lambda e: e.dma_start(out=t[:, :, 0, :], in_=d0[i0, :, :].rearrange("p (a b) -> p a b", a=8)),
                  ("wi", sl), writes=[("wi", sl)])
            k.dma("pool", lambda e: e.dma_start(out=t[:, :, 1, :], in_=d1[i1, :, :].rearrange("p (a b) -> p a b", a=8)),
                  ("wi", sl), writes=[("wi", sl)])
            return sl, t

        for g in range(4):
            sl, w = load_pair(wbc_d, g, wbc_d, 4 + g)
            conv_silu(16 + g, ("wi", sl), lambda kc, w=w: w[:, kc, 0, :], BT, "BT")
            conv_silu(20 + g, ("wi", sl), lambda kc, w=w: w[:, kc, 1, :], CT, "CT")
            for jj in range(4):
                j = g * 4 + jj
                sl, w = load_pair(wz_d, j, wx_d, j)
                for tt in range(NTT):
                    bk = psb()
                    for kc in range(8):
                        k.op("pe", lambda e, bk=bk, kc=kc, tt=tt, w=w: e.matmul(
                            ps[bk][:, :], lhsT=w[:, kc, 0, :], rhs=hT[:, kc, tt * TT:(tt + 1) * TT], start=(kc == 0), stop=(kc == 7)),
                            reads=[("wi", sl), ("hm", kc, tt)], writes=[("ps", bk)], inc=(kc == 7))
                    k.op("act", lambda e, bk=bk, tt=tt: e.activation(out=zs[:, tt * TT:(tt + 1) * TT], in_=ps[bk][:, :], func=AF.Silu),
                         reads=[("ps", bk)], writes=["zs"])
                conv_silu(j, ("wi", sl), lambda kc, w=w: w[:, kc, 1, :], xc, "xc")
                for tb in range(16):
                    bk = psb()
                    k.op("pe", lambda e, bk=bk, tb=tb: e.matmul(ps[bk][:, 0:128], lhsT=xc[:, tb * 128:(tb + 1) * 128], rhs=ident_b[:],
                                                                start=True, stop=True),
                         reads=["xc", "identb"], writes=[("ps", bk)])
                    for hh in range(2):
                        h = 2 * j + hh
                        k.op("act", lambda e, bk=bk, tb=tb, hh=hh, h=h: e.activation(
                            out=xdt[:, tb, hh * 64:(hh + 1) * 64], in_=ps[bk][:, hh * 64:(hh + 1) * 64], func=AF.Copy,
                            scale=dt_tok[:, tb, h:h + 1]),
                            reads=[("ps", bk), "dt_tok"], writes=["xdt"])
                for hh in range(2):
                    h = 2 * j + hh
                    lo, hi = hh * 64, hh * 64 + 64
                    for qt in range(NTT):
                        b_y = psr(0, 3, "sy")
                        b_r = psr(3, 5, "sr")
                        for lb in range(4):
                            di = psr(0, 2, "dg")
                            k.op("dve", lambda e, di=di, lb=lb, qt=qt, h=h: e.tensor_scalar(
                                out=dg[di][:], in0=ident_f[:], scalar1=A_tok[:, 4 * qt + lb, h:h + 1], scalar2=None, op0=ALU.mult),
                                reads=["identf", "A_tok"], writes=[("dg", di)])
                            k.op("pe", lambda e, di=di, lb=lb, b_r=b_r: e.matmul(
                                ps[b_r][:, lb * 128:(lb + 1) * 128], lhsT=ones_f[:], rhs=dg[di][:], start=True, stop=True),
                                reads=[("dg", di), "onesf"], writes=[("ps", b_r)])
                        nkb = 4 * qt + 4
                        info = {}

                        def stage_ab(kb, qt=qt, h=h, b_r=b_r, info=info):
                            off = max(0, kb - 4 * qt) * 128
                            ncol = TT - off
                            q0 = qt * TT + off
                            b_s = psr(5, 8, "ss")
                            k.op("pe", lambda e, b_s=b_s, kb=kb, q0=q0, ncol=ncol: e.matmul(
                                ps[b_s][:, 0:ncol], lhsT=BT[:, kb * 128:(kb + 1) * 128], rhs=CT[:, q0:q0 + ncol], start=True, stop=True),
                                reads=["BT", "CT"], writes=[("ps", b_s)])
                            dix = psr(0, len(dec), "dec")
                            d_ = dec[dix]
                            if kb >= 4 * qt:
                                k.op("dve", lambda e, d_=d_, b_r=b_r, off=off, ncol=ncol, kb=kb, h=h: e.tensor_scalar(
                                    out=d_[:, 0:ncol], in0=ps[b_r][:, off:TT], scalar1=A_tok[:, kb, h:h + 1], scalar2=0.0,
                                    op0=ALU.subtract, op1=ALU.min),
                                    reads=[("ps", b_r), "A_tok"], writes=[("dec", dix)])
                                k.op("act", lambda e, d_=d_, ncol=ncol: e.activation(out=d_[:, 0:ncol], in_=d_[:, 0:ncol], func=AF.Exp),
                                     reads=[("dec", dix)], writes=[("dec", dix)])
                            else:
                                k.op("act", lambda e, d_=d_, b_r=b_r, kb=kb, h=h: e.activation(
                                    out=d_[:, :], in_=ps[b_r][:, :], func=AF.Exp, bias=nA_tok[:, kb, h:h + 1], scale=1.0),
                                    reads=[("ps", b_r), "nA_tok"], writes=[("dec", dix)])
                            info[kb] = (off, ncol, b_s, dix, d_)

                        def stage_b(kb, qt=qt, info=info):
                            off, ncol, b_s, dix, d_ = info[kb]
                            pi = psr(0, len(pT_sl), "spT")
                            pT = pT_sl[pi]
                            k.op("dve", lambda e, pT=pT, d_=d_, b_s=b_s, ncol=ncol: e.tensor_tensor(
                                out=pT[:, 0:ncol], in0=ps[b_s][:, 0:ncol], in1=d_[:, 0:ncol], op=ALU.mult),
                                reads=[("ps", b_s), ("dec", dix)], writes=[("pT", pi)])
                            if kb >= 4 * qt:
                                k.op("dve", lambda e, pT=pT: e.tensor_tensor(out=pT[:, 0:128], in0=pT[:, 0:128], in1=tri[:], op=ALU.mult),
                                     reads=[("pT", pi), "tri"], writes=[("pT", pi)])
                            info[kb] = (off, ncol, pi, pT)

                        def stage_c(kb, b_y=b_y, nkb=nkb, info=info):
                            off, ncol, pi, pT = info[kb]
                            k.op("pe", lambda e, kb=kb, off=off, ncol=ncol, pT=pT, b_y=b_y: e.matmul(
                                ps[b_y][:, off:TT], lhsT=xdt[:, kb, :], rhs=pT[:, 0:ncol], start=(kb == 0), stop=(kb == nkb - 1)),
                                reads=["xdt", ("pT", pi)], writes=[("ps", b_y)], inc=(kb == nkb - 1))

                        for step in range(nkb + 2):
                            if step < nkb:
                                stage_ab(step)
                            if 0 <= step - 1 < nkb:
                                stage_b(step - 1)
                            if step - 2 >= 0:
                                stage_c(step - 2)
                        qs = slice(qt * TT, (qt + 1) * TT)
                        k.op("dve", lambda e, b_y=b_y, lo=lo, hi=hi, qs=qs, j=j: e.scalar_tensor_tensor(
                            out=tmp[lo:hi, :], in0=xc[lo:hi, qs], scalar=dsk[lo:hi, j:j + 1], in1=ps[b_y][lo:hi, :],
                            op0=ALU.mult, op1=ALU.add),
                            reads=["xc", "dsk", ("ps", b_y)], writes=["tmpy"])
                        k.op("dve", lambda e, lo=lo, hi=hi, qs=qs, jj=jj: e.tensor_tensor(
                            out=yg[lo:hi, jj, qs], in0=tmp[lo:hi, :], in1=zs[lo:hi, qs], op=ALU.mult),
                            reads=["tmpy", "zs"], writes=[("yg", jj, qt)])
                for qt in range(NTT):
                    qs = slice(qt * TT, (qt + 1) * TT)
                    k.op("act", lambda e, jj=jj, qs=qs: e.activation(out=sq[:], in_=yg[:, jj, qs], func=AF.Square),
                         reads=[("yg", jj, qt)], writes=["sq"])
                    bk = psb()
                    k.op("pe", lambda e, bk=bk: e.matmul(ps[bk][:, :], lhsT=ones_bf[:], rhs=sq[:], start=True, stop=True),
                         reads=["sq", "ones"], writes=[("ps", bk)])
                    if jj == 0:
                        k.op("act", lambda e, bk=bk, qs=qs: e.activation(out=ssq[:, qs], in_=ps[bk][:, :], func=AF.Copy),
                             reads=[("ps", bk)], writes=[("ssq", qt)])
                    else:
                        k.op("dve", lambda e, bk=bk, qs=qs: e.tensor_tensor(out=ssq[:, qs], in0=ps[bk][:, :], in1=ssq[:, qs], op=ALU.add),
                             reads=[("ps", bk), ("ssq", qt)], writes=[("ssq", qt)])
            for qt in range(NTT):
                qs = slice(qt * TT, (qt + 1) * TT)
                k.op("act", lambda e, qs=qs: e.activation(out=rs[:], in_=ssq[:, qs], func=AF.Ln, bias=epsr[:, 0:1], scale=1.0 / 512),
                     reads=[("ssq", qt), "epsrms"], writes=["rs_tmp"])
                k.op("act", lambda e: e.activation(out=rs[:], in_=rs[:], func=AF.Exp, scale=-0.5), reads=["rs_tmp"], writes=["rs_tmp"])
                for jj in range(4):
                    j = g * 4 + jj
                    k.op("dve", lambda e, jj=jj, j=j, qs=qs: e.scalar_tensor_tensor(
                        out=yg[:, jj, qs], in0=yg[:, jj, qs], scalar=ng[:, j:j + 1], in1=rs[:], op0=ALU.mult, op1=ALU.mult),
                        reads=[("yg", jj, qt), "ng", "rs_tmp"], writes=[("yg", jj, qt)])
            for dc in range(8):
                sl = cnt["wo"] % NWO
                cnt["wo"] += 1
                w = wo_sl[sl]
                k.dma("pool", lambda e, w=w, dc=dc, g=g: e.dma_start(
                    out=w[:, 0:4, :].rearrange("p a b -> p (a b)"), in_=wo_d[g, dc, :, :]), ("wo", sl), writes=[("wo", sl)])
                gi = pidx(l, 1, dc, b)
                for gtt in range(NTT):
                    bk = psb()
                    for jj in range(4):
                        k.op("pe", lambda e, bk=bk, jj=jj, gtt=gtt, w=w: e.matmul(
                            ps[bk][:, :], lhsT=w[:, jj, :], rhs=yg[:, jj, gtt * TT:(gtt + 1) * TT], start=(jj == 0), stop=(jj == 3)),
                            reads=[("wo", sl), ("yg", jj, gtt)], writes=[("ps", bk)], inc=(jj == 3))
                    k.op("dve", lambda e, bk=bk, dc=dc, gtt=gtt, gi=gi: e.scalar_tensor_tensor(
                        out=xT[:, dc, gtt * TT:(gtt + 1) * TT], in0=ps[bk][:, :], scalar=gcs[:, gi:gi + 1],
                        in1=xT[:, dc, gtt * TT:(gtt + 1) * TT], op0=ALU.mult, op1=ALU.add),
                        reads=[("ps", bk), ("x", dc, gtt), "gcs"], writes=[("x", dc, gtt)])
        k.barrier(dma_keys=["ld_small", "ld_mw"])
        ar.reset(m)
        zb = ar.alloc("zb", [128, 8, TT], BF16)
        zq = ar.alloc("zq", [128, 8, TT], BF16)
        st = ar.alloc("st", [128, 6, TT], F32)
        for gtt in range(NTT):
            layer_norm_tile(l, 1, gtt, zb, zq, st)
        k.barrier()
        ar.reset(m)

    for b in range(n_seq):
        for c in range(8):
            for gtt in range(NTT):
                k.dma("sp", lambda e, c=c, gtt=gtt, b=b: e.dma_start(
                    out=xT[:, c, gtt * TT:(gtt + 1) * TT],
                    in_=xT_d[b, c * 128:(c + 1) * 128, gtt * TT:(gtt + 1) * TT]),
                    "ld_x", writes=[("x", c, gtt)])
        for (l, j) in sublayers:
            if j != 1:
                ffn_sublayer(l, j, b)
            elif l % 3 == 0:
                mla_sublayer(l, b)
            elif l % 3 == 1:
                diff_sublayer(l, b)
            else:
                ssd_sublayer(l, b)
        for c in range(8):
            for gtt in range(NTT):
                k.dma("sp", lambda e, c=c, gtt=gtt, b=b: e.dma_start(
                    out=oT_d[b, c * 128:(c + 1) * 128, gtt * TT:(gtt + 1) * TT],
                    in_=xT[:, c, gtt * TT:(gtt + 1) * TT]),
                    "st_x", reads=[("x", c, gtt)])
    k.final_wait("sp", ["st_x"])
    k.emit()
    k.names = list(dr.keys())
    return nc, k


def prep_inputs(inp, names, n_cores=8):
    f = lambda a: np.asarray(a, dtype=np.float32)
    shared = {}
    for nm in names:
        p = nm.split("_")
        if nm in ("xT", "cT", "pos"):
            continue
        if nm == "b_modT":
            v = f(inp["b_mod"]).reshape(DEPTH, 72, 128).transpose(2, 0, 1).reshape(128, DEPTH * 72)
        elif nm == "ln_gT":
            v = f(inp["ln_g"]).reshape(DEPTH, 3, 8, 128).transpose(3, 0, 1, 2).reshape(128, -1)
        elif nm == "ln_bT":
            v = f(inp["ln_b"]).reshape(DEPTH, 3, 8, 128).transpose(3, 0, 1, 2).reshape(128, -1)
        elif nm.startswith("w_mod_"):
            v = f(inp["w_mod"][int(p[2])])
        elif nm.startswith("ffn_wi_"):
            l, jj = int(p[2]), int(p[3])
            w = f(inp["ffn_w_in"][l, jj]).reshape(8, 128, 2, NFG, 128)
            v = w.transpose(3, 1, 0, 2, 4).reshape(NFG, 128, 2048)
        elif nm.startswith("ffn_wo_"):
            l, jj = int(p[2]), int(p[3])
            w = f(inp["ffn_w_out"][l, jj]).reshape(NFG, 128, 8, 128)
            v = w.transpose(2, 1, 0, 3).reshape(8, 128, NFG * 128)
        elif nm == "invf4":
            inv = (10000.0 ** (-np.arange(0, 64, 2, dtype=np.float32) / 64.0)).astype(np.float32)
            v = np.tile(inv, 4).reshape(128, 1)
        elif nm == "tri":
            v = (np.arange(128)[None, :] >= np.arange(128)[:, None]).astype(np.float32)
        elif nm.startswith("mla_win_"):
            v = f(inp["mla_w_in"][int(p[2])]).reshape(8, 128, 704)
        elif nm.startswith("mla_qg_"):
            v = f(inp["mla_q_norm_g"][int(p[2])]).reshape(3, 128).T
        elif nm.startswith("mla_kvg_"):
            v = f(inp["mla_kv_norm_g"][int(p[2])]).reshape(2, 128).T
        elif nm.startswith("mla_wqn_"):
            v = f(inp["mla_w_q_up"][int(p[2])]).reshape(3, 128, 8, 192)[:, :, :, :128].reshape(3, 128, 1024)
        elif nm.startswith("mla_wqr_"):
            v = f(inp["mla_w_q_up"][int(p[2])]).reshape(3, 128, 8, 192)[:, :, :, 128:].reshape(3, 128, 512)
        elif nm.startswith("mla_wk_"):
            v = f(inp["mla_w_kv_up"][int(p[2])]).reshape(2, 128, 8, 256)[:, :, :, :128].reshape(2, 128, 1024)
        elif nm.startswith("mla_wv_"):
            v = f(inp["mla_w_kv_up"][int(p[2])]).reshape(2, 128, 8, 256)[:, :, :, 128:].reshape(2, 128, 1024)
        elif nm.startswith("mla_wo_"):
            w = f(inp["mla_w_out"][int(p[2])]).reshape(8, 128, 8, 128)
            v = w.transpose(2, 1, 0, 3).reshape(8, 128, 1024)
        elif nm == "ident":
            v = np.eye(128, dtype=np.float32)
        elif nm in ("ssm_wz", "ssm_wx"):
            o_ = 0 if nm == "ssm_wz" else 2048
            w = f(inp["ssm_w_in"][0])[:, o_:o_ + 2048].reshape(8, 128, 16, 128)
            v = w.transpose(2, 1, 0, 3).reshape(16, 128, 1024)
        elif nm == "ssm_wbc":
            w = f(inp["ssm_w_in"][0])[:, 4096:5120].reshape(8, 128, 8, 128)
            v = w.transpose(2, 1, 0, 3).reshape(8, 128, 1024)
        elif nm == "ssm_wdt":
            w = f(inp["ssm_w_in"][0])[:, 5120:5152].reshape(8, 128, 32)
            v = w.transpose(1, 0, 2).reshape(128, 256)
        elif nm == "ssm_cw":
            w = f(inp["ssm_conv_w"][0]).reshape(4, 24, 128)
            v = w.transpose(2, 1, 0).reshape(128, 96)
        elif nm == "ssm_cb":
            v = f(inp["ssm_conv_b"][0]).reshape(24, 128).T
        elif nm == "ssm_dsk":
            v = np.repeat(f(inp["ssm_d_skip"][0]), 64).reshape(16, 128).T
        elif nm == "ssm_ng":
            v = f(inp["ssm_norm_g"][0]).reshape(16, 128).T
        elif nm == "ssm_alog":
            v = f(inp["ssm_a_log"][0]).reshape(1, 32)
        elif nm == "ssm_dtb":
            v = f(inp["ssm_dt_bias"][0]).reshape(1, 32)
        elif nm == "ssm_wo":
            w = f(inp["ssm_w_out"][0]).reshape(4, 4, 128, 8, 128)
            v = w.transpose(0, 3, 2, 1, 4).reshape(4, 8, 128, 512)
        elif nm == "diff_wi":
            w = f(inp["diff_w_in"][0]).reshape(8, 128, 12, 256)
            v = w.transpose(2, 1, 0, 3).reshape(12, 128, 2048)
        elif nm == "diff_wo":
            w = f(inp["diff_w_out"][0]).reshape(8, 128, 8, 128)
            v = w.transpose(2, 1, 0, 3).reshape(8, 128, 1024)
        elif nm == "diff_lq":
            v = f(inp["diff_lambda_q"][0]).reshape(1, 128)
        elif nm == "diff_lk":
            v = f(inp["diff_lambda_k"][0]).reshape(1, 128)
        elif nm == "diff_g":
            v = f(inp["diff_subln_g"][0]).reshape(128, 1)
        else:
            raise KeyError(nm)
        shared[nm] = np.ascontiguousarray(v, dtype=np.float32)
    x = f(inp["x"])
    c = f(inp["c"])
    maps = []
    for i in range(n_cores):
        m = dict(shared)
        m["xT"] = np.ascontiguousarray(x[NB * i:NB * (i + 1)].transpose(0, 2, 1))
        m["cT"] = np.ascontiguousarray(c[NB * i:NB * (i + 1)].T.reshape(8, 128, NB).transpose(1, 0, 2))
        if "pos" in names:
            m["pos"] = np.ascontiguousarray(np.asarray(inp["positions"])[NB * i:NB * (i + 1)].astype(np.int32))
        maps.append(m)
    return maps


_CACHE = {}


def run(inp, n_seq=NB, sublayers=None, trace=False, same_sync=True, debug=False):
    key = (n_seq, tuple(sublayers) if sublayers is not None else None, same_sync)
    nc, k = build(n_seq, sublayers, same_sync, debug)
    maps = prep_inputs(inp, k.names)
    res = run_bass_kernel_spmd(nc, maps, core_ids=list(range(8)), trace=trace)
    outs = [r["oT"] for r in res.results]
    out = np.concatenate([o.transpose(0, 2, 1) for o in outs], axis=0)
    return out, res


def kernel(**inputs):
    out, _ = run(inputs)
    return np.ascontiguousarray(out.astype(np.float32))
```

```python
import numpy as np
import concourse.bass as bass
import concourse.mybir as mybir
from concourse.bass_utils import run_bass_kernel_spmd

F32 = mybir.dt.float32
BF16 = mybir.dt.bfloat16
I32 = mybir.dt.int32
AF = mybir.ActivationFunctionType
ALU = mybir.AluOpType

D = 1024
S = 2048
NB = 4
DEPTH = 4
FH = 2816
NFG = FH // 128
LN_EPS = 1e-5
RMS_EPS = 1e-6
ALPHA = (2.0 * DEPTH) ** 0.25
EPS_LN = LN_EPS / (ALPHA * ALPHA)
TT = 512
NTT = S // TT


class K:
    def __init__(self, nc, same_sync=True):
        self.nc = nc
        self.same_sync = same_sync
        self.eng = {"pe": nc.tensor, "act": nc.scalar, "dve": nc.vector, "pool": nc.gpsimd, "sp": nc.sync}
        self.prog = {e: [] for e in self.eng}
        self.sems = {}
        self.cnt = {}
        for e in ("pe", "act", "dve", "poolc"):
            self.sems[e] = nc.alloc_semaphore("sem_" + e)
            self.cnt[e] = 0
        self.dma_total = {}
        self.waited = {e: {} for e in self.eng}
        self.lastw = {}
        self.readers = {}
        self.nps = 0
        self.ninstr = 0

    def dma_sem(self, key):
        if key not in self.sems:
            self.sems[key] = self.nc.alloc_semaphore("d_" + str(key).replace(" ", ""))
            self.dma_total[key] = 0
        return key

    def _deps(self, eng, own, reads, writes):
        deps = {}

        def need(st):
            if st is None:
                return
            if deps.get(st[0], 0) < st[1]:
                deps[st[0]] = st[1]

        for t in reads:
            need(self.lastw.get(t))
        for t in writes:
            need(self.lastw.get(t))
            for r in self.readers.get(t, {}).items():
                need(r)
        out = []
        for sk, val in deps.items():
            if sk == own and (own == "pe" or not self.same_sync):
                continue
            if sk in self.dma_total:
                val = self.dma_total[sk]
            if self.waited[eng].get(sk, 0) >= val:
                continue
            self.waited[eng][sk] = val
            out.append((sk, val))
        return out

    def _stamp(self, st, reads, writes):
        for t in reads:
            d = self.readers.setdefault(t, {})
            if d.get(st[0], 0) < st[1]:
                d[st[0]] = st[1]
        for t in writes:
            self.lastw[t] = st
            self.readers[t] = {}

    def op(self, eng, fn, reads=(), writes=(), inc=True):
        own = "poolc" if eng == "pool" else eng
        waits = self._deps(eng, own, reads, writes)
        if inc:
            self.cnt[own] += 1
            st = (own, self.cnt[own])
            self.prog[eng].append((waits, fn, (own, 1)))
        else:
            st = (own, self.cnt[own] + 1)
            self.prog[eng].append((waits, fn, None))
        self._stamp(st, reads, writes)
        self.ninstr += 1

    def dma(self, queue, fn, semkey, reads=(), writes=()):
        self.dma_sem(semkey)
        waits = self._deps(queue, None, reads, writes)
        if isinstance(semkey, str) and semkey.startswith("ld_") and semkey != "ld_x":
            prev = self.dma_total[semkey]
            if prev and self.waited[queue].get(semkey, 0) < prev:
                self.waited[queue][semkey] = prev
                waits.append((semkey, prev))
        self.dma_total[semkey] += 16
        st = (semkey, self.dma_total[semkey])
        self.prog[queue].append((waits, fn, (semkey, 16)))
        self._stamp(st, reads, writes)
        self.ninstr += 1

    def barrier(self, engines=("pe", "act", "dve", "sp", "pool"), dma_keys=()):
        for e in engines:
            waits = []
            for sk in ("pe", "act", "dve"):
                if sk == e:
                    continue
                val = self.cnt[sk]
                if self.waited[e].get(sk, 0) < val:
                    self.waited[e][sk] = val
                    waits.append((sk, val))
            for sk in dma_keys:
                val = self.dma_total.get(sk, 0)
                if val and self.waited[e].get(sk, 0) < val:
                    self.waited[e][sk] = val
                    waits.append((sk, val))
            if waits:
                self.prog[e].append((waits, None, None))

    def final_wait(self, eng, semkeys):
        waits = [(sk, self.dma_total[sk]) for sk in semkeys if sk in self.dma_total]
        self.prog[eng].append((waits, None, None))

    def emit(self):
        nc = self.nc
        sems = self.sems

        def run(e, lst):
            for waits, fn, inc in lst:
                for sk, val in waits:
                    e.wait_ge(sems[sk], val)
                if fn is not None:
                    ins = fn(e)
                    if inc is not None:
                        ins.then_inc(sems[inc[0]], inc[1])

        with nc.Block() as block:
            @block.sync
            def _(e):
                run(e, self.prog["sp"])

            @block.gpsimd
            def _(e):
                run(e, self.prog["pool"])

            @block.tensor
            def _(e):
                run(e, self.prog["pe"])

            @block.vector
            def _(e):
                run(e, self.prog["dve"])

            @block.scalar
            def _(e):
                run(e, self.prog["act"])


class Arena:
    def __init__(self, nc):
        self.nc = nc
        base = (nc.sbuf_base + 63) // 64 * 64
        total = nc.sbuf_top - base - 64
        total = total // 64 * 64
        pad = base - nc.sbuf_base
        self.slab = nc.alloc_sbuf_tensor("arena", [128, (total + pad) // 4], F32)
        self.base = base
        self.end = base + total
        self.cur = base
        self.n = 0

    def alloc(self, name, shape, dtype, parts=128):
        nbytes = int(np.prod(shape[1:])) * (2 if dtype == BF16 else 4)
        nbytes = (nbytes + 63) // 64 * 64
        off = self.cur
        assert off + nbytes <= self.end, (name, off, nbytes, self.end)
        self.cur += nbytes
        self.n += 1
        return self.nc.alloc_sbuf_tensor_at("%s_%d" % (name, self.n), list(shape), dtype, offset=off)

    def mark(self):
        return self.cur

    def reset(self, m):
        self.cur = m


def build(n_seq=NB, sublayers=None, same_sync=True, debug=False):
    if sublayers is None:
        sublayers = [(l, j) for l in range(DEPTH) for j in range(3)]
    nc = bass.Bass("TRN2", target_bir_lowering=False)
    dr = {}

    def din(name, shape, dt=F32):
        dr[name] = nc.dram_tensor(name, list(shape), dt, kind="ExternalInput").ap()
        return dr[name]

    def dget(name, shape, dt=F32):
        if name not in dr:
            din(name, shape, dt)
        return dr[name]

    xT_d = din("xT", [NB, D, S])
    cT_d = din("cT", [128, 8, NB])
    bmod_d = din("b_modT", [128, DEPTH * 72])
    lng_d = din("ln_gT", [128, DEPTH * 3 * 8])
    lnb_d = din("ln_bT", [128, DEPTH * 3 * 8])
    oT_d = nc.dram_tensor("oT", [NB, D, S], F32, kind="ExternalOutput").ap()
    layers_used = sorted(set(l for l, _ in sublayers))

    k = K(nc, same_sync=same_sync)
    ar = Arena(nc)
    ps = [nc.alloc_psum_tensor("ps%d" % i, [128, 512], F32) for i in range(8)]

    def psb():
        b = k.nps % 8
        k.nps += 1
        return b

    xT = ar.alloc("xT", [128, 8, S], F32)
    modsb = ar.alloc("modsb", [128, DEPTH * 72 * NB], F32)
    sc1 = ar.alloc("sc1", [128, DEPTH * 3 * 8 * NB], F32)
    gcs = ar.alloc("gcs", [128, DEPTH * 3 * 8 * NB], F32)
    lng = ar.alloc("lng", [128, DEPTH * 3 * 8], F32)
    lnb = ar.alloc("lnb", [128, DEPTH * 3 * 8], F32)
    ones_bf = ar.alloc("ones", [128, 128], BF16)
    epsln = ar.alloc("epsln", [128, 1], F32)
    NWI, NWO = 2, 2
    wi_sl = [ar.alloc("wi", [128, 8, 2, 128], BF16) for _ in range(NWI)]
    wo_sl = [ar.alloc("wo", [128, NFG, 128], BF16) for _ in range(NWO)]
    cnt = {"wi": 0, "wo": 0}
    phase_mark = ar.mark()

    def midx(l, j, m, c, b):
        return (((l * 3 + j) * 3 + m) * 8 + c) * NB + b

    def pidx(l, j, c, b):
        return ((l * 3 + j) * 8 + c) * NB + b

    k.op("dve", lambda e: e.memset(ones_bf[:], 1.0), writes=["ones"])
    k.op("dve", lambda e: e.memset(epsln[:], EPS_LN), writes=["epsln"])
    k.dma("sp", lambda e: e.dma_start(out=lng[:], in_=lng_d[:, :]), "ld_small", writes=["lng"])
    k.dma("sp", lambda e: e.dma_start(out=lnb[:], in_=lnb_d[:, :]), "ld_small", writes=["lnb"])
    m0 = ar.mark()
    cond = ar.alloc("cond", [128, 8, NB], F32)
    bmod = ar.alloc("bmod", [128, DEPTH * 72], F32)
    wm = [ar.alloc("wm", [128, 8, D], F32) for _ in range(2)]
    k.dma("sp", lambda e: e.dma_start(out=cond[:], in_=cT_d[:, :, :]), "ld_small", writes=["cond"])
    k.dma("sp", lambda e: e.dma_start(out=bmod[:], in_=bmod_d[:, :]), "ld_small", writes=["bmod"])
    k.op("act", lambda e: e.activation(out=cond[:], in_=cond[:], func=AF.Silu), reads=["cond"], writes=["cond"])
    g = 0
    for l in layers_used:
        wmod_l = dget("w_mod_%d" % l, [D, 9 * D])
        for jm in range(9):
            sl = g % 2
            g += 1
            wmt = wm[sl]
            for kc in range(8):
                k.dma("sp", lambda e, wmt=wmt, kc=kc, wmod_l=wmod_l, jm=jm: e.dma_start(
                    out=wmt[:, kc, :], in_=wmod_l[kc * 128:(kc + 1) * 128, jm * D:(jm + 1) * D]),
                    ("wm", sl), writes=[("wm", sl)])
            b_ = psb()
            for c in range(8):
                for kc in range(8):
                    k.op("pe", lambda e, b_=b_, wmt=wmt, c=c, kc=kc: e.matmul(
                        ps[b_][:, c * NB:(c + 1) * NB], lhsT=wmt[:, kc, c * 128:(c + 1) * 128], rhs=cond[:, kc, :],
                        start=(kc == 0), stop=(kc == 7)),
                        reads=[("wm", sl), "cond"], writes=[("ps", b_)], inc=(c == 7 and kc == 7))
            base = (l * 9 + jm) * 8 * NB
            bb = (l * 9 + jm) * 8
            k.op("dve", lambda e, b_=b_, base=base, bb=bb: e.tensor_tensor(
                out=modsb[:, base:base + 8 * NB].rearrange("p (c b) -> p c b", b=NB),
                in0=ps[b_][:, 0:8 * NB].rearrange("p (c b) -> p c b", b=NB),
                in1=bmod[:, bb:bb + 8].unsqueeze(2).to_broadcast([128, 8, NB]), op=ALU.add),
                reads=[("ps", b_), "bmod"], writes=["modsb"])
    for l in layers_used:
        for j in range(3):
            wgt = (0.5 if j != 1 else 1.0) / ALPHA
            s0 = midx(l, j, 1, 0, 0)
            g0 = midx(l, j, 2, 0, 0)
            p0 = pidx(l, j, 0, 0)
            n = 8 * NB
            k.op("dve", lambda e, s0=s0, p0=p0, n=n: e.tensor_scalar(
                out=sc1[:, p0:p0 + n], in0=modsb[:, s0:s0 + n], scalar1=1.0, scalar2=None, op0=ALU.add),
                reads=["modsb"], writes=["sc1"])
            k.op("dve", lambda e, g0=g0, p0=p0, n=n, wgt=wgt: e.tensor_scalar(
                out=gcs[:, p0:p0 + n], in0=modsb[:, g0:g0 + n], scalar1=wgt, scalar2=None, op0=ALU.mult),
                reads=["modsb"], writes=["gcs"])
    k.barrier(dma_keys=["ld_small", ("wm", 0), ("wm", 1)])
    ar.reset(m0)
    phase_mark = ar.mark()

    def load_wi(l, jj, fg, slots):
        sl = cnt["wi"] % len(slots)
        cnt["wi"] += 1
        t = slots[sl]
        k.dma("pool", lambda e: e.dma_start(out=t[:].rearrange("p a b c -> p (a b c)"), in_=dget("ffn_wi_%d_%d" % (l, jj), [NFG, 128, 2048])[fg, :, :]),
              ("wi", sl), writes=[("wi", sl)])
        return sl

    def load_wo(l, jj, dc):
        sl = cnt["wo"] % NWO
        cnt["wo"] += 1
        t = wo_sl[sl]
        for h in range(2):
            k.dma("pool", lambda e, h=h: e.dma_start(
                out=t[:, h * 11:(h + 1) * 11, :].rearrange("p a b -> p (a b)"),
                in_=dget("ffn_wo_%d_%d" % (l, jj), [8, 128, NFG * 128])[dc, :, h * 1408:(h + 1) * 1408]),
                ("wo", sl), writes=[("wo", sl)])
        return sl

    def layer_norm_tile(l, j, gtt, zb, zq, st):
        tsl = slice(gtt * TT, (gtt + 1) * TT)
        for c in range(8):
            k.op("dve", lambda e, c=c: e.tensor_copy(out=zb[:, c, :], in_=xT[:, c, tsl]),
                 reads=[("x", c, gtt)], writes=[("zb", c)])
            k.op("act", lambda e, c=c: e.activation(out=zq[:, c, :], in_=xT[:, c, tsl], func=AF.Square),
                 reads=[("x", c, gtt)], writes=[("zq", c)])
        b1 = psb()
        for c in range(8):
            k.op("pe", lambda e, c=c: e.matmul(ps[b1][:, :], lhsT=ones_bf[:], rhs=zb[:, c, :], start=(c == 0), stop=(c == 7)),
                 reads=[("zb", c), "ones"], writes=[("ps", b1)], inc=(c == 7))
        b2 = psb()
        for c in range(8):
            k.op("pe", lambda e, c=c: e.matmul(ps[b2][:, :], lhsT=ones_bf[:], rhs=zq[:, c, :], start=(c == 0), stop=(c == 7)),
                 reads=[("zq", c), "ones"], writes=[("ps", b2)], inc=(c == 7))
        mean, msq, var, rstd, nmr = (st[:, i, :] for i in range(5))
        k.op("act", lambda e: e.activation(out=mean, in_=ps[b1][:, :], func=AF.Copy, scale=1.0 / D),
             reads=[("ps", b1)], writes=["st_mean"])
        k.op("act", lambda e: e.activation(out=msq, in_=ps[b1][:, :], func=AF.Square, scale=1.0 / D),
             reads=[("ps", b1)], writes=["st_msq"])
        k.op("dve", lambda e: e.scalar_tensor_tensor(out=var, in0=ps[b2][:, :], scalar=1.0 / D, in1=msq,
                                                     op0=ALU.mult, op1=ALU.subtract),
             reads=[("ps", b2), "st_msq"], writes=["st_var"])
        k.op("act", lambda e: e.activation(out=var, in_=var, func=AF.Ln, bias=epsln[:, 0:1], scale=1.0),
             reads=["st_var", "epsln"], writes=["st_var"])
        rstd = ps[b1][:, :]
        nmr = ps[b2][:, :]
        k.op("act", lambda e: e.activation(out=rstd, in_=var, func=AF.Exp, scale=-0.5),
             reads=["st_var"], writes=[("ps", b1)])
        k.op("dve", lambda e: e.scalar_tensor_tensor(out=nmr, in0=mean, scalar=-1.0, in1=rstd,
                                                     op0=ALU.mult, op1=ALU.mult),
             reads=["st_mean", ("ps", b1)], writes=[("ps", b2)])
        tmps = [st[:, 5, :], st[:, 3, :]]
        for c in range(8):
            gi = (l * 3 + j) * 8 + c
            tmp = tmps[c % 2]
            tk = "st_tmp%d" % (c % 2)
            k.op("dve", lambda e, c=c, tmp=tmp: e.tensor_tensor(out=tmp, in0=xT[:, c, tsl], in1=rstd, op=ALU.mult),
                 reads=[("x", c, gtt), ("ps", b1)], writes=[tk])
            k.op("dve", lambda e, tmp=tmp: e.tensor_tensor(out=tmp, in0=tmp, in1=nmr, op=ALU.add),
                 reads=[tk, ("ps", b2)], writes=[tk])
            k.op("act", lambda e, c=c, gi=gi, tmp=tmp: e.activation(out=xT[:, c, tsl], in_=tmp, func=AF.Identity,
                                                            scale=lng[:, gi:gi + 1], bias=lnb[:, gi:gi + 1]),
                 reads=[tk, "lng", "lnb"], writes=[("x", c, gtt)])

    def modulate(l, j, b, dst, dst_tok, gtts):
        for i, gtt in enumerate(gtts):
            for c in range(8):
                si = pidx(l, j, c, b)
                hi = midx(l, j, 0, c, b)
                if c % 2 == 0:
                    k.op("act", lambda e, i=i, c=c, gtt=gtt, si=si, hi=hi: e.activation(
                        out=dst[:, c, i * TT:(i + 1) * TT], in_=xT[:, c, gtt * TT:(gtt + 1) * TT], func=AF.Identity,
                        scale=sc1[:, si:si + 1], bias=modsb[:, hi:hi + 1]),
                        reads=[("x", c, gtt), "sc1", "modsb"], writes=[(dst_tok, c, i)])
                else:
                    k.op("dve", lambda e, i=i, c=c, gtt=gtt, si=si, hi=hi: e.tensor_scalar(
                        out=dst[:, c, i * TT:(i + 1) * TT], in0=xT[:, c, gtt * TT:(gtt + 1) * TT],
                        scalar1=sc1[:, si:si + 1], scalar2=modsb[:, hi:hi + 1], op0=ALU.mult, op1=ALU.add),
                        reads=[("x", c, gtt), "sc1", "modsb"], writes=[(dst_tok, c, i)])

    def ffn_sublayer(l, j, b):
        jj = 0 if j == 0 else 1
        m = ar.mark()
        hT = [ar.alloc("hT", [128, 8, 2 * TT], BF16) for _ in range(2)]
        aT = ar.alloc("aT", [128, NFG, 2 * TT], BF16)
        slots = wi_sl + [ar.alloc("wix", [128, 8, 2, 128], BF16) for _ in range(2)]
        sg = [ar.alloc("sg", [128, TT], F32) for _ in range(2)]
        zb = ar.alloc("zb", [128, 8, TT], BF16)
        zq = ar.alloc("zq", [128, 8, TT], BF16)
        st = ar.alloc("st", [128, 6, TT], F32)
        nsg = 0
        for p in range(2):
            h = hT[p]
            htok = "h%d" % p
            modulate(l, j, b, h, htok, [2 * p, 2 * p + 1])
            for fg in range(NFG):
                sl = load_wi(l, jj, fg, slots)
                w = slots[sl]
                for tt in range(2):
                    bg, bu = psb(), psb()
                    for gu, bk in ((0, bg), (1, bu)):
                        for kc in range(8):
                            k.op("pe", lambda e, bk=bk, gu=gu, kc=kc, tt=tt, w=w, h=h: e.matmul(
                                ps[bk][:, :], lhsT=w[:, kc, gu, :], rhs=h[:, kc, tt * TT:(tt + 1) * TT],
                                start=(kc == 0), stop=(kc == 7)),
                                reads=[("wi", sl), (htok, kc, tt)], writes=[("ps", bk)], inc=(kc == 7))
                    sgt = sg[nsg % 2]
                    sgk = ("sg", nsg % 2)
                    nsg += 1
                    k.op("act", lambda e, bg=bg, sgt=sgt: e.activation(out=sgt[:], in_=ps[bg][:, :], func=AF.Silu),
                         reads=[("ps", bg)], writes=[sgk])
                    k.op("dve", lambda e, bu=bu, sgt=sgt, fg=fg, tt=tt: e.tensor_tensor(
                        out=aT[:, fg, tt * TT:(tt + 1) * TT], in0=ps[bu][:, :], in1=sgt[:], op=ALU.mult),
                        reads=[("ps", bu), sgk], writes=[("a", fg, tt)])
            for dc in range(8):
                sl = load_wo(l, jj, dc)
                w = wo_sl[sl]
                gi = pidx(l, j, dc, b)
                for tt in range(2):
                    gtt = 2 * p + tt
                    bk = psb()
                    for fc in range(NFG):
                        k.op("pe", lambda e, bk=bk, fc=fc, tt=tt, w=w: e.matmul(
                            ps[bk][:, :], lhsT=w[:, fc, :], rhs=aT[:, fc, tt * TT:(tt + 1) * TT],
                            start=(fc == 0), stop=(fc == NFG - 1)),
                            reads=[("wo", sl), ("a", fc, tt)], writes=[("ps", bk)], inc=(fc == NFG - 1))
                    k.op("dve", lambda e, bk=bk, dc=dc, gtt=gtt, gi=gi: e.scalar_tensor_tensor(
                        out=xT[:, dc, gtt * TT:(gtt + 1) * TT], in0=ps[bk][:, :], scalar=gcs[:, gi:gi + 1],
                        in1=xT[:, dc, gtt * TT:(gtt + 1) * TT], op0=ALU.mult, op1=ALU.add),
                        reads=[("ps", bk), ("x", dc, gtt), "gcs"], writes=[("x", dc, gtt)])
            for tt in range(2):
                layer_norm_tile(l, j, 2 * p + tt, zb, zq, st)
        k.barrier()
        ar.reset(m)

    dumps = []

    def dump(name, tile_, shape, dtype, reads):
        if not debug or name in dumps:
            return
        dumps.append(name)
        d_ = nc.dram_tensor("dbg_" + name, list(shape), dtype, kind="ExternalOutput").ap()
        k.dma("sp", lambda e: e.dma_start(out=d_, in_=tile_), "st_x", reads=reads)

    rr = {}

    def psr(lo, hi, key):
        i = rr.get(key, 0)
        rr[key] = i + 1
        return lo + i % (hi - lo)

    def out_proj_and_ln(l, b, oT, nchunks, wname, tokp):
        wd = dget(wname, [8, 128, nchunks * 128])
        for dc in range(8):
            sl = cnt["wo"] % NWO
            cnt["wo"] += 1
            w = wo_sl[sl]
            k.dma("pool", lambda e, w=w, dc=dc: e.dma_start(
                out=w[:, 0:nchunks, :].rearrange("p a b -> p (a b)"), in_=wd[dc, :, :]),
                ("wo", sl), writes=[("wo", sl)])
            gi = pidx(l, 1, dc, b)
            for gtt in range(NTT):
                bk = psb()
                for hc in range(nchunks):
                    k.op("pe", lambda e, bk=bk, hc=hc, gtt=gtt, w=w: e.matmul(
                        ps[bk][:, :], lhsT=w[:, hc, :], rhs=oT[:, hc, gtt * TT:(gtt + 1) * TT],
                        start=(hc == 0), stop=(hc == nchunks - 1)),
                        reads=[("wo", sl), (tokp, hc, gtt)], writes=[("ps", bk)], inc=(hc == nchunks - 1))
                k.op("dve", lambda e, bk=bk, dc=dc, gtt=gtt, gi=gi: e.scalar_tensor_tensor(
                    out=xT[:, dc, gtt * TT:(gtt + 1) * TT], in0=ps[bk][:, :], scalar=gcs[:, gi:gi + 1],
                    in1=xT[:, dc, gtt * TT:(gtt + 1) * TT], op0=ALU.mult, op1=ALU.add),
                    reads=[("ps", bk), ("x", dc, gtt), "gcs"], writes=[("x", dc, gtt)])
        zb = ar.alloc("zb", [128, 8, TT], BF16)
        zq = ar.alloc("zq", [128, 8, TT], BF16)
        st = ar.alloc("st", [128, 6, TT], F32)
        for gtt in range(NTT):
            layer_norm_tile(l, 1, gtt, zb, zq, st)

    def attention_head(q_parts, k_parts, v_fn, out_cb, scale, rd, pT_sl, rden, tri):
        LA = len(pT_sl) - 1
        np_ = len(q_parts)
        for qt in range(NTT):
            b_o = psr(0, 2, "ao")
            b_d = psr(2, 4, "ad")
            nkb = 4 * qt + 4
            info = {}

            def stage_ab(kb):
                off = max(0, kb - 4 * qt) * 128
                ncol = TT - off
                q0 = qt * TT + off
                b_s = psr(4, 8, "as")
                for i in range(np_):
                    k.op("pe", lambda e, i=i, kb=kb, q0=q0, ncol=ncol, b_s=b_s: e.matmul(
                        ps[b_s][:, 0:ncol], lhsT=k_parts[i](kb), rhs=q_parts[i](q0, q0 + ncol),
                        start=(i == 0), stop=(i == np_ - 1)),
                        reads=rd, writes=[("ps", b_s)], inc=(i == np_ - 1))
                pi = psr(0, len(pT_sl), "pT")
                pT = pT_sl[pi]
                k.op("act", lambda e, b_s=b_s, ncol=ncol, pT=pT: e.activation(
                    out=pT[:, 0:ncol], in_=ps[b_s][:, 0:ncol], func=AF.Exp, scale=scale),
                    reads=[("ps", b_s)], writes=[("pT", pi)])
                if kb >= 4 * qt:
                    k.op("dve", lambda e, pT=pT: e.tensor_tensor(out=pT[:, 0:128], in0=pT[:, 0:128], in1=tri[:], op=ALU.mult),
                         reads=[("pT", pi), "tri"], writes=[("pT", pi)])
                info[kb] = (off, ncol, pi, pT)

            def stage_c(kb):
                off, ncol, pi, pT = info[kb]
                k.op("pe", lambda e, kb=kb, off=off, ncol=ncol, pT=pT, b_o=b_o: e.matmul(
                    ps[b_o][:, off:TT], lhsT=v_fn(kb), rhs=pT[:, 0:ncol], start=(kb == 0), stop=(kb == nkb - 1)),
                    reads=rd + [("pT", pi)], writes=[("ps", b_o)], inc=False)
                k.op("pe", lambda e, kb=kb, off=off, ncol=ncol, pT=pT, b_d=b_d: e.matmul(
                    ps[b_d][:, off:TT], lhsT=ones_bf[:], rhs=pT[:, 0:ncol], start=(kb == 0), stop=(kb == nkb - 1)),
                    reads=[("pT", pi), "ones"], writes=[("ps", b_d)], inc=True)

            for step in range(nkb + LA):
                if step < nkb:
                    stage_ab(step)
                if step - LA >= 0:
                    stage_c(step - LA)
            k.op("dve", lambda e, b_d=b_d: e.reciprocal(out=rden[:], in_=ps[b_d][:, :]),
                 reads=[("ps", b_d)], writes=["rden"])
            out_cb(qt, b_o)

    def rope_tables(b, cosb, sinb):
        m = ar.mark()
        posi = ar.alloc("posi", [128, S], I32)
        ang = ar.alloc("ang", [128, S], F32)
        kk = ar.alloc("kk", [128, S], F32)
        r2 = ar.alloc("r2", [128, S], F32)
        pos_d = dget("pos", [NB, S], I32)
        invf_d = dget("invf4", [128, 1])
        invf = ar.alloc("invf", [128, 1], F32)
        k.dma("sp", lambda e: e.dma_start(out=posi[:], in_=pos_d[b:b + 1, :].to_broadcast([128, S])), "ld_small", writes=["posi"])
        k.dma("sp", lambda e: e.dma_start(out=invf[:], in_=invf_d[:, :]), "ld_small", writes=["invf"])
        MAGIC = 12582912.0
        C1 = 6.28125
        C2 = 2.0 * np.pi - 6.28125
        PI = 3.14159
        k.op("dve", lambda e: e.tensor_copy(out=ang[:], in_=posi[:]), reads=["posi"], writes=["ang"])
        k.op("dve", lambda e: e.tensor_scalar(out=ang[:], in0=ang[:], scalar1=invf[:, 0:1], scalar2=None, op0=ALU.mult),
             reads=["ang", "invf"], writes=["ang"])
        k.op("dve", lambda e: e.tensor_scalar(out=kk[:], in0=ang[:], scalar1=float(1.0 / (2 * np.pi)), scalar2=None, op0=ALU.mult),
             reads=["ang"], writes=["kk"])
        k.op("dve", lambda e: e.tensor_scalar(out=kk[:], in0=kk[:], scalar1=MAGIC, scalar2=None, op0=ALU.add),
             reads=["kk"], writes=["kk"])
        k.op("dve", lambda e: e.tensor_scalar(out=kk[:], in0=kk[:], scalar1=MAGIC, scalar2=None, op0=ALU.subtract),
             reads=["kk"], writes=["kk"])
        k.op("dve", lambda e: e.scalar_tensor_tensor(out=ang[:], in0=kk[:], scalar=-C1, in1=ang[:], op0=ALU.mult, op1=ALU.add),
             reads=["kk", "ang"], writes=["ang"])
        k.op("dve", lambda e: e.scalar_tensor_tensor(out=ang[:], in0=kk[:], scalar=float(-C2), in1=ang[:], op0=ALU.mult, op1=ALU.add),
             reads=["kk", "ang"], writes=["ang"])
        k.op("dve", lambda e: e.tensor_scalar(out=r2[:], in0=ang[:], scalar1=float(np.pi / 2), scalar2=None, op0=ALU.add),
             reads=["ang"], writes=["r2"])
        k.op("dve", lambda e: e.tensor_scalar(out=kk[:], in0=r2[:], scalar1=float(np.pi), scalar2=None, op0=ALU.is_gt),
             reads=["r2"], writes=["kk"])
        k.op("dve", lambda e: e.scalar_tensor_tensor(out=r2[:], in0=kk[:], scalar=float(-2 * np.pi), in1=r2[:], op0=ALU.mult, op1=ALU.add),
             reads=["kk", "r2"], writes=["r2"])
        for t_, nm in ((ang, "ang"), (r2, "r2")):
            k.op("dve", lambda e, t_=t_: e.tensor_scalar(out=t_[:], in0=t_[:], scalar1=-PI, scalar2=PI, op0=ALU.max, op1=ALU.min),
                 reads=[nm], writes=[nm])
        k.op("act", lambda e: e.activation(out=sinb[:], in_=ang[:], func=AF.Sin), reads=["ang"], writes=["sinb"])
        k.op("act", lambda e: e.activation(out=cosb[:], in_=r2[:], func=AF.Sin), reads=["r2"], writes=["cosb"])
        k.barrier(dma_keys=["ld_small"])
        ar.reset(m)

    def rms_scale(bank, n, eps_t, dst):
        k.op("act", lambda e: e.activation(out=dst, in_=ps[bank][:, :], func=AF.Ln, bias=eps_t[:, 0:1], scale=1.0 / n),
             reads=[("ps", bank), "epsrms"], writes=["rs_tmp"])
        k.op("act", lambda e: e.activation(out=dst, in_=dst, func=AF.Exp, scale=-0.5),
             reads=["rs_tmp"], writes=["rs_tmp"])

    def load_const(name, shape, tile_, key):
        d_ = dget(name, shape)
        k.dma("sp", lambda e: e.dma_start(out=tile_[:], in_=d_), "ld_small", writes=[key])

    def mla_sublayer(l, b):
        a = l // 3
        m = ar.mark()
        scale = 192.0 ** -0.5
        epsr = ar.alloc("epsr", [128, 1], F32)
        k.op("dve", lambda e: e.memset(epsr[:], RMS_EPS), writes=["epsrms"])
        tri_f = ar.alloc("trif", [128, 128], F32)
        tri = ar.alloc("tri", [128, 128], BF16)
        load_const("tri", [128, 128], tri_f, "trif")
        k.op("dve", lambda e: e.tensor_copy(out=tri[:], in_=tri_f[:]), reads=["trif"], writes=["tri"])
        qg = ar.alloc("qg", [128, 3], F32)
        kvg = ar.alloc("kvg", [128, 2], F32)
        load_const("mla_qg_%d" % a, [128, 3], qg, "qg")
        load_const("mla_kvg_%d" % a, [128, 2], kvg, "kvg")
        cosb = ar.alloc("cosb", [128, S], BF16)
        sinb = ar.alloc("sinb", [128, S], BF16)
        rope_tables(b, cosb, sinb)
        cqn = ar.alloc("cqn", [128, 3, S], BF16)
        ckvn = ar.alloc("ckvn", [128, 2, S], BF16)
        kRz = [ar.alloc("kRz", [128, S], BF16) for _ in range(2)]
        for i_ in range(2):
            k.op("dve", lambda e, i_=i_: e.memset(kRz[i_][:], 0.0), writes=[("kR", t_) for t_ in range(NTT)])
        mB = ar.mark()
        hT = ar.alloc("hT", [128, 8, S], BF16)
        wmi = ar.alloc("wmi", [128, 8, 704], BF16)
        wkr2 = ar.alloc("wkr2", [128, 8, 128], BF16)
        wkrot2 = ar.alloc("wkrot2", [128, 8, 128], BF16)
        lat = ar.alloc("lat", [128, 5, TT], F32)
        sqb = ar.alloc("sqb", [128, 5, TT], BF16)
        rsq = ar.alloc("rsq", [128, TT], F32)
        t1 = ar.alloc("t1", [128, TT], F32)
        t2 = ar.alloc("t2", [128, TT], F32)
        win_d = dget("mla_win_%d" % a, [8, 128, 704])
        for kc in range(8):
            k.dma("pool", lambda e, kc=kc: e.dma_start(out=wmi[:, kc, :], in_=win_d[kc, :, :]), "ld_mw", writes=["wmi"])
        for hh in range(2):
            o = hh * 64
            k.op("act", lambda e, o=o: e.activation(out=wkr2[:, :, o:o + 64], in_=wmi[:, :, 640:704], func=AF.Copy),
                 reads=["wmi"], writes=["wkr2"])
            k.op("act", lambda e, o=o: e.activation(out=wkrot2[:, :, o:o + 32], in_=wmi[:, :, 672:704], func=AF.Copy, scale=-1.0),
                 reads=["wmi"], writes=["wkrot2"])
            k.op("act", lambda e, o=o: e.activation(out=wkrot2[:, :, o + 32:o + 64], in_=wmi[:, :, 640:672], func=AF.Copy),
                 reads=["wmi"], writes=["wkrot2"])
        modulate(l, 1, b, hT, "hm", list(range(NTT)))
        for tt in range(NTT):
            tsl = slice(tt * TT, (tt + 1) * TT)
            for mc in range(5):
                bk = psb()
                for kc in range(8):
                    k.op("pe", lambda e, tsl=tsl, bk=bk, mc=mc, kc=kc: e.matmul(
                        ps[bk][:, :], lhsT=wmi[:, kc, mc * 128:(mc + 1) * 128], rhs=hT[:, kc, tsl],
                        start=(kc == 0), stop=(kc == 7)),
                        reads=["wmi", ("hm", kc, tt)], writes=[("ps", bk)], inc=(kc == 7))
                k.op("act", lambda e, tsl=tsl, bk=bk, mc=mc: e.activation(out=lat[:, mc, :], in_=ps[bk][:, :], func=AF.Copy),
                     reads=[("ps", bk)], writes=[("lat", mc)])
                k.op("act", lambda e, tsl=tsl, bk=bk, mc=mc: e.activation(out=sqb[:, mc, :], in_=ps[bk][:, :], func=AF.Square),
                     reads=[("ps", bk)], writes=[("sqb", mc)])
            for (c0, c1, n, gt, gk, dst, dk) in ((0, 3, 384, qg, "qg", cqn, "cqn"), (3, 5, 256, kvg, "kvg", ckvn, "ckvn")):
                bk = psb()
                for mc in range(c0, c1):
                    k.op("pe", lambda e, tsl=tsl, bk=bk, mc=mc, c0=c0, c1=c1: e.matmul(
                        ps[bk][:, :], lhsT=ones_bf[:], rhs=sqb[:, mc, :], start=(mc == c0), stop=(mc == c1 - 1)),
                        reads=[("sqb", mc), "ones"], writes=[("ps", bk)], inc=(mc == c1 - 1))
                rms_scale(bk, n, epsr, rsq[:])
                for mc in range(c0, c1):
                    k.op("dve", lambda e, tsl=tsl, mc=mc, c0=c0, gt=gt, dst=dst: e.scalar_tensor_tensor(
                        out=dst[:, mc - c0, tsl], in0=lat[:, mc, :], scalar=gt[:, mc - c0:mc - c0 + 1], in1=rsq[:],
                        op0=ALU.mult, op1=ALU.mult),
                        reads=[("lat", mc), gk, "rs_tmp"], writes=[(dk, mc - c0, tt)])
            b1, b2 = psb(), psb()
            for (bk, wt, wk_) in ((b1, wkr2, "wkr2"), (b2, wkrot2, "wkrot2")):
                for kc in range(8):
                    k.op("pe", lambda e, tsl=tsl, bk=bk, wt=wt, kc=kc: e.matmul(
                        ps[bk][:, :], lhsT=wt[:, kc, :], rhs=hT[:, kc, tsl], start=(kc == 0), stop=(kc == 7)),
                        reads=[wk_, ("hm", kc, tt)], writes=[("ps", bk)], inc=(kc == 7))
            k.op("dve", lambda e, tsl=tsl, b1=b1: e.tensor_tensor(out=t1[:], in0=ps[b1][:, :], in1=cosb[:, tsl], op=ALU.mult),
                 reads=[("ps", b1), "cosb"], writes=["t1"])
            k.op("dve", lambda e, tsl=tsl, b2=b2: e.tensor_tensor(out=t2[:], in0=ps[b2][:, :], in1=sinb[:, tsl], op=ALU.mult),
                 reads=[("ps", b2), "sinb"], writes=["t2"])
            for i_ in range(2):
                lo_, hi_ = i_ * 64, i_ * 64 + 64
                k.op("dve", lambda e, tsl=tsl, i_=i_, lo_=lo_, hi_=hi_: e.tensor_tensor(
                    out=kRz[i_][lo_:hi_, tsl], in0=t1[lo_:hi_, :], in1=t2[lo_:hi_, :], op=ALU.add),
                    reads=["t1", "t2"], writes=[("kR", tt)])
        k.barrier(dma_keys=["ld_mw", "ld_small"])
        dump("cosb", cosb[:], [128, S], BF16, ["cosb"])
        dump("sinb", sinb[:], [128, S], BF16, ["sinb"])
        dump("cqn", cqn[:], [128, 3, S], BF16, [])
        dump("ckvn", ckvn[:], [128, 2, S], BF16, [])
        dump("hT", hT[:], [128, 8, S], BF16, [])
        k.barrier(dma_keys=["st_x"])
        ar.reset(mB)
        oT = ar.alloc("oT", [128, 8, S], BF16)
        mB = ar.mark()
        wqn = ar.alloc("wqn", [128, 3, 256], BF16)
        wqr = ar.alloc("wqr", [128, 3, 128], BF16)
        wqrot = ar.alloc("wqrot", [128, 3, 128], BF16)
        wk = ar.alloc("wk", [128, 2, 256], BF16)
        wv = ar.alloc("wv", [128, 2, 256], BF16)
        qn = ar.alloc("qn", [128, 2, S], BF16)
        qr = ar.alloc("qr", [128, S], BF16)
        kn = ar.alloc("kn", [128, 2, S], BF16)
        V = ar.alloc("V", [128, 16, 256], BF16)
        pT_sl = [ar.alloc("pT", [128, TT], BF16) for _ in range(4)]
        rden = ar.alloc("rden", [128, TT], F32)
        t1 = ar.alloc("t1", [128, TT], F32)
        t2 = ar.alloc("t2", [128, TT], F32)
        wqn_d = dget("mla_wqn_%d" % a, [3, 128, 1024])
        wqr_d = dget("mla_wqr_%d" % a, [3, 128, 512])
        wk_d = dget("mla_wk_%d" % a, [2, 128, 1024])
        wv_d = dget("mla_wv_%d" % a, [2, 128, 1024])
        for pr in range(4):
            for kc in range(3):
                k.dma("pool", lambda e, kc=kc, pr=pr: e.dma_start(out=wqn[:, kc, :], in_=wqn_d[kc, :, pr * 256:(pr + 1) * 256]),
                      "ld_mw", writes=["wqn"])
                k.dma("pool", lambda e, kc=kc, pr=pr: e.dma_start(out=wqr[:, kc, :], in_=wqr_d[kc, :, pr * 128:(pr + 1) * 128]),
                      "ld_mw", writes=["wqr"])
            for kc in range(2):
                k.dma("pool", lambda e, kc=kc, pr=pr: e.dma_start(out=wk[:, kc, :], in_=wk_d[kc, :, pr * 256:(pr + 1) * 256]),
                      "ld_mw", writes=["wk"])
                k.dma("pool", lambda e, kc=kc, pr=pr: e.dma_start(out=wv[:, kc, :], in_=wv_d[kc, :, pr * 256:(pr + 1) * 256]),
                      "ld_mw", writes=["wv"])
            for hh in range(2):
                o = hh * 64
                k.op("act", lambda e, o=o: e.activation(out=wqrot[:, :, o:o + 32], in_=wqr[:, :, o + 32:o + 64], func=AF.Copy, scale=-1.0),
                     reads=["wqr"], writes=["wqrot"])
                k.op("act", lambda e, o=o: e.activation(out=wqrot[:, :, o + 32:o + 64], in_=wqr[:, :, o:o + 32], func=AF.Copy),
                     reads=["wqr"], writes=["wqrot"])
            for tt in range(NTT):
                tsl = slice(tt * TT, (tt + 1) * TT)
                for hh in range(2):
                    bk = psb()
                    for kc in range(3):
                        k.op("pe", lambda e, bk=bk, hh=hh, kc=kc, tsl=tsl: e.matmul(
                            ps[bk][:, :], lhsT=wqn[:, kc, hh * 128:(hh + 1) * 128], rhs=cqn[:, kc, tsl],
                            start=(kc == 0), stop=(kc == 2)),
                            reads=["wqn", ("cqn", kc, tt)], writes=[("ps", bk)], inc=(kc == 2))
                    k.op("act", lambda e, bk=bk, hh=hh, tsl=tsl: e.activation(out=qn[:, hh, tsl], in_=ps[bk][:, :], func=AF.Copy),
                         reads=[("ps", bk)], writes=[("qn", hh, tt)])
                    bk = psb()
                    for kc in range(2):
                        k.op("pe", lambda e, bk=bk, hh=hh, kc=kc, tsl=tsl: e.matmul(
                            ps[bk][:, :], lhsT=wk[:, kc, hh * 128:(hh + 1) * 128], rhs=ckvn[:, kc, tsl],
                            start=(kc == 0), stop=(kc == 1)),
                            reads=["wk", ("ckvn", kc, tt)], writes=[("ps", bk)], inc=(kc == 1))
                    k.op("act", lambda e, bk=bk, hh=hh, tsl=tsl: e.activation(out=kn[:, hh, tsl], in_=ps[bk][:, :], func=AF.Copy),
                         reads=[("ps", bk)], writes=[("kn", hh, tt)])
                b1, b2 = psb(), psb()
                for (bk, wt, wk_) in ((b1, wqr, "wqr"), (b2, wqrot, "wqrot")):
                    for kc in range(3):
                        k.op("pe", lambda e, bk=bk, wt=wt, kc=kc, tsl=tsl: e.matmul(
                            ps[bk][:, :], lhsT=wt[:, kc, :], rhs=cqn[:, kc, tsl], start=(kc == 0), stop=(kc == 2)),
                            reads=[wk_, ("cqn", kc, tt)], writes=[("ps", bk)], inc=(kc == 2))
                k.op("dve", lambda e, b1=b1, tsl=tsl: e.tensor_tensor(out=t1[:], in0=ps[b1][:, :], in1=cosb[:, tsl], op=ALU.mult),
                     reads=[("ps", b1), "cosb"], writes=["t1"])
                k.op("dve", lambda e, b2=b2, tsl=tsl: e.tensor_tensor(out=t2[:], in0=ps[b2][:, :], in1=sinb[:, tsl], op=ALU.mult),
                     reads=[("ps", b2), "sinb"], writes=["t2"])
                k.op("dve", lambda e, tsl=tsl: e.tensor_tensor(out=qr[:, tsl], in0=t1[:], in1=t2[:], op=ALU.add),
                     reads=["t1", "t2"], writes=[("qr", tt)])
                for tb in range(4):
                    gtb = tt * 4 + tb
                    bk = psb()
                    for kc in range(2):
                        k.op("pe", lambda e, bk=bk, kc=kc, gtb=gtb: e.matmul(
                            ps[bk][:, 0:256], lhsT=ckvn[:, kc, gtb * 128:(gtb + 1) * 128], rhs=wv[:, kc, :],
                            start=(kc == 0), stop=(kc == 1)),
                            reads=["wv", ("ckvn", kc, tt)], writes=[("ps", bk)], inc=(kc == 1))
                    k.op("act", lambda e, bk=bk, gtb=gtb: e.activation(out=V[:, gtb, :], in_=ps[bk][:, 0:256], func=AF.Copy),
                         reads=[("ps", bk)], writes=[("V", gtb)])
            rd_all = [("qn", hh_, t_) for hh_ in range(2) for t_ in range(NTT)] + [("kn", hh_, t_) for hh_ in range(2) for t_ in range(NTT)] \
                + [("qr", t_) for t_ in range(NTT)] + [("kR", t_) for t_ in range(NTT)] + [("V", g_) for g_ in range(16)]
            for hh in range(2):
                h = pr * 2 + hh
                lo, hi = hh * 64, hh * 64 + 64

                def out_cb(qt, b_o, h=h):
                    k.op("dve", lambda e, qt=qt, b_o=b_o, h=h: e.tensor_tensor(
                        out=oT[:, h, qt * TT:(qt + 1) * TT], in0=ps[b_o][:, :], in1=rden[:], op=ALU.mult),
                        reads=[("ps", b_o), "rden"], writes=[("oT", h, qt)])

                attention_head(
                    q_parts=[lambda c0, c1, hh=hh: qn[:, hh, c0:c1], lambda c0, c1: qr[:, c0:c1]],
                    k_parts=[lambda kb, hh=hh: kn[:, hh, kb * 128:(kb + 1) * 128], lambda kb, hh=hh: kRz[hh][:, kb * 128:(kb + 1) * 128]],
                    v_fn=lambda kb, hh=hh: V[:, kb, hh * 128:(hh + 1) * 128],
                    out_cb=out_cb, scale=scale, rd=rd_all, pT_sl=pT_sl, rden=rden, tri=tri)
        k.barrier(dma_keys=["ld_mw"])
        dump("tri", tri[:], [128, 128], BF16, [])
        dump("kRz0", kRz[0][:], [128, S], BF16, [])
        dump("kRz1", kRz[1][:], [128, S], BF16, [])
        for i_ in range(4):
            dump("pT%d" % i_, pT_sl[i_][:], [128, TT], BF16, [])
        dump("rden", rden[:], [128, TT], F32, [])
        dump("qn", qn[:], [128, 2, S], BF16, [])
        dump("qr", qr[:], [128, S], BF16, [])
        dump("kn", kn[:], [128, 2, S], BF16, [])
        dump("V", V[:], [128, 16, 256], BF16, [])
        dump("oT", oT[:], [128, 8, S], BF16, [])
        k.barrier(dma_keys=["st_x"])
        ar.reset(mB)
        out_proj_and_ln(l, b, oT, 8, "mla_wo_%d" % a, "oT")
        k.barrier()
        ar.reset(m)

    def diff_sublayer(l, b):
        m = ar.mark()
        scale = 64.0 ** -0.5
        lam_init = 0.8 - 0.6 * float(np.exp(-0.3 * l))
        epsr = ar.alloc("epsr", [128, 1], F32)
        k.op("dve", lambda e: e.memset(epsr[:], RMS_EPS), writes=["epsrms"])
        tri_f = ar.alloc("trif", [128, 128], F32)
        tri = ar.alloc("tri", [128, 128], BF16)
        load_const("tri", [128, 128], tri_f, "trif")
        k.op("dve", lambda e: e.tensor_copy(out=tri[:], in_=tri_f[:]), reads=["trif"], writes=["tri"])
        lq = ar.alloc("lq", [128, 128], F32)
        lk = ar.alloc("lk", [128, 128], F32)
        lam = ar.alloc("lam", [128, 4], F32)
        gsub = ar.alloc("gsub", [128, 1], F32)
        lq_d = dget("diff_lq", [1, 128])
        lk_d = dget("diff_lk", [1, 128])
        k.dma("sp", lambda e: e.dma_start(out=lq[:], in_=lq_d[0:1, :].to_broadcast([128, 128])), "ld_small", writes=["lq"])
        k.dma("sp", lambda e: e.dma_start(out=lk[:], in_=lk_d[0:1, :].to_broadcast([128, 128])), "ld_small", writes=["lk"])
        load_const("diff_g", [128, 1], gsub, "gsub")
        k.op("dve", lambda e: e.tensor_tensor(out=lq[:], in0=lq[:], in1=lk[:], op=ALU.mult), reads=["lq", "lk"], writes=["lq"])
        k.op("dve", lambda e: e.tensor_reduce(out=lam[:, 0:2], in_=lq[:].rearrange("p (a b) -> p a b", a=2),
                                              axis=mybir.AxisListType.X, op=ALU.add), reads=["lq"], writes=["lam"])
        k.op("act", lambda e: e.activation(out=lam[:, 0:2], in_=lam[:, 0:2], func=AF.Exp), reads=["lam"], writes=["lam"])
        k.op("dve", lambda e: e.tensor_tensor(out=lam[:, 2:3], in0=lam[:, 1:2], in1=lam[:, 0:1], op=ALU.subtract),
             reads=["lam"], writes=["lam"])
        k.op("dve", lambda e: e.tensor_scalar(out=lam[:, 3:4], in0=lam[:, 2:3], scalar1=-lam_init, scalar2=None, op0=ALU.add),
             reads=["lam"], writes=["lam"])
        k.op("dve", lambda e: e.tensor_scalar(out=gsub[:], in0=gsub[:], scalar1=1.0 - lam_init, scalar2=None, op0=ALU.mult),
             reads=["gsub"], writes=["gsub"])
        hT = ar.alloc("hT", [128, 8, S], BF16)
        oT = ar.alloc("oT", [128, 8, S], BF16)
        mB = ar.mark()
        qT = ar.alloc("qT", [128, 2, S], BF16)
        kz = [ar.alloc("kz", [128, 2, S], BF16) for _ in range(2)]
        for i_ in range(2):
            k.op("dve", lambda e, i_=i_: e.memset(kz[i_][:], 0.0), writes=[("kT", hh_, t_) for hh_ in range(2) for t_ in range(NTT)])
        V = ar.alloc("V", [128, 16, 256], BF16)
        o0 = ar.alloc("o0", [128, S], F32)
        pT_sl = [ar.alloc("pT", [128, TT], BF16) for _ in range(3)]
        rden = ar.alloc("rden", [128, TT], F32)
        t1 = ar.alloc("t1", [128, TT], F32)
        sq = ar.alloc("sq", [128, TT], BF16)
        rs = ar.alloc("rs", [128, TT], F32)
        wd = dget("diff_wi", [12, 128, 2048])
        modulate(l, 1, b, hT, "hm", list(range(NTT)))

        def load_grp(g):
            sl = cnt["wi"] % NWI
            cnt["wi"] += 1
            t = wi_sl[sl]
            k.dma("pool", lambda e: e.dma_start(out=t[:].rearrange("p a b c -> p (a b c)"), in_=wd[g, :, :]),
                  ("wi", sl), writes=[("wi", sl)])
            return sl, t

        for pr in range(4):
            for (g, dst, dk) in ((pr, qT, "qT"), (4 + pr, None, "kT")):
                sl, w = load_grp(g)
                for tt in range(NTT):
                    for hh in range(2):
                        bk = psb()
                        for kc in range(8):
                            k.op("pe", lambda e, bk=bk, w=w, kc=kc, hh=hh, tt=tt: e.matmul(
                                ps[bk][:, :], lhsT=w[:, kc, hh, :], rhs=hT[:, kc, tt * TT:(tt + 1) * TT],
                                start=(kc == 0), stop=(kc == 7)),
                                reads=[("wi", sl), ("hm", kc, tt)], writes=[("ps", bk)], inc=(kc == 7))
                        if dst is not None:
                            k.op("act", lambda e, bk=bk, dst=dst, hh=hh, tt=tt: e.activation(
                                out=dst[:, hh, tt * TT:(tt + 1) * TT], in_=ps[bk][:, :], func=AF.Copy),
                                reads=[("ps", bk)], writes=[(dk, hh, tt)])
                        else:
                            for i_ in range(2):
                                lo_, hi_ = i_ * 64, i_ * 64 + 64
                                k.op("act", lambda e, bk=bk, hh=hh, tt=tt, i_=i_, lo_=lo_, hi_=hi_: e.activation(
                                    out=kz[i_][lo_:hi_, hh, tt * TT:(tt + 1) * TT], in_=ps[bk][lo_:hi_, :], func=AF.Copy),
                                    reads=[("ps", bk)], writes=[(dk, hh, tt)])
            sl, w = load_grp(8 + pr)
            for gtb in range(16):
                bk = psb()
                for kc in range(8):
                    k.op("pe", lambda e, bk=bk, w=w, kc=kc, gtb=gtb: e.matmul(
                        ps[bk][:, 0:256], lhsT=hT[:, kc, gtb * 128:(gtb + 1) * 128],
                        rhs=w[:, kc, :, :].rearrange("p a b -> p (a b)"), start=(kc == 0), stop=(kc == 7)),
                        reads=[("wi", sl), ("hm", kc, gtb // 4)], writes=[("ps", bk)], inc=(kc == 7))
                k.op("act", lambda e, bk=bk, gtb=gtb: e.activation(out=V[:, gtb, :], in_=ps[bk][:, 0:256], func=AF.Copy),
                     reads=[("ps", bk)], writes=[("V", gtb)])
            rd_all = [("qT", hh_, t_) for hh_ in range(2) for t_ in range(NTT)] + [("kT", hh_, t_) for hh_ in range(2) for t_ in range(NTT)] \
                + [("V", g_) for g_ in range(16)]
            for hh in range(2):
                h = pr * 2 + hh
                for i in range(2):
                    lo, hi = i * 64, i * 64 + 64

                    def out_cb(qt, b_o, h=h, i=i):
                        qs = slice(qt * TT, (qt + 1) * TT)
                        if i == 0:
                            k.op("dve", lambda e: e.tensor_tensor(out=o0[:, qs], in0=ps[b_o][:, :], in1=rden[:], op=ALU.mult),
                                 reads=[("ps", b_o), "rden"], writes=[("o0", qt)])
                            return
                        k.op("dve", lambda e: e.tensor_tensor(out=t1[:], in0=ps[b_o][:, :], in1=rden[:], op=ALU.mult),
                             reads=[("ps", b_o), "rden"], writes=["t1"])
                        k.op("dve", lambda e: e.scalar_tensor_tensor(out=t1[:], in0=t1[:], scalar=lam[:, 3:4], in1=o0[:, qs],
                                                                     op0=ALU.mult, op1=ALU.add),
                             reads=["t1", "lam", ("o0", qt)], writes=["t1"])
                        k.op("act", lambda e: e.activation(out=sq[:], in_=t1[:], func=AF.Square), reads=["t1"], writes=["sq"])
                        bk = psb()
                        k.op("pe", lambda e: e.matmul(ps[bk][:, :], lhsT=ones_bf[:], rhs=sq[:], start=True, stop=True),
                             reads=["sq", "ones"], writes=[("ps", bk)])
                        rms_scale(bk, 128, epsr, rs[:])
                        k.op("dve", lambda e: e.scalar_tensor_tensor(out=oT[:, h, qs], in0=t1[:], scalar=gsub[:, 0:1], in1=rs[:],
                                                                     op0=ALU.mult, op1=ALU.mult),
                             reads=["t1", "gsub", "rs_tmp"], writes=[("oT", h, qt)])

                    attention_head(
                        q_parts=[lambda c0, c1, hh=hh: qT[:, hh, c0:c1]],
                        k_parts=[lambda kb, hh=hh, i=i: kz[i][:, hh, kb * 128:(kb + 1) * 128]],
                        v_fn=lambda kb, hh=hh: V[:, kb, hh * 128:(hh + 1) * 128],
                        out_cb=out_cb, scale=scale, rd=rd_all, pT_sl=pT_sl, rden=rden, tri=tri)
        k.barrier(dma_keys=["ld_small"])
        dump("d_hT", hT[:], [128, 8, S], BF16, [])
        dump("d_qT", qT[:], [128, 2, S], BF16, [])
        dump("d_kz0", kz[0][:], [128, 2, S], BF16, [])
        dump("d_kz1", kz[1][:], [128, 2, S], BF16, [])
        dump("d_V", V[:], [128, 16, 256], BF16, [])
        dump("d_oT", oT[:], [128, 8, S], BF16, [])
        dump("d_lam", lam[:], [128, 4], F32, [])
        dump("d_gsub", gsub[:], [128, 1], F32, [])
        dump("d_tri", tri[:], [128, 128], BF16, [])
        k.barrier(dma_keys=["st_x"])
        ar.reset(mB)
        out_proj_and_ln(l, b, oT, 8, "diff_wo", "oT")
        k.barrier()
        ar.reset(m)

    def ssd_sublayer(l, b):
        m = ar.mark()
        NH = 32
        ident_f = ar.alloc("identf", [128, 128], F32)
        ident_b = ar.alloc("identb", [128, 128], BF16)
        tri_f = ar.alloc("trif", [128, 128], F32)
        tri = ar.alloc("tri", [128, 128], BF16)
        ones_f = ar.alloc("onesf", [128, 128], F32)
        onec = ar.alloc("onec", [128, 1], F32)
        epsr = ar.alloc("epsr", [128, 1], F32)
        cw = ar.alloc("cw", [128, 24, 4], F32)
        cb = ar.alloc("cb", [128, 24], F32)
        dsk = ar.alloc("dsk", [128, 16], F32)
        ng = ar.alloc("ng", [128, 16], F32)
        a_b = ar.alloc("a_b", [128, NH], F32)
        dtb = ar.alloc("dtb", [128, NH], F32)
        wdt = ar.alloc("wdt", [128, 8, NH], BF16)
        load_const("ident", [128, 128], ident_f, "identf")
        load_const("tri", [128, 128], tri_f, "trif")
        k.op("dve", lambda e: e.tensor_copy(out=ident_b[:], in_=ident_f[:]), reads=["identf"], writes=["identb"])
        k.op("dve", lambda e: e.tensor_copy(out=tri[:], in_=tri_f[:]), reads=["trif"], writes=["tri"])
        k.op("dve", lambda e: e.memset(ones_f[:], 1.0), writes=["onesf"])
        k.op("dve", lambda e: e.memset(onec[:], 1.0), writes=["onec"])
        k.op("dve", lambda e: e.memset(epsr[:], RMS_EPS), writes=["epsrms"])
        cw_d = dget("ssm_cw", [128, 96])
        k.dma("sp", lambda e: e.dma_start(out=cw[:].rearrange("p a b -> p (a b)"), in_=cw_d[:, :]), "ld_small", writes=["cw"])
        load_const("ssm_cb", [128, 24], cb, "cb")
        load_const("ssm_dsk", [128, 16], dsk, "dsk")
        load_const("ssm_ng", [128, 16], ng, "ng")
        al_d = dget("ssm_alog", [1, NH])
        db_d = dget("ssm_dtb", [1, NH])
        k.dma("sp", lambda e: e.dma_start(out=a_b[:], in_=al_d[0:1, :].to_broadcast([128, NH])), "ld_small", writes=["a_b"])
        k.dma("sp", lambda e: e.dma_start(out=dtb[:], in_=db_d[0:1, :].to_broadcast([128, NH])), "ld_small", writes=["dtb"])
        k.op("act", lambda e: e.activation(out=a_b[:], in_=a_b[:], func=AF.Exp), reads=["a_b"], writes=["a_b"])
        k.op("dve", lambda e: e.tensor_scalar(out=a_b[:], in0=a_b[:], scalar1=-1.0, scalar2=None, op0=ALU.mult),
             reads=["a_b"], writes=["a_b"])
        wdt_d = dget("ssm_wdt", [128, 8 * NH])
        k.dma("pool", lambda e: e.dma_start(out=wdt[:].rearrange("p a b -> p (a b)"), in_=wdt_d[:, :]), "ld_mw", writes=["wdt"])
        hT = ar.alloc("hT", [128, 8, S], BF16)
        modulate(l, 1, b, hT, "hm", list(range(NTT)))
        hm_all = [("hm", kc_, t_) for kc_ in range(8) for t_ in range(NTT)]
        dt_tok = ar.alloc("dt_tok", [128, 16, NH], F32)
        dA_tok = ar.alloc("dA_tok", [128, 16, NH], F32)
        A_tok = ar.alloc("A_tok", [128, 16, NH], F32)
        for tb in range(16):
            bk = psb()
            for kc in range(8):
                k.op("pe", lambda e, bk=bk, kc=kc, tb=tb: e.matmul(
                    ps[bk][:, 0:NH], lhsT=hT[:, kc, tb * 128:(tb + 1) * 128], rhs=wdt[:, kc, :], start=(kc == 0), stop=(kc == 7)),
                    reads=["wdt"] + hm_all, writes=[("ps", bk)], inc=(kc == 7))
            k.op("dve", lambda e, bk=bk, tb=tb: e.tensor_tensor(out=dt_tok[:, tb, :], in0=ps[bk][:, 0:NH], in1=dtb[:], op=ALU.add),
                 reads=[("ps", bk), "dtb"], writes=["dt_tok"])
        k.op("act", lambda e: e.activation(out=dt_tok[:], in_=dt_tok[:], func=AF.Exp), reads=["dt_tok"], writes=["dt_tok"])
        k.op("act", lambda e: e.activation(out=dt_tok[:], in_=dt_tok[:], func=AF.Ln, bias=onec[:, 0:1], scale=1.0),
             reads=["dt_tok", "onec"], writes=["dt_tok"])
        k.op("dve", lambda e: e.tensor_tensor(out=dA_tok[:], in0=dt_tok[:], in1=a_b[:].unsqueeze(1).to_broadcast([128, 16, NH]), op=ALU.mult),
             reads=["dt_tok", "a_b"], writes=["dA_tok"])
        for tb in range(16):
            bk = psb()
            for t2 in range(tb + 1):
                k.op("pe", lambda e, bk=bk, t2=t2, tb=tb: e.matmul(
                    ps[bk][:, 0:NH], lhsT=(tri_f[:] if t2 == tb else ones_f[:]), rhs=dA_tok[:, t2, :],
                    start=(t2 == 0), stop=(t2 == tb)),
                    reads=["dA_tok", "trif", "onesf"], writes=[("ps", bk)], inc=(t2 == tb))
            k.op("act", lambda e, bk=bk, tb=tb: e.activation(out=A_tok[:, tb, :], in_=ps[bk][:, 0:NH], func=AF.Copy),
                 reads=[("ps", bk)], writes=["A_tok"])
        nA_tok = ar.alloc("nA_tok", [128, 16, NH], F32)
        k.op("dve", lambda e: e.tensor_scalar(out=nA_tok[:], in0=A_tok[:], scalar1=-1.0, scalar2=None, op0=ALU.mult),
             reads=["A_tok"], writes=["nA_tok"])
        BT = ar.alloc("BT", [128, S], BF16)
        CT = ar.alloc("CT", [128, S], BF16)
        HS = S // 2
        pc = ar.alloc("pc", [128, 3 + HS], F32)
        acc = ar.alloc("acc", [128, HS], F32)
        zs = ar.alloc("zs", [128, S], BF16)
        xc = ar.alloc("xc", [128, S], BF16)
        xdt = ar.alloc("xdt", [128, 16, 128], BF16)
        yg = ar.alloc("yg", [128, 4, S], BF16)
        ssq = ar.alloc("ssq", [128, S], F32)
        dec = [ar.alloc("dec", [128, TT], F32) for _ in range(3)]
        pT_sl = [ar.alloc("pT", [128, TT], BF16) for _ in range(3)]
        dg = [ar.alloc("dg", [128, 128], F32) for _ in range(2)]
        tmp = ar.alloc("tmpy", [128, TT], F32)
        sq = ar.alloc("sq", [128, TT], BF16)
        rs = ar.alloc("rs", [128, TT], F32)

        def conv_silu(ch, wslot_tok, w_ap, dst, dkey):
            k.op("dve", lambda e: e.memset(pc[:, 0:3], 0.0), writes=["pc"])
            for hf in range(2):
                for t2 in range(2):
                    tt = hf * 2 + t2
                    bk = psb()
                    for kc in range(8):
                        k.op("pe", lambda e, bk=bk, kc=kc, tt=tt: e.matmul(
                            ps[bk][:, :], lhsT=w_ap(kc), rhs=hT[:, kc, tt * TT:(tt + 1) * TT], start=(kc == 0), stop=(kc == 7)),
                            reads=[wslot_tok, ("hm", kc, tt)], writes=[("ps", bk)], inc=(kc == 7))
                    k.op("act", lambda e, bk=bk, t2=t2: e.activation(out=pc[:, 3 + t2 * TT:3 + (t2 + 1) * TT], in_=ps[bk][:, :], func=AF.Copy),
                         reads=[("ps", bk)], writes=["pc"])
                k.op("act", lambda e: e.activation(out=acc[:], in_=pc[:, 3:3 + HS], func=AF.Identity,
                                                   scale=cw[:, ch, 3:4], bias=cb[:, ch:ch + 1]),
                     reads=["pc", "cw", "cb"], writes=["acc"])
                for j_ in (2, 1, 0):
                    k.op("dve", lambda e, j_=j_: e.scalar_tensor_tensor(out=acc[:], in0=pc[:, j_:j_ + HS], scalar=cw[:, ch, j_:j_ + 1],
                                                                       in1=acc[:], op0=ALU.mult, op1=ALU.add),
                         reads=["pc", "cw", "acc"], writes=["acc"])
                k.op("act", lambda e, hf=hf: e.activation(out=dst[:, hf * HS:(hf + 1) * HS], in_=acc[:], func=AF.Silu),
                     reads=["acc"], writes=[dkey])
                if hf == 0:
                    k.op("dve", lambda e: e.tensor_copy(out=pc[:, 0:3], in_=pc[:, HS:HS + 3]), reads=["pc"], writes=["pc"])

        wz_d = dget("ssm_wz", [16, 128, 1024])
        wx_d = dget("ssm_wx", [16, 128, 1024])
        wbc_d = dget("ssm_wbc", [8, 128, 1024])
        wo_d = dget("ssm_wo", [4, 8, 128, 512])

        def load_pair(d0, i0, d1, i1):
            sl = cnt["wi"] % NWI
            cnt["wi"] += 1
            t = wi_sl[sl]
            k.dma("pool", lambda e: e.dma_start(out=t[:, :, 0, :], in_=d0[i0, :, :].rearrange("p (a b) -> p a b", a=8)),
                  ("wi", sl), writes=[("wi", sl)])
            k.dma("pool", lambda e: e.dma_start(out=t[:, :, 1, :], in_=d1[i1, :, :].rearrange("p (a b) -> p a b", a=8)),
                  ("wi", sl), writes=[("wi", sl)])
            return sl, t

        for g in range(4):
            sl, w = load_pair(wbc_d, g, wbc_d, 4 + g)
            conv_silu(16 + g, ("wi", sl), lambda kc, w=w: w[:, kc, 0, :], BT, "BT")
            conv_silu(20 + g, ("wi", sl), lambda kc, w=w: w[:, kc, 1, :], CT, "CT")
            for jj in range(4):
                j = g * 4 + jj
                sl, w = load_pair(wz_d, j, wx_d, j)
                for tt in range(NTT):
                    bk = psb()
                    for kc in range(8):
                        k.op("pe", lambda e, bk=bk, kc=kc, tt=tt, w=w: e.matmul(
                            ps[bk][:, :], lhsT=w[:, kc, 0, :], rhs=hT[:, kc, tt * TT:(tt + 1) * TT], start=(kc == 0), stop=(kc == 7)),
                            reads=[("wi", sl), ("hm", kc, tt)], writes=[("ps", bk)], inc=(kc == 7))
                    k.op("act", lambda e, bk=bk, tt=tt: e.activation(out=zs[:, tt * TT:(tt + 1) * TT], in_=ps[bk][:, :], func=AF.Silu),
                         reads=[("ps", bk)], writes=["zs"])
                conv_silu(j, ("wi", sl), lambda kc, w=w: w[:, kc, 1, :], xc, "xc")
                for tb in range(16):
                    bk = psb()
                    k.op("pe", lambda e, bk=bk, tb=tb: e.matmul(ps[bk][:, 0:128], lhsT=xc[:, tb * 128:(tb + 1) * 128], rhs=ident_b[:],
                                                                start=True, stop=True),
                         reads=["xc", "identb"], writes=[("ps", bk)])
                    for hh in range(2):
                        h = 2 * j + hh
                        k.op("act", lambda e, bk=bk, tb=tb, hh=hh, h=h: e.activation(
                            out=xdt[:, tb, hh * 64:(hh + 1) * 64], in_=ps[bk][:, hh * 64:(hh + 1) * 64], func=AF.Copy,
                            scale=dt_tok[:, tb, h:h + 1]),
                            reads=[("ps", bk), "dt_tok"], writes=["xdt"])
                for hh in range(2):
                    h = 2 * j + hh
                    lo, hi = hh * 64, hh * 64 + 64
                    for qt in range(NTT):
                        b_y = psr(0, 3, "sy")
                        b_r = psr(3, 5, "sr")
                        for lb in range(4):
                            di = psr(0, 2, "dg")
                            k.op("dve", lambda e, di=di, lb=lb, qt=qt, h=h: e.tensor_scalar(
                                out=dg[di][:], in0=ident_f[:], scalar1=A_tok[:, 4 * qt + lb, h:h + 1], scalar2=None, op0=ALU.mult),
                                reads=["identf", "A_tok"], writes=[("dg", di)])
                            k.op("pe", lambda e, di=di, lb=lb, b_r=b_r: e.matmul(
                                ps[b_r][:, lb * 128:(lb + 1) * 128], lhsT=ones_f[:], rhs=dg[di][:], start=True, stop=True),
                                reads=[("dg", di), "onesf"], writes=[("ps", b_r)])
                        nkb = 4 * qt + 4
                        info = {}

                        def stage_ab(kb, qt=qt, h=h, b_r=b_r, info=info):
                            off = max(0, kb - 4 * qt) * 128
                            ncol = TT - off
                            q0 = qt * TT + off
                            b_s = psr(5, 8, "ss")
                            k.op("pe", lambda e, b_s=b_s, kb=kb, q0=q0, ncol=ncol: e.matmul(
                                ps[b_s][:, 0:ncol], lhsT=BT[:, kb * 128:(kb + 1) * 128], rhs=CT[:, q0:q0 + ncol], start=True, stop=True),
                                reads=["BT", "CT"], writes=[("ps", b_s)])
                            dix = psr(0, len(dec), "dec")
                            d_ = dec[dix]
                            if kb >= 4 * qt:
                                k.op("dve", lambda e, d_=d_, b_r=b_r, off=off, ncol=ncol, kb=kb, h=h: e.tensor_scalar(
                                    out=d_[:, 0:ncol], in0=ps[b_r][:, off:TT], scalar1=A_tok[:, kb, h:h + 1], scalar2=0.0,
                                    op0=ALU.subtract, op1=ALU.min),
                                    reads=[("ps", b_r), "A_tok"], writes=[("dec", dix)])
                                k.op("act", lambda e, d_=d_, ncol=ncol: e.activation(out=d_[:, 0:ncol], in_=d_[:, 0:ncol], func=AF.Exp),
                                     reads=[("dec", dix)], writes=[("dec", dix)])
                            else:
                                k.op("act", lambda e, d_=d_, b_r=b_r, kb=kb, h=h: e.activation(
                                    out=d_[:, :], in_=ps[b_r][:, :], func=AF.Exp, bias=nA_tok[:, kb, h:h + 1], scale=1.0),
                                    reads=[("ps", b_r), "nA_tok"], writes=[("dec", dix)])
                            info[kb] = (off, ncol, b_s, dix, d_)

                        def stage_b(kb, qt=qt, info=info):
                            off, ncol, b_s, dix, d_ = info[kb]
                            pi = psr(0, len(pT_sl), "spT")
                            pT = pT_sl[pi]
                            k.op("dve", lambda e, pT=pT, d_=d_, b_s=b_s, ncol=ncol: e.tensor_tensor(
                                out=pT[:, 0:ncol], in0=ps[b_s][:, 0:ncol], in1=d_[:, 0:ncol], op=ALU.mult),
                                reads=[("ps", b_s), ("dec", dix)], writes=[("pT", pi)])
                            if kb >= 4 * qt:
                                k.op("dve", lambda e, pT=pT: e.tensor_tensor(out=pT[:, 0:128], in0=pT[:, 0:128], in1=tri[:], op=ALU.mult),
                                     reads=[("pT", pi), "tri"], writes=[("pT", pi)])
                            info[kb] = (off, ncol, pi, pT)

                        def stage_c(kb, b_y=b_y, nkb=nkb, info=info):
                            off, ncol, pi, pT = info[kb]
                            k.op("pe", lambda e, kb=kb, off=off, ncol=ncol, pT=pT, b_y=b_y: e.matmul(
                                ps[b_y][:, off:TT], lhsT=xdt[:, kb, :], rhs=pT[:, 0:ncol], start=(kb == 0), stop=(kb == nkb - 1)),
                                reads=["xdt", ("pT", pi)], writes=[("ps", b_y)], inc=(kb == nkb - 1))

                        for step in range(nkb + 2):
                            if step < nkb:
                                stage_ab(step)
                            if 0 <= step - 1 < nkb:
                                stage_b(step - 1)
                            if step - 2 >= 0:
                                stage_c(step - 2)
                        qs = slice(qt * TT, (qt + 1) * TT)
                        k.op("dve", lambda e, b_y=b_y, lo=lo, hi=hi, qs=qs, j=j: e.scalar_tensor_tensor(
                            out=tmp[lo:hi, :], in0=xc[lo:hi, qs], scalar=dsk[lo:hi, j:j + 1], in1=ps[b_y][lo:hi, :],
                            op0=ALU.mult, op1=ALU.add),
                            reads=["xc", "dsk", ("ps", b_y)], writes=["tmpy"])
                        k.op("dve", lambda e, lo=lo, hi=hi, qs=qs, jj=jj: e.tensor_tensor(
                            out=yg[lo:hi, jj, qs], in0=tmp[lo:hi, :], in1=zs[lo:hi, qs], op=ALU.mult),
                            reads=["tmpy", "zs"], writes=[("yg", jj, qt)])
                for qt in range(NTT):
                    qs = slice(qt * TT, (qt + 1) * TT)
                    k.op("act", lambda e, jj=jj, qs=qs: e.activation(out=sq[:], in_=yg[:, jj, qs], func=AF.Square),
                         reads=[("yg", jj, qt)], writes=["sq"])
                    bk = psb()
                    k.op("pe", lambda e, bk=bk: e.matmul(ps[bk][:, :], lhsT=ones_bf[:], rhs=sq[:], start=True, stop=True),
                         reads=["sq", "ones"], writes=[("ps", bk)])
                    if jj == 0:
                        k.op("act", lambda e, bk=bk, qs=qs: e.activation(out=ssq[:, qs], in_=ps[bk][:, :], func=AF.Copy),
                             reads=[("ps", bk)], writes=[("ssq", qt)])
                    else:
                        k.op("dve", lambda e, bk=bk, qs=qs: e.tensor_tensor(out=ssq[:, qs], in0=ps[bk][:, :], in1=ssq[:, qs], op=ALU.add),
                             reads=[("ps", bk), ("ssq", qt)], writes=[("ssq", qt)])
            for qt in range(NTT):
                qs = slice(qt * TT, (qt + 1) * TT)
                k.op("act", lambda e, qs=qs: e.activation(out=rs[:], in_=ssq[:, qs], func=AF.Ln, bias=epsr[:, 0:1], scale=1.0 / 512),
                     reads=[("ssq", qt), "epsrms"], writes=["rs_tmp"])
                k.op("act", lambda e: e.activation(out=rs[:], in_=rs[:], func=AF.Exp, scale=-0.5), reads=["rs_tmp"], writes=["rs_tmp"])
                for jj in range(4):
                    j = g * 4 + jj
                    k.op("dve", lambda e, jj=jj, j=j, qs=qs: e.scalar_tensor_tensor(
                        out=yg[:, jj, qs], in0=yg[:, jj, qs], scalar=ng[:, j:j + 1], in1=rs[:], op0=ALU.mult, op1=ALU.mult),
                        reads=[("yg", jj, qt), "ng", "rs_tmp"], writes=[("yg", jj, qt)])
            for dc in range(8):
                sl = cnt["wo"] % NWO
                cnt["wo"] += 1
                w = wo_sl[sl]
                k.dma("pool", lambda e, w=w, dc=dc, g=g: e.dma_start(
                    out=w[:, 0:4, :].rearrange("p a b -> p (a b)"), in_=wo_d[g, dc, :, :]), ("wo", sl), writes=[("wo", sl)])
                gi = pidx(l, 1, dc, b)
                for gtt in range(NTT):
                    bk = psb()
                    for jj in range(4):
                        k.op("pe", lambda e, bk=bk, jj=jj, gtt=gtt, w=w: e.matmul(
                            ps[bk][:, :], lhsT=w[:, jj, :], rhs=yg[:, jj, gtt * TT:(gtt + 1) * TT], start=(jj == 0), stop=(jj == 3)),
                            reads=[("wo", sl), ("yg", jj, gtt)], writes=[("ps", bk)], inc=(jj == 3))
                    k.op("dve", lambda e, bk=bk, dc=dc, gtt=gtt, gi=gi: e.scalar_tensor_tensor(
                        out=xT[:, dc, gtt * TT:(gtt + 1) * TT], in0=ps[bk][:, :], scalar=gcs[:, gi:gi + 1],
                        in1=xT[:, dc, gtt * TT:(gtt + 1) * TT], op0=ALU.mult, op1=ALU.add),
                        reads=[("ps", bk), ("x", dc, gtt), "gcs"], writes=[("x", dc, gtt)])
        k.barrier(dma_keys=["ld_small", "ld_mw"])
        ar.reset(m)
        zb = ar.alloc("zb", [128, 8, TT], BF16)
        zq = ar.alloc("zq", [128, 8, TT], BF16)
        st = ar.alloc("st", [128, 6, TT], F32)
        for gtt in range(NTT):
            layer_norm_tile(l, 1, gtt, zb, zq, st)
        k.barrier()
        ar.reset(m)

    for b in range(n_seq):
        for c in range(8):
            for gtt in range(NTT):
                k.dma("sp", lambda e, c=c, gtt=gtt, b=b: e.dma_start(
                    out=xT[:, c, gtt * TT:(gtt + 1) * TT],
                    in_=xT_d[b, c * 128:(c + 1) * 128, gtt * TT:(gtt + 1) * TT]),
                    "ld_x", writes=[("x", c, gtt)])
        for (l, j) in sublayers:
            if j != 1:
                ffn_sublayer(l, j, b)
            elif l % 3 == 0:
                mla_sublayer(l, b)
            elif l % 3 == 1:
                diff_sublayer(l, b)
            else:
                ssd_sublayer(l, b)
        for c in range(8):
            for gtt in range(NTT):
                k.dma("sp", lambda e, c=c, gtt=gtt, b=b: e.dma_start(
                    out=oT_d[b, c * 128:(c + 1) * 128, gtt * TT:(gtt + 1) * TT],
                    in_=xT[:, c, gtt * TT:(gtt + 1) * TT]),
                    "st_x", reads=[("x", c, gtt)])
    k.final_wait("sp", ["st_x"])
    k.emit()
    k.names = list(dr.keys())
    return nc, k


def prep_inputs(inp, names, n_cores=8):
    f = lambda a: np.asarray(a, dtype=np.float32)
    shared = {}
    for nm in names:
        p = nm.split("_")
        if nm in ("xT", "cT", "pos"):
            continue
        if nm == "b_modT":
            v = f(inp["b_mod"]).reshape(DEPTH, 72, 128).transpose(2, 0, 1).reshape(128, DEPTH * 72)
        elif nm == "ln_gT":
            v = f(inp["ln_g"]).reshape(DEPTH, 3, 8, 128).transpose(3, 0, 1, 2).reshape(128, -1)
        elif nm == "ln_bT":
            v = f(inp["ln_b"]).reshape(DEPTH, 3, 8, 128).transpose(3, 0, 1, 2).reshape(128, -1)
        elif nm.startswith("w_mod_"):
            v = f(inp["w_mod"][int(p[2])])
        elif nm.startswith("ffn_wi_"):
            l, jj = int(p[2]), int(p[3])
            w = f(inp["ffn_w_in"][l, jj]).reshape(8, 128, 2, NFG, 128)
            v = w.transpose(3, 1, 0, 2, 4).reshape(NFG, 128, 2048)
        elif nm.startswith("ffn_wo_"):
            l, jj = int(p[2]), int(p[3])
            w = f(inp["ffn_w_out"][l, jj]).reshape(NFG, 128, 8, 128)
            v = w.transpose(2, 1, 0, 3).reshape(8, 128, NFG * 128)
        elif nm == "invf4":
            inv = (10000.0 ** (-np.arange(0, 64, 2, dtype=np.float32) / 64.0)).astype(np.float32)
            v = np.tile(inv, 4).reshape(128, 1)
        elif nm == "tri":
            v = (np.arange(128)[None, :] >= np.arange(128)[:, None]).astype(np.float32)
        elif nm.startswith("mla_win_"):
            v = f(inp["mla_w_in"][int(p[2])]).reshape(8, 128, 704)
        elif nm.startswith("mla_qg_"):
            v = f(inp["mla_q_norm_g"][int(p[2])]).reshape(3, 128).T
        elif nm.startswith("mla_kvg_"):
            v = f(inp["mla_kv_norm_g"][int(p[2])]).reshape(2, 128).T
        elif nm.startswith("mla_wqn_"):
            v = f(inp["mla_w_q_up"][int(p[2])]).reshape(3, 128, 8, 192)[:, :, :, :128].reshape(3, 128, 1024)
        elif nm.startswith("mla_wqr_"):
            v = f(inp["mla_w_q_up"][int(p[2])]).reshape(3, 128, 8, 192)[:, :, :, 128:].reshape(3, 128, 512)
        elif nm.startswith("mla_wk_"):
            v = f(inp["mla_w_kv_up"][int(p[2])]).reshape(2, 128, 8, 256)[:, :, :, :128].reshape(2, 128, 1024)
        elif nm.startswith("mla_wv_"):
            v = f(inp["mla_w_kv_up"][int(p[2])]).reshape(2, 128, 8, 256)[:, :, :, 128:].reshape(2, 128, 1024)
        elif nm.startswith("mla_wo_"):
            w = f(inp["mla_w_out"][int(p[2])]).reshape(8, 128, 8, 128)
            v = w.transpose(2, 1, 0, 3).reshape(8, 128, 1024)
        elif nm == "ident":
            v = np.eye(128, dtype=np.float32)
        elif nm in ("ssm_wz", "ssm_wx"):
            o_ = 0 if nm == "ssm_wz" else 2048
            w = f(inp["ssm_w_in"][0])[:, o_:o_ + 2048].reshape(8, 128, 16, 128)
            v = w.transpose(2, 1, 0, 3).reshape(16, 128, 1024)
        elif nm == "ssm_wbc":
            w = f(inp["ssm_w_in"][0])[:, 4096:5120].reshape(8, 128, 8, 128)
            v = w.transpose(2, 1, 0, 3).reshape(8, 128, 1024)
        elif nm == "ssm_wdt":
            w = f(inp["ssm_w_in"][0])[:, 5120:5152].reshape(8, 128, 32)
            v = w.transpose(1, 0, 2).reshape(128, 256)
        elif nm == "ssm_cw":
            w = f(inp["ssm_conv_w"][0]).reshape(4, 24, 128)
            v = w.transpose(2, 1, 0).reshape(128, 96)
        elif nm == "ssm_cb":
            v = f(inp["ssm_conv_b"][0]).reshape(24, 128).T
        elif nm == "ssm_dsk":
            v = np.repeat(f(inp["ssm_d_skip"][0]), 64).reshape(16, 128).T
        elif nm == "ssm_ng":
            v = f(inp["ssm_norm_g"][0]).reshape(16, 128).T
        elif nm == "ssm_alog":
            v = f(inp["ssm_a_log"][0]).reshape(1, 32)
        elif nm == "ssm_dtb":
            v = f(inp["ssm_dt_bias"][0]).reshape(1, 32)
        elif nm == "ssm_wo":
            w = f(inp["ssm_w_out"][0]).reshape(4, 4, 128, 8, 128)
            v = w.transpose(0, 3, 2, 1, 4).reshape(4, 8, 128, 512)
        elif nm == "diff_wi":
            w = f(inp["diff_w_in"][0]).reshape(8, 128, 12, 256)
            v = w.transpose(2, 1, 0, 3).reshape(12, 128, 2048)
        elif nm == "diff_wo":
            w = f(inp["diff_w_out"][0]).reshape(8, 128, 8, 128)
            v = w.transpose(2, 1, 0, 3).reshape(8, 128, 1024)
        elif nm == "diff_lq":
            v = f(inp["diff_lambda_q"][0]).reshape(1, 128)
        elif nm == "diff_lk":
            v = f(inp["diff_lambda_k"][0]).reshape(1, 128)
        elif nm == "diff_g":
            v = f(inp["diff_subln_g"][0]).reshape(128, 1)
        else:
            raise KeyError(nm)
        shared[nm] = np.ascontiguousarray(v, dtype=np.float32)
    x = f(inp["x"])
    c = f(inp["c"])
    maps = []
    for i in range(n_cores):
        m = dict(shared)
        m["xT"] = np.ascontiguousarray(x[NB * i:NB * (i + 1)].transpose(0, 2, 1))
        m["cT"] = np.ascontiguousarray(c[NB * i:NB * (i + 1)].T.reshape(8, 128, NB).transpose(1, 0, 2))
        if "pos" in names:
            m["pos"] = np.ascontiguousarray(np.asarray(inp["positions"])[NB * i:NB * (i + 1)].astype(np.int32))
        maps.append(m)
    return maps


_CACHE = {}


def run(inp, n_seq=NB, sublayers=None, trace=False, same_sync=True, debug=False):
    key = (n_seq, tuple(sublayers) if sublayers is not None else None, same_sync)
    nc, k = build(n_seq, sublayers, same_sync, debug)
    maps = prep_inputs(inp, k.names)
    res = run_bass_kernel_spmd(nc, maps, core_ids=list(range(8)), trace=trace)
    outs = [r["oT"] for r in res.results]
    out = np.concatenate([o.transpose(0, 2, 1) for o in outs], axis=0)
    return out, res


def kernel(**inputs):
    out, _ = run(inputs)
    return np.ascontiguousarray(out.astype(np.float32))
```
